# Optimizing a Trainium2 kernel written in Bass

```python
import jax, jax.numpy as jnp
from jax import lax
import numpy as np

D_MODEL = 1024
BATCH = 1
SEQ = 16384
DEPTH = 1
DEC_BATCH = 8
DEC_SEQ = 2048
PAST_LEN = 128

MIX_WIDTH = D_MODEL
GLA_WIDTH = MIX_WIDTH // 2
CONV_WIDTH = MIX_WIDTH - GLA_WIDTH
GLA_HEADS = 4
GLA_DV = GLA_WIDTH // GLA_HEADS
GLA_KEY_WIDTH = GLA_WIDTH // 2
GLA_DK = GLA_KEY_WIDTH // GLA_HEADS
GATE_RANK = 16
GATE_TAU = 16.0
CHUNK = 64
CONV_GROUPS = 8
CONV_GROUP_DIM = CONV_WIDTH // CONV_GROUPS
CONV_K = 3
D_FF = 2816
FFN_RESIDUAL = 0.5
EPS = 1e-6
IN_SPLIT_SIZES = (GLA_KEY_WIDTH, GLA_KEY_WIDTH, GLA_WIDTH, GLA_WIDTH,
                  GATE_RANK, GATE_RANK, CONV_WIDTH, CONV_WIDTH, CONV_WIDTH)
N_IN = 2 * GLA_KEY_WIDTH + 2 * GLA_WIDTH + 2 * GATE_RANK + 3 * CONV_WIDTH

kernel_name = "hybrid_gla_shortconv_macaron_encoder"


def rms_norm(x, gain):
    xf = x.astype(jnp.float32)
    inv = lax.rsqrt(jnp.mean(xf * xf, axis=-1, keepdims=True) + EPS)
    return (xf * inv).astype(x.dtype) * gain


def swiglu(x, w_gate, w_up, w_down):
    return (jax.nn.silu(x @ w_gate) * (x @ w_up)) @ w_down


def gla_chunked(q, k, v, log_a):
    bsz, nh, seq_len, dk = q.shape
    dv = v.shape[-1]
    n_chunks = seq_len // CHUNK

    def to_chunks(t):
        return t.reshape(bsz, nh, n_chunks, CHUNK, t.shape[-1]).transpose(2, 0, 1, 3, 4)

    qc, kc, vc, ac = (to_chunks(t) for t in (q, k, v, log_a))
    cum = jnp.cumsum(ac, axis=3)
    total = cum[:, :, :, -1:, :]
    q_dec = qc * jnp.exp(cum)
    k_inv = kc * jnp.exp(-cum)
    k_tail = kc * jnp.exp(total - cum)
    causal_in_chunk = jnp.tril(jnp.ones((CHUNK, CHUNK), dtype=bool))
    scores = jnp.einsum('nbhid,nbhjd->nbhij', q_dec, k_inv)
    intra = jnp.einsum('nbhij,nbhjv->nbhiv', jnp.where(causal_in_chunk, scores, 0.0), vc)

    def step(state, inp):
        qd, kt, vv, tot = inp
        out = jnp.einsum('bhid,bhdv->bhiv', qd, state)
        state = state * jnp.exp(tot[:, :, 0, :])[..., None] + jnp.einsum('bhjd,bhjv->bhdv', kt, vv)
        return state, out

    s0 = jnp.zeros((bsz, nh, dk, dv), jnp.float32)
    _, inter = lax.scan(step, s0, (q_dec, k_tail, vc, total))
    o = intra + inter
    return o.transpose(1, 2, 0, 3, 4).reshape(bsz, nh, seq_len, dv)


def bidirectional_gla(q, k, v, log_a_fwd, log_a_bwd):
    flip = lambda t: jnp.flip(t, axis=2)
    fwd = gla_chunked(q, k, v, log_a_fwd)
    bwd = flip(gla_chunked(flip(q), flip(k), flip(v), flip(log_a_bwd)))
    return fwd + bwd


def token_mixer(u, w_in, gate_fwd_w, gate_fwd_b, gate_bwd_w, gate_bwd_b,
                gla_head_norm, conv_w, conv_group_norm, w_out):
    bsz, seq_len, _ = u.shape
    split_points = np.cumsum(IN_SPLIT_SIZES)[:-1].tolist()
    q, k, v, g, r_f, r_b, b_gate, c_gate, h_in = jnp.split(u @ w_in, split_points, axis=-1)

    def heads(t, d):
        return t.reshape(bsz, seq_len, GLA_HEADS, d).transpose(0, 2, 1, 3).astype(jnp.float32)

    qh = heads(q, GLA_DK) * (GLA_DK ** -0.5)
    kh = heads(k, GLA_DK)
    vh = heads(v, GLA_DV)
    log_a_f = heads(jax.nn.log_sigmoid((r_f @ gate_fwd_w + gate_fwd_b).astype(jnp.float32)), GLA_DK) / GATE_TAU
    log_a_b = heads(jax.nn.log_sigmoid((r_b @ gate_bwd_w + gate_bwd_b).astype(jnp.float32)), GLA_DK) / GATE_TAU
    o = bidirectional_gla(qh, kh, vh, log_a_f, log_a_b)
    o = o * lax.rsqrt(jnp.mean(o * o, axis=-1, keepdims=True) + EPS)
    o = o.transpose(0, 2, 1, 3).reshape(bsz, seq_len, GLA_WIDTH).astype(u.dtype)
    gla_out = o * gla_head_norm * jax.nn.silu(g)

    z = c_gate * h_in
    pad = CONV_K // 2
    zp = jnp.pad(z, ((0, 0), (pad, pad), (0, 0)))
    conv = sum(zp[:, i:i + seq_len] * conv_w[i] for i in range(CONV_K))
    yc = (b_gate * conv).reshape(bsz, seq_len, CONV_GROUPS, CONV_GROUP_DIM).astype(jnp.float32)
    yc = yc * lax.rsqrt(jnp.mean(yc * yc, axis=-1, keepdims=True) + EPS)
    conv_out = yc.reshape(bsz, seq_len, CONV_WIDTH).astype(u.dtype) * conv_group_norm

    return jnp.concatenate([gla_out, conv_out], axis=-1) @ w_out


def encoder_layer(x, ffn1_norm, ffn1_w_gate, ffn1_w_up, ffn1_w_down, mix_norm, w_in,
                  gate_fwd_w, gate_fwd_b, gate_bwd_w, gate_bwd_b, gla_head_norm,
                  conv_w, conv_group_norm, w_out, ffn2_norm, ffn2_w_gate, ffn2_w_up, ffn2_w_down):
    x = x + FFN_RESIDUAL * swiglu(rms_norm(x, ffn1_norm), ffn1_w_gate, ffn1_w_up, ffn1_w_down)
    x = x + token_mixer(rms_norm(x, mix_norm), w_in, gate_fwd_w, gate_fwd_b, gate_bwd_w, gate_bwd_b,
                        gla_head_norm, conv_w, conv_group_norm, w_out)
    x = x + FFN_RESIDUAL * swiglu(rms_norm(x, ffn2_norm), ffn2_w_gate, ffn2_w_up, ffn2_w_down)
    return x


def trunk(x, layer_params, final_norm):
    for l in range(DEPTH):
        x = encoder_layer(x, *[p[l] for p in layer_params])
    return rms_norm(x, final_norm)


def setup_inputs(seed: int = 0) -> dict:
    key = jax.random.key(seed)
    ks = jax.random.split(key, 24)
    f32 = jnp.float32
    nrm = lambda k, shape, scale: jax.random.normal(k, shape, f32) * scale
    gain = lambda k, shape: 1.0 + 0.02 * jax.random.normal(k, shape, f32)
    return {
        "x_prompt": jax.random.normal(ks[0], (BATCH, SEQ, D_MODEL), f32),
        "x_sample": jax.random.normal(ks[1], (DEC_BATCH, DEC_SEQ, D_MODEL), f32),
        "ffn1_norm": gain(ks[2], (DEPTH, D_MODEL)),
        "ffn1_w_gate": nrm(ks[3], (DEPTH, D_MODEL, D_FF), D_MODEL ** -0.5),
        "ffn1_w_up": nrm(ks[4], (DEPTH, D_MODEL, D_FF), D_MODEL ** -0.5),
        "ffn1_w_down": nrm(ks[5], (DEPTH, D_FF, D_MODEL), D_FF ** -0.5),
        "mix_norm": gain(ks[6], (DEPTH, D_MODEL)),
        "w_in": nrm(ks[7], (DEPTH, D_MODEL, N_IN), D_MODEL ** -0.5),
        "gate_fwd_w": nrm(ks[8], (DEPTH, GATE_RANK, GLA_KEY_WIDTH), GATE_RANK ** -0.5),
        "gate_fwd_b": nrm(ks[9], (DEPTH, GLA_KEY_WIDTH), 0.1),
        "gate_bwd_w": nrm(ks[10], (DEPTH, GATE_RANK, GLA_KEY_WIDTH), GATE_RANK ** -0.5),
        "gate_bwd_b": nrm(ks[11], (DEPTH, GLA_KEY_WIDTH), 0.1),
        "gla_head_norm": gain(ks[12], (DEPTH, GLA_WIDTH)),
        "conv_w": nrm(ks[13], (DEPTH, CONV_K, CONV_WIDTH), CONV_K ** -0.5),
        "conv_group_norm": gain(ks[14], (DEPTH, CONV_WIDTH)),
        "w_out": nrm(ks[15], (DEPTH, MIX_WIDTH, D_MODEL), MIX_WIDTH ** -0.5),
        "ffn2_norm": gain(ks[16], (DEPTH, D_MODEL)),
        "ffn2_w_gate": nrm(ks[17], (DEPTH, D_MODEL, D_FF), D_MODEL ** -0.5),
        "ffn2_w_up": nrm(ks[18], (DEPTH, D_MODEL, D_FF), D_MODEL ** -0.5),
        "ffn2_w_down": nrm(ks[19], (DEPTH, D_FF, D_MODEL), D_FF ** -0.5),
        "final_norm": gain(ks[20], (D_MODEL,)),
    }


def reference(x_prompt, x_sample, ffn1_norm, ffn1_w_gate, ffn1_w_up, ffn1_w_down, mix_norm,
              w_in, gate_fwd_w, gate_fwd_b, gate_bwd_w, gate_bwd_b, gla_head_norm, conv_w,
              conv_group_norm, w_out, ffn2_norm, ffn2_w_gate, ffn2_w_up, ffn2_w_down, final_norm):
    layer_params = (ffn1_norm, ffn1_w_gate, ffn1_w_up, ffn1_w_down, mix_norm, w_in,
                    gate_fwd_w, gate_fwd_b, gate_bwd_w, gate_bwd_b, gla_head_norm,
                    conv_w, conv_group_norm, w_out, ffn2_norm, ffn2_w_gate, ffn2_w_up, ffn2_w_down)
    y_prompt = trunk(x_prompt, layer_params, final_norm)
    y_sample = trunk(x_sample, layer_params, final_norm)
    return (y_prompt, y_sample)
```

```python
import os
import numpy as np
from contextlib import ExitStack
import concourse.bass as bass
import concourse.mybir as mybir
from concourse.bass_utils import run_bass_kernel_spmd

F32 = mybir.dt.float32
BF16 = mybir.dt.bfloat16
ALU = mybir.AluOpType
AF = mybir.ActivationFunctionType

NCORES = 8
D = 1024
DFF = 2816
KC = 8
FC = 22
TOK = 4096
T = 512
L = 2048
NSEG = 2
TPS = L // T
EPS = 1e-6
NSLOT = 4
SLOT = 2816
NDS = 40

N_IN_PAD = 3136
IN_PANELS = [(0, 256), (256, 256), (512, 256), (768, 256), (1024, 256), (1280, 256), (1536, 64),
             (1600, 256), (1856, 256), (2112, 256), (2368, 256), (2624, 256), (2880, 256)]
NCM = 770
P_G1, P_U1, P_D1, P_IN, P_OUT, P_G2, P_U2, P_D2 = 0, 11, 22, 30, 43, 47, 58, 69
NPAN = 80
NPC = NPAN // NCORES


class TR:
    def __init__(self, nc, stack):
        self.nc = nc
        self.eng = {"pe": nc.tensor, "act": nc.scalar, "dve": nc.vector,
                    "pool": nc.gpsimd, "sp": nc.sync}
        self.q = {k: [] for k in self.eng}
        self.tsem = {k: stack.enter_context(nc.semaphore("t_" + k))
                     for k in ("pe", "act", "dve", "pool")}
        self.tcnt = {k: 0 for k in self.tsem}
        self.dsem = [stack.enter_context(nc.semaphore("d%d" % i)) for i in range(NDS)]
        self.dcnt = [0] * NDS
        self.drr = 0
        self.seen = {k: {} for k in self.eng}
        self.lastw = {}
        self.readers = {}
        self.out_tickets = []
        self.defer = None

    def record(self, stage_fn):
        assert self.defer is None
        self.defer = []
        try:
            stage_fn()
            return self.defer
        finally:
            self.defer = None

    def emit_merged(self, *lists):
        lists = [l for l in lists if l]
        pos = [0] * len(lists)
        total = sum(len(l) for l in lists)
        for _ in range(total):
            best, bi = None, -1
            for i, l in enumerate(lists):
                if pos[i] < len(l):
                    frac = pos[i] / len(l)
                    if best is None or frac < best:
                        best, bi = frac, i
            e, fn, reads, writes = lists[bi][pos[bi]]
            pos[bi] += 1
            self.op(e, fn, reads, writes)

    def _sem(self, k):
        return self.tsem[k] if isinstance(k, str) else self.dsem[k[1]]

    def _deps(self, e, reads, writes):
        need = {}

        def add(t):
            if t is None:
                return
            k, v = t
            if need.get(k, 0) < v:
                need[k] = v
        for r in reads:
            add(self.lastw.get(r))
        for w in writes:
            add(self.lastw.get(w))
            for k, v in self.readers.get(w, {}).items():
                add((k, v))
        out = []
        for k, v in need.items():
            if k == e and e == "pe":
                continue
            if self.seen[e].get(k, 0) >= v:
                continue
            self.seen[e][k] = v
            out.append((k, v))
        return out

    def _commit(self, ticket, reads, writes):
        for w in writes:
            self.lastw[w] = ticket
            self.readers[w] = {}
        k, v = ticket
        for r in reads:
            d = self.readers.setdefault(r, {})
            if d.get(k, 0) < v:
                d[k] = v

    def op(self, e, fn, reads=(), writes=()):
        reads = list(reads)
        writes = list(writes)
        if self.defer is not None:
            self.defer.append((e, fn, reads, writes))
            return None
        waits = self._deps(e, reads, writes)
        self.tcnt[e] += 1
        ticket = (e, self.tcnt[e])
        sem = self.tsem[e]

        def run(eng):
            for k, v in waits:
                eng.wait_ge(self._sem(k), v)
            ins = fn(eng)
            ins.then_inc(sem, 1)
        self.q[e].append(run)
        self._commit(ticket, reads, writes)
        return ticket

    def dma(self, e, out, in_, reads=(), writes=(), is_output=False, fn=None):
        reads = list(reads)
        writes = list(writes)
        i = self.drr
        self.drr = (self.drr + 1) % NDS
        waits = self._deps(e, reads, writes)
        prev = self.dcnt[i]
        if prev > 0 and self.seen[e].get(("d", i), 0) < prev:
            waits.append((("d", i), prev))
            self.seen[e][("d", i)] = prev
        self.dcnt[i] += 16
        ticket = (("d", i), self.dcnt[i])
        sem = self.dsem[i]

        def run(eng):
            for k, v in waits:
                eng.wait_ge(self._sem(k), v)
            if fn is None:
                eng.dma_start(out=out, in_=in_).then_inc(sem, 16)
            else:
                fn(eng).then_inc(sem, 16)
        self.q[e].append(run)
        self._commit(ticket, reads, writes)
        if is_output:
            self.out_tickets.append(ticket)
        return ticket

    def barrier(self):
        snap = {k: v for k, v in self.tcnt.items() if v > 0}
        for e in ("pe", "act", "dve", "pool"):
            waits = []
            for k, v in snap.items():
                if k == e and e == "pe":
                    continue
                if self.seen[e].get(k, 0) >= v:
                    continue
                self.seen[e][k] = v
                waits.append((k, v))

            def run(eng, waits=waits):
                for k, v in waits:
                    eng.wait_ge(self._sem(k), v)
            self.q[e].append(run)

    def finish(self):
        need = {}
        for k, v in self.out_tickets:
            if need.get(k, 0) < v:
                need[k] = v
        waits = list(need.items())

        def run(eng):
            for k, v in waits:
                eng.wait_ge(self._sem(k), v)
        self.q["sp"].append(run)

    def emit(self):
        nc = self.nc
        with nc.Block() as block:
            @block.tensor
            def _(eng):
                for f in self.q["pe"]:
                    f(eng)

            @block.scalar
            def _(eng):
                for f in self.q["act"]:
                    f(eng)

            @block.vector
            def _(eng):
                for f in self.q["dve"]:
                    f(eng)

            @block.gpsimd
            def _(eng):
                for f in self.q["pool"]:
                    f(eng)

            @block.sync
            def _(eng):
                for f in self.q["sp"]:
                    f(eng)


def build_w():
    nc = bass.Bass("TRN2", target_bir_lowering=False)
    src = nc.dram_tensor("wsrc", [NPC, 128, SLOT], F32, kind="ExternalInput").ap()
    scl = nc.dram_tensor("wscale", [NPC, 128, SLOT], F32, kind="ExternalInput").ap()
    dst = nc.dram_tensor("wdst", [NPC, 128, SLOT], BF16, kind="ExternalOutput").ap()
    stack = ExitStack()
    with stack:
        tr = TR(nc, stack)
        sin = [stack.enter_context(nc.sbuf_tensor("sin%d" % i, [128, SLOT], F32)) for i in range(3)]
        ssc = [stack.enter_context(nc.sbuf_tensor("ssc%d" % i, [128, SLOT], F32)) for i in range(3)]
        sout = [stack.enter_context(nc.sbuf_tensor("sout%d" % i, [128, SLOT], BF16)) for i in range(3)]
        engs = ["dve", "pool", "dve"]
        for j in range(NPC):
            b = j % 3
            tr.dma("sp", sin[b][:, :], src[j], writes=[("sin", b)])
            tr.dma("act", ssc[b][:, :], scl[j], writes=[("ssc", b)])
            tr.op(engs[b], lambda e, b=b: e.tensor_tensor(sout[b][:, :], sin[b][:, :], ssc[b][:, :], ALU.mult),
                  reads=[("sin", b), ("ssc", b)], writes=[("sout", b)])
            tr.dma("sp", dst[j], sout[b][:, :], reads=[("sout", b)], writes=[("dst", j)], is_output=True)
        tr.finish()
        tr.emit()
    return nc


def bcast_mid(ap_, n):
    from concourse.ap import AP
    a = ap_.ap
    return AP(ap_.tensor, ap_.offset, [list(a[0]), [0, n]] + [list(x) for x in a[1:]])


SEGBUFS = [("sQ", 128, 2 * L), ("sK", 128, 2 * L), ("sKtok", 128, 16 * 256), ("sVtok", 128, 16 * 512),
           ("sR", 64, L), ("sM", 128, 8 * L), ("sZ", 128, 4 * (L + 2))]


def build_nc(stage=None, mode="B"):
    stage = int(os.environ.get('KSTAGE', '99')) if stage is None else stage
    nc = bass.Bass("TRN2", target_bir_lowering=False)
    dt = nc.dram_tensor
    xT = dt("xT", [128, KC, L], F32, kind="ExternalInput").ap()
    wpan = dt("wpan", [NPAN, 128, SLOT], BF16, kind="ExternalInput").ap()
    gains = dt("gains", [128, 4 * KC], F32, kind="ExternalInput").ap()
    gw_d = dt("gw", [64, 512], F32, kind="ExternalInput").ap()
    gb_d = dt("gb", [128, 512], F32, kind="ExternalInput").ap()
    vecs_d = dt("vecs", [128, 20], F32, kind="ExternalInput").ap()
    cmat_d = dt("cmat", [128, NCM], F32, kind="ExternalInput").ap()
    segd = {}
    if mode == "A":
        xout = dt("xout", [128, 528], F32, kind="ExternalOutput").ap()
        x1p = dt("x1p", [128, KC, L], F32, kind="ExternalOutput").ap()
        for nm, pp, n in SEGBUFS:
            segd[nm] = dt(nm, [pp, n], BF16, kind="ExternalOutput").ap()
    else:
        yT = dt("yT", [128, KC, TOK], F32, kind="ExternalOutput").ap()
        xall = dt("xall", [NCORES * 128, 528], F32, kind="ExternalInput").ap()
        x1p = dt("x1p", [128, KC, L], F32, kind="ExternalInput").ap()
        for nm, pp, n in SEGBUFS:
            segd[nm] = dt(nm, [pp, n], BF16, kind="ExternalInput").ap()
        x1s = dt("x1s", [128, KC, L], F32).ap()
    rmask_d = dt("rmask", [128, 32], F32, kind="ExternalInput").ap() if mode == "B" else None
    KDBG = int(os.environ.get('KDBG', '0'))
    if KDBG:
        dbg = dt("dbg", [128, 8 * L], BF16, kind="ExternalOutput").ap()

    stack = ExitStack()
    with stack:
        tr = TR(nc, stack)

        def sb(name, n, dtype, parts=128):
            return stack.enter_context(nc.sbuf_tensor(name, [parts, n], dtype))

        xt_t = sb("xt", KC * T, F32)
        xn_t = sb("xn", KC * T, BF16)
        h_t = sb("h", FC * T, BF16)
        ring_t = sb("ring", NSLOT * SLOT, BF16)
        sq_t = sb("sq", KC * T, BF16)
        rstd_t = sb("rstd", T, F32)
        sg_t = sb("sgt", 4 * T, F32)
        gains_t = sb("gains_sb", 4 * KC, F32)
        ones_t = sb("ones", 128, BF16)
        ones128_t = sb("ones128", 128, BF16)
        g64_t = sb("g64", 128, BF16)
        vecs_t = sb("vecs_sb", 20, F32)
        gwst_t = sb("gwst", 512, F32, parts=64)
        gw_t = sb("gw_sb", 512, BF16, parts=64)
        gb_t = sb("gb_sb", 512, F32)
        onesrow_t = sb("onesrow", 128, BF16, parts=1)
        cmb_t = sb("cmat_bf", NCM, BF16)
        sph_t = sb("sp_hi", 512, BF16)
        spl_t = sb("sp_lo", 512, BF16)
        qT_t = sb("qT", 2 * L, BF16)
        kT_t = sb("kT", 2 * L, BF16)
        ktok_t = sb("ktok", 16 * 256, BF16)
        vtok_t = sb("vtok", 16 * 512, BF16)
        rT_t = sb("rT", L, BF16, parts=64)
        m_t = sb("m", 8 * L, BF16)
        z_t = sb("z", 4 * (L + 2), BF16)
        srun_t = sb("srun", 2 * 2 * 128, F32)
        rs_t = sb("rs", T, F32)
        tt_t = sb("tt", T, F32)
        cv_t = sb("cv", 2 * T, F32)
        dec_t = sb("dec", 16, F32)
        ctmp_t = sb("ctmp", T, F32)
        rmask_t = sb("rmask_sb", 32, F32)
        dacc_t = sb("dacc", 4, F32)
        xst_t = sb("xst", 528, F32)
        xg_t = sb("xg", 2 * 528, F32)
        hal_t = sb("hal", 8, F32)
        dp_t = sb("dp", 4, F32)

        ps = [stack.enter_context(nc.psum_tensor("ps%d" % i, [128, 512], F32)) for i in range(8)]
        ps_pool = {"all": [0, 1, 2, 3, 4, 5], "G": [0, 1], "H": [2, 3, 4, 5], "H3": [2, 3, 4], "C": [5],
                   "G0": [0], "G1": [1], "H0": [2, 3], "H1": [4, 5]}
        ps_rr = {k: 0 for k in ps_pool}
        ps_cur = ["all"]

        def ps_next():
            pool = ps_cur[0]
            lst = ps_pool[pool]
            i = ps_rr[pool]
            ps_rr[pool] = (i + 1) % len(lst)
            return lst[i]

        def staged(pool, fn, *a):
            def run():
                ps_cur[0] = pool
                try:
                    fn(*a)
                finally:
                    ps_cur[0] = "all"
            return tr.record(run)

        xt = xt_t[:, :].rearrange("p (c n) -> p c n", n=T)
        xn = xn_t[:, :].rearrange("p (c n) -> p c n", n=T)
        hh = h_t[:, :].rearrange("p (c n) -> p c n", n=T)
        sq = sq_t[:, :].rearrange("p (c n) -> p c n", n=T)
        qT3 = qT_t[:, :].rearrange("p (c n) -> p c n", n=L)
        kT3 = kT_t[:, :].rearrange("p (c n) -> p c n", n=L)
        ktok3 = ktok_t[:, :].rearrange("p (b n) -> p b n", n=256)
        vtok3 = vtok_t[:, :].rearrange("p (b n) -> p b n", n=512)
        m3 = m_t[:, :].rearrange("p (c n) -> p c n", n=L)
        z3 = z_t[:, :].rearrange("p (c n) -> p c n", n=L + 2)
        sbsave = h_t[:, 0:32 * 256].rearrange("p (n q v) -> p n q v", q=2, v=128)
        srun = srun_t[:, :].rearrange("p (d q v) -> p d q v", q=2, v=128)
        XT_ALL = [("xt", c) for c in range(KC)]
        XN_ALL = [("xn", c) for c in range(KC)]
        SQ_ALL = [("sq", c) for c in range(KC)]
        H_ALL = [("h", k) for k in range(FC)]
        SGT_ALL = [("sgt", i) for i in range(4)]

        def cm(i):
            return cmb_t[:, i * 128:(i + 1) * 128]
        M_CF, M_CB, M_TB, M_TF, MASK_F, MASK_B = (cm(i) for i in range(6))
        CI = cmb_t[:, 768:770]

        tr.op("pool", lambda e: e.memset(ones_t[:, :], 1.0 / D), writes=["ones"])
        tr.op("pool", lambda e: e.memset(ones128_t[:, :], 1.0 / 128), writes=["ones128"])
        tr.op("pool", lambda e: e.memset(g64_t[:, :], 0.0), writes=["g64"])
        tr.op("pool", lambda e: e.memset(g64_t[0:64, 0:64], 1.0 / 64), reads=["g64"], writes=["g64"])
        tr.op("pool", lambda e: e.memset(g64_t[64:128, 64:128], 1.0 / 64), reads=["g64"], writes=["g64"])
        tr.op("pool", lambda e: e.memset(onesrow_t[:, :], 1.0), writes=["onesrow"])
        tr.dma("sp", gains_t[:, :], gains, writes=["gains"])
        cmat_t = xg_t[:, 0:NCM]
        tr.dma("sp", cmat_t, cmat_d, writes=[("xg", 0), ("xg", 1)])
        tr.dma("sp", vecs_t[:, :], vecs_d, writes=["vecs"])
        tr.dma("sp", gwst_t[:, :], gw_d, writes=["gwst"])
        tr.dma("sp", gb_t[:, :], gb_d, writes=["gb"])
        if mode == "B":
            tr.dma("sp", rmask_t[:, :], rmask_d, writes=["rmask"])
        tr.op("dve", lambda e: e.tensor_copy(gw_t[:, :], gwst_t[:, :]), reads=["gwst"], writes=["gw"])
        tr.op("dve", lambda e: e.tensor_copy(cmb_t[:, :], cmat_t), reads=[("xg", 0), ("xg", 1)], writes=["cmat"])


        KFAST = int(os.environ.get('KFAST', '0'))
        def PW(idx, n):
            return (wpan[idx][:, 0:n], n)
        panels = []

        def add_ffn(f):
            pg, pu, pd = (P_G1, P_U1, P_D1) if f == 1 else (P_G2, P_U2, P_D2)
            for j in range(11):
                panels.append(PW(pg + j, 2048))
                panels.append(PW(pu + j, 2048))
            for m in range(8):
                panels.append(PW(pd + m, 2816))

        def add_phase1():
            for ti in range(TPS):
                add_ffn(1)
                for pj, (c0, nc_) in enumerate(IN_PANELS):
                    panels.append(PW(P_IN + pj, 8 * nc_))

        def add_phase3():
            for ti in range(TPS):
                for j in range(4):
                    panels.append(PW(P_OUT + j, 2048))
                add_ffn(2)
        if mode == "A":
            add_phase1()
        else:
            add_phase1()
            add_phase3()
            add_phase3()
        ring_issued = [0]
        pi = [0]

        def ring_advance(n_done):
            lim = min(len(panels), n_done + NSLOT)
            while ring_issued[0] < lim:
                i = ring_issued[0]
                ap, n = panels[i]
                s = i % NSLOT
                tr.dma("sp", ring_t[:, s * SLOT:s * SLOT + n], ap, reads=[], writes=[("ring", s)])
                ring_issued[0] += 1

        def ring_get():
            i = pi[0]
            assert i < ring_issued[0], (i, ring_issued[0])
            s = i % NSLOT
            n = panels[i][1]
            pi[0] += 1
            return ring_t[:, s * SLOT:s * SLOT + n], ("ring", s)

        def act_rsqrt(out_ap, in_ap, rd, wr):
            tr.op("act", lambda e: e.activation(out_ap, in_ap, AF.Ln, bias=EPS), reads=rd, writes=wr)
            tr.op("act", lambda e: e.activation(out_ap, out_ap, AF.Exp, scale=-0.5), reads=wr, writes=wr)

        def rmsnorm_stats():
            tr.op("act", lambda e: e.activation(sq_t[:, :], xt_t[:, :], AF.Square),
                  reads=XT_ALL, writes=SQ_ALL)
            b = ps_next()

            def mm(e, b=b):
                ins = None
                for c in range(KC):
                    ins = e.matmul(ps[b][:, :], ones_t[:, :], sq[:, c, :], start=(c == 0), stop=(c == KC - 1))
                return ins
            tr.op("pe", mm, reads=SQ_ALL + ["ones"], writes=[("ps", b)])
            act_rsqrt(rstd_t[:, :], ps[b][:, :], [("ps", b)], ["rstd"])

        def stats_sq(m):
            tr.op("act", lambda e, m=m: e.activation(sq[:, m, :], xt[:, m, :], AF.Square),
                  reads=[("xt", m)], writes=[("sq", m)])

        def stats_mm(m):
            tr.op("pe", lambda e, m=m: e.matmul(ps[7][:, :], ones_t[:, :], sq[:, m, :], start=(m == 0), stop=(m == KC - 1)),
                  reads=[("sq", m), "ones"], writes=[("ps", 7)])

        def stats_chunk(m):
            stats_sq(m)
            if m > 0:
                stats_mm(m - 1)
            if m == KC - 1:
                stats_mm(m)

        def stats_finish():
            act_rsqrt(rstd_t[:, :], ps[7][:, :], [("ps", 7)], ["rstd"])

        def rmsnorm_apply(gi, out3, out_keys):
            if gi == 3:
                for c in range(KC):
                    tr.op("dve", lambda e, c=c: e.scalar_tensor_tensor(
                        out3[:, c, :], xt[:, c, :], gains_t[:, gi * KC + c:gi * KC + c + 1], rstd_t[:, :],
                        ALU.mult, ALU.mult),
                        reads=[("xt", c), "rstd", "gains"], writes=[out_keys[c]])
                return
            NV = 5
            tr.op("dve", lambda e: e.tensor_tensor(out3[:, 0:NV, :], xt[:, 0:NV, :], bcast_mid(rstd_t[:, :], NV), ALU.mult),
                  reads=[("xt", c) for c in range(NV)] + ["rstd"], writes=[out_keys[c] for c in range(NV)])
            tr.op("pool", lambda e: e.tensor_tensor(out3[:, NV:KC, :], xt[:, NV:KC, :], bcast_mid(rstd_t[:, :], KC - NV), ALU.mult),
                  reads=[("xt", c) for c in range(NV, KC)] + ["rstd"], writes=[out_keys[c] for c in range(NV, KC)])

        def mm_gu(e, w3, b, jj):
            ins = None
            for c in range(KC):
                ins = e.matmul(ps[b][:, :], w3[:, c, jj * 128:(jj + 1) * 128], xn[:, c, :],
                               start=(c == 0), stop=(c == KC - 1))
            return ins

        def mm_down(e, w3, b):
            ins = None
            for k in range(FC):
                ins = e.matmul(ps[b][:, :], w3[:, k, :], hh[:, k, :], start=(k == 0), stop=(k == FC - 1))
            return ins

        def sigmoid_chain(o, psb, bkey, okey):
            tr.op("act", lambda e: e.activation(o, psb, AF.Exp, scale=-1.0), reads=[bkey], writes=[okey])
            tr.op("act", lambda e: e.activation(o, o, AF.Ln, bias=1.0), reads=[okey], writes=[okey])
            tr.op("act", lambda e: e.activation(o, o, AF.Exp, scale=-1.0), reads=[okey], writes=[okey])

        def ffn():
            if KFAST:
                for _ in range(30):
                    ring_get()
                    ring_advance(pi[0])
                return
            for j in range(11):
                wg, kg = ring_get()
                wu, ku = ring_get()
                wg3 = wg.rearrange("p (c n) -> p c n", n=256)
                wu3 = wu.rearrange("p (c n) -> p c n", n=256)
                for jj in range(2):
                    fc = 2 * j + jj
                    bg = ps_next()
                    bu = ps_next()
                    tr.op("pe", lambda e, w3=wg3, b=bg, jj=jj: mm_gu(e, w3, b, jj), reads=XN_ALL + [kg], writes=[("ps", bg)])
                    tr.op("pe", lambda e, w3=wu3, b=bu, jj=jj: mm_gu(e, w3, b, jj), reads=XN_ALL + [ku], writes=[("ps", bu)])
                    sgv = sg_t[:, (fc % 2) * T:(fc % 2 + 1) * T]
                    sgw = sg_t[:, (2 + fc % 2) * T:(3 + fc % 2) * T]
                    ka = ("sgt", fc % 2)
                    kb = ("sgt", 2 + fc % 2)
                    sigmoid_chain(sgv, ps[bg][:, :], ("ps", bg), ka)
                    tr.op("dve", lambda e, b=bg, i=sgv, o=sgw: e.tensor_tensor(o, ps[b][:, :], i, ALU.mult),
                          reads=[("ps", bg), ka], writes=[kb])
                    tr.op("dve", lambda e, b=bu, i=sgw, fc=fc: e.tensor_tensor(hh[:, fc, :], ps[b][:, :], i, ALU.mult),
                          reads=[("ps", bu), kb], writes=[("h", fc)])
                ring_advance(pi[0])
            for m in range(8):
                wd, kd = ring_get()
                wd3 = wd.rearrange("p (k n) -> p k n", n=128)
                b = ps_next()
                tr.op("pe", lambda e, w3=wd3, b=b: mm_down(e, w3, b), reads=H_ALL + [kd], writes=[("ps", b)])
                tr.op("dve", lambda e, b=b, m=m: e.scalar_tensor_tensor(
                    xt[:, m, :], ps[b][:, :], 0.5, xt[:, m, :], ALU.mult, ALU.add),
                    reads=[("ps", b), ("xt", m)], writes=[("xt", m)])
                stats_chunk(m)
                ring_advance(pi[0])
            stats_finish()

        def mm_fm(e, w3, c0, M, b):
            ins = None
            for c in range(KC):
                ins = e.matmul(ps[b][0:M, :], w3[:, c, c0:c0 + M], xn[:, c, :], start=(c == 0), stop=(c == KC - 1))
            return ins

        def mm_tm(e, w3, blk, N, b):
            ins = None
            for c in range(KC):
                ins = e.matmul(ps[b][:, 0:N], xn[:, c, blk * 128:(blk + 1) * 128], w3[:, c, 0:N],
                               start=(c == 0), stop=(c == KC - 1))
            return ins

        evac_rr = [0]

        def evac(out_ap, in_ap, rd, wr, scale=None):
            e = "act" if evac_rr[0] % 2 == 0 else "dve"
            evac_rr[0] += 1
            if e == "act":
                if scale is None:
                    tr.op("act", lambda g: g.copy(out_ap, in_ap), reads=rd, writes=wr)
                else:
                    tr.op("act", lambda g: g.mul(out_ap, in_ap, scale), reads=rd, writes=wr)
            else:
                if scale is None:
                    tr.op("dve", lambda g: g.tensor_copy(out_ap, in_ap), reads=rd, writes=wr)
                else:
                    tr.op("dve", lambda g: g.tensor_scalar_mul(out_ap, in_ap, scale), reads=rd, writes=wr)

        def projection(ti):
            tl = ti * T
            w, kw = ring_get()
            w3 = w.rearrange("p (c n) -> p c n", n=256)
            for c2 in range(2):
                b = ps_next()
                tr.op("pe", lambda e, w3=w3, c2=c2, b=b: mm_fm(e, w3, c2 * 128, 128, b), reads=XN_ALL + [kw], writes=[("ps", b)])
                evac(qT3[:, c2, tl:tl + T], ps[b][:, :], [("ps", b)], [("qT", c2, ti)], scale=0.125)
            ring_advance(pi[0])
            w, kw = ring_get()
            w3 = w.rearrange("p (c n) -> p c n", n=256)
            for c2 in range(2):
                b = ps_next()
                tr.op("pe", lambda e, w3=w3, c2=c2, b=b: mm_fm(e, w3, c2 * 128, 128, b), reads=XN_ALL + [kw], writes=[("ps", b)])
                evac(kT3[:, c2, tl:tl + T], ps[b][:, :], [("ps", b)], [("kT", c2, ti)])
            for bl in range(4):
                blk = ti * 4 + bl
                b = ps_next()
                tr.op("pe", lambda e, w3=w3, bl=bl, b=b: mm_tm(e, w3, bl, 256, b), reads=XN_ALL + [kw], writes=[("ps", b)])
                evac(ktok3[:, blk, :], ps[b][:, 0:256], [("ps", b)], [("ktok", blk)])
            ring_advance(pi[0])
            for hv in range(2):
                w, kw = ring_get()
                w3 = w.rearrange("p (c n) -> p c n", n=256)
                for bl in range(4):
                    blk = ti * 4 + bl
                    b = ps_next()
                    tr.op("pe", lambda e, w3=w3, bl=bl, b=b: mm_tm(e, w3, bl, 256, b), reads=XN_ALL + [kw], writes=[("ps", b)])
                    evac(vtok3[:, blk, hv * 256:(hv + 1) * 256], ps[b][:, 0:256], [("ps", b)], [("vtok", blk, hv)])
                ring_advance(pi[0])
            for hg in range(2):
                w, kw = ring_get()
                w3 = w.rearrange("p (c n) -> p c n", n=256)
                for c2 in range(2):
                    c = hg * 2 + c2
                    b = ps_next()
                    tr.op("pe", lambda e, w3=w3, c2=c2, b=b: mm_fm(e, w3, c2 * 128, 128, b), reads=XN_ALL + [kw], writes=[("ps", b)])
                    sgv = sg_t[:, (c % 2) * T:(c % 2 + 1) * T]
                    ka = ("sgt", c % 2)
                    sigmoid_chain(sgv, ps[b][:, :], ("ps", b), ka)
                    tr.op("dve", lambda e, b=b, c=c, i=sgv: e.scalar_tensor_tensor(
                        m3[:, c, tl:tl + T], ps[b][:, :], vecs_t[:, 16 + c:17 + c], i, ALU.mult, ALU.mult),
                        reads=[("ps", b), ka, "vecs"], writes=[("m", c, ti)])
                ring_advance(pi[0])
            w, kw = ring_get()
            w3 = w.rearrange("p (c n) -> p c n", n=64)
            b = ps_next()
            tr.op("pe", lambda e, w3=w3, b=b: mm_fm(e, w3, 0, 64, b), reads=XN_ALL + [kw], writes=[("ps", b)])
            evac(rT_t[:, tl:tl + T], ps[b][0:64, :], [("ps", b)], [("rT", ti)])
            ring_advance(pi[0])
            for hb in range(2):
                w, kw = ring_get()
                w3 = w.rearrange("p (c n) -> p c n", n=256)
                for c2 in range(2):
                    c = hb * 2 + c2
                    b = ps_next()
                    tr.op("pe", lambda e, w3=w3, c2=c2, b=b: mm_fm(e, w3, c2 * 128, 128, b), reads=XN_ALL + [kw], writes=[("ps", b)])
                    evac(m3[:, 4 + c, tl:tl + T], ps[b][:, :], [("ps", b)], [("m", 4 + c, ti)])
                ring_advance(pi[0])
            cb = []
            for hc in range(2):
                w, kw = ring_get()
                w3 = w.rearrange("p (c n) -> p c n", n=256)
                for c2 in range(2):
                    b = ps_next()
                    cb.append(b)
                    tr.op("pe", lambda e, w3=w3, c2=c2, b=b: mm_fm(e, w3, c2 * 128, 128, b), reads=XN_ALL + [kw], writes=[("ps", b)])
                ring_advance(pi[0])
            cst = [cv_t[:, 0:T], cv_t[:, T:2 * T], ctmp_t[:, :], tt_t[:, :]]
            ckey = [("cv", 0), ("cv", 1), "ctmp", "tt"]
            for c in range(4):
                evac(cst[c], ps[cb[c]][:, :], [("ps", cb[c])], [ckey[c]])
            for hc in range(2):
                w, kw = ring_get()
                w3 = w.rearrange("p (c n) -> p c n", n=256)
                for c2 in range(2):
                    c = hc * 2 + c2
                    b = ps_next()
                    tr.op("pe", lambda e, w3=w3, c2=c2, b=b: mm_fm(e, w3, c2 * 128, 128, b), reads=XN_ALL + [kw], writes=[("ps", b)])
                    tr.op("dve", lambda e, b=b, c=c: e.tensor_tensor(
                        z3[:, c, 1 + tl:1 + tl + T], ps[b][:, :], cst[c], ALU.mult),
                        reads=[("ps", b), ckey[c]], writes=[("z", c, ti)])
                ring_advance(pi[0])

        def f32tmp(i):
            return sg_t[:, i * 256:(i + 1) * 256]
        SP = [f32tmp(0), f32tmp(1)]
        ET = f32tmp(2)
        ETD = [f32tmp(2), f32tmp(7)]
        SPH = [sph_t[:, 0:256], sph_t[:, 256:512]]
        SPL = [spl_t[:, 0:256], spl_t[:, 256:512]]
        EQ = [f32tmp(3), f32tmp(4)]
        EI = [f32tmp(5), f32tmp(6)]

        def b16tmp(i):
            return xn_t[:, i * 256:(i + 1) * 256]
        KTB = [[b16tmp(0), b16tmp(1)], [b16tmp(2), b16tmp(3)]]
        QDB = [[xn_t[:, 1024 + par * 512 + d * 256:1024 + par * 512 + (d + 1) * 256] for par in range(2)] for d in range(2)]
        KIB = [[xn_t[:, 2048 + par * 512 + d * 256:2048 + par * 512 + (d + 1) * 256] for par in range(2)] for d in range(2)]
        PM = [xn_t[:, 3072 + i * 128:3072 + (i + 1) * 128] for i in range(4)]
        SFB = [xn_t[:, 3584 + i * 256:3584 + (i + 1) * 256].rearrange("p (q v) -> p q v", v=128) for i in range(2)]
        DECB = [[0, 8], [4, 12]]

        def fence(keys):
            tr.barrier()

        def gate_sp(d, blk):
            b = ps_next()
            r0 = 0 if d == 0 else 32

            tr.op("pe", lambda e, b=b: e.matmul(ps[b][:, 0:256], rT_t[r0:r0 + 16, blk * 128:(blk + 1) * 128],
                                                gw_t[r0:r0 + 16, d * 256:(d + 1) * 256], start=True, stop=True),
                  reads=[("rT", blk // 4), "gw"], writes=[("ps", b)])
            tr.op("dve", lambda e, b=b: e.tensor_tensor(SP[d], ps[b][:, 0:256], gb_t[:, d * 256:(d + 1) * 256], ALU.add),
                  reads=[("ps", b), "gb"], writes=[("sp", d)])
            tr.op("act", lambda e: e.activation(SP[d], SP[d], AF.Exp, scale=-1.0), reads=[("sp", d)], writes=[("sp", d)])
            tr.op("act", lambda e: e.activation(SP[d], SP[d], AF.Ln, bias=1.0), reads=[("sp", d)], writes=[("sp", d)])
            tr.op("act", lambda e: e.copy(SPH[d], SP[d]), reads=[("sp", d)], writes=[("sph", d)])
            tr.op("dve", lambda e: e.tensor_tensor(SPL[d], SP[d], SPH[d], ALU.subtract),
                  reads=[("sp", d), ("sph", d)], writes=[("spl", d)])

        def gate_tail(d, blk, par):
            b = ps_next()
            Mt = M_TF if d == 0 else M_TB

            def mm(e, b=b):
                e.matmul(ps[b][:, 0:256], Mt, SPH[d], start=True, stop=False)
                return e.matmul(ps[b][:, 0:256], Mt, SPL[d], start=False, stop=True)
            tr.op("pe", mm, reads=[("sph", d), ("spl", d), "cmat"], writes=[("ps", b)])
            tr.op("act", lambda e, b=b: e.activation(ETD[d], ps[b][:, 0:256], AF.Exp), reads=[("ps", b)], writes=[("et", d)])
            tr.op("pool", lambda e: e.tensor_tensor(KTB[d][par], ktok3[:, blk, :], ETD[d], ALU.mult),
                  reads=[("et", d), ("ktok", blk)], writes=[("kt", d, par)])

        def gate_dec_ci(d, blk, par):
            b = ps_next()
            base = DECB[d][par]

            def mm(e, b=b):
                ins = None
                for q in range(2):
                    e.matmul(ps[b][:, q * 2:q * 2 + 2], SPH[d][:, q * 128:(q + 1) * 128], CI, start=True, stop=False)
                    ins = e.matmul(ps[b][:, q * 2:q * 2 + 2], SPL[d][:, q * 128:(q + 1) * 128], CI, start=False, stop=True)
                return ins
            tr.op("pe", mm, reads=[("sph", d), ("spl", d), "cmat"], writes=[("ps", b)])
            tr.op("act", lambda e, b=b: e.activation(dec_t[:, base:base + 4], ps[b][:, 0:4], AF.Exp),
                  reads=[("ps", b)], writes=[("dec", d, par)])

        def state_update(d, blk, ch, par):
            for q in range(2):
                b = ps_next()
                r0 = ch * 64
                tr.op("pe", lambda e, b=b, q=q, r0=r0: e.matmul(
                    ps[b][:, 0:256], KTB[d][par][r0:r0 + 64, q * 128:(q + 1) * 128],
                    vtok3[r0:r0 + 64, blk, q * 256:(q + 1) * 256], start=True, stop=True),
                    reads=[("kt", d, par), ("vtok", blk, q)], writes=[("ps", b)])
                for hh_ in range(2):
                    p0 = hh_ * 64
                    col = DECB[d][par] + q * 2 + ch
                    tr.op("dve", lambda e, b=b, q=q, p0=p0, col=col, hh_=hh_: e.scalar_tensor_tensor(
                        srun[p0:p0 + 64, d, q, :], srun[p0:p0 + 64, d, q, :], dec_t[p0:p0 + 64, col:col + 1],
                        ps[b][p0:p0 + 64, hh_ * 128:(hh_ + 1) * 128], ALU.mult, ALU.add),
                        reads=[("ps", b), ("dec", d, par), ("srun", d, q)], writes=[("srun", d, q)])

        SRUN_ALL = [("srun", d, q) for d in range(2) for q in range(2)]

        def local_scan_and_exchange(do_scan, do_fold):
            tr.barrier()
            tr.op("pool", lambda e: e.memset(srun_t[:, :], 0.0), writes=SRUN_ALL)
            tr.op("pool", lambda e: e.memset(dacc_t[:, :], 1.0), writes=[("dacc", 0), ("dacc", 1)])
            tr.op("pool", lambda e: e.memset(hal_t[:, :], 0.0), writes=["hal"])
            if do_scan:
                local_scan()
            if do_fold:
                fold_exchange()

        def local_scan():
            orders = {1: list(range(15, -1, -1)), 0: list(range(16))}

            def G(d, blk):
                gate_sp(d, blk)
                gate_tail(d, blk, blk % 2)
                gate_dec_ci(d, blk, blk % 2)

            def H(d, blk):
                par = blk % 2
                for ch in ((1, 0) if d == 1 else (0, 1)):
                    state_update(d, blk, ch, par)
                base = DECB[d][par]
                dv = dec_t[:, base:base + 4].rearrange("p (q c) -> p q c", c=2)
                for ch in range(2):
                    tr.op("dve", lambda e, d=d, dv=dv, ch=ch: e.tensor_tensor(
                        dacc_t[:, d * 2:d * 2 + 2], dacc_t[:, d * 2:d * 2 + 2], dv[:, :, ch], ALU.mult),
                        reads=[("dec", d, par), ("dacc", d)], writes=[("dacc", d)])
            G(1, orders[1][0])
            G(0, orders[0][0])
            for i in range(16):
                streams = []
                for d in (1, 0):
                    if i + 1 < 16:
                        streams.append(staged("G%d" % d, G, d, orders[d][i + 1]))
                    streams.append(staged("H%d" % d, H, d, orders[d][i]))
                tr.emit_merged(*streams)
            tr.op("dve", lambda e: e.tensor_copy(xst_t[:, 0:512], srun_t[:, :]), reads=SRUN_ALL, writes=["xst"])
            tr.op("dve", lambda e: e.tensor_copy(xst_t[:, 512:516], dacc_t[:, :]), reads=[("dacc", 0), ("dacc", 1), "xst"], writes=["xst"])
            tr.op("dve", lambda e: e.tensor_copy(xst_t[:, 516:520].rearrange("p (c o) -> p c o", o=1), z3[:, :, 1:2]),
                  reads=[("z", c, 0) for c in range(4)] + ["xst"], writes=["xst"])
            tr.op("dve", lambda e: e.tensor_copy(xst_t[:, 520:524].rearrange("p (c o) -> p c o", o=1), z3[:, :, L:L + 1]),
                  reads=[("z", c, TPS - 1) for c in range(4)] + ["xst"], writes=["xst"])
            tr.op("pool", lambda e: e.memset(xst_t[:, 524:528], 0.0), reads=["xst"], writes=["xst"])
            tr.dma("sp", xout, xst_t[:, :], reads=["xst"], writes=["xout"], is_output=True)

        def fold_exchange():
            step = 0
            for d in (0, 1):
                for cp in (range(NCORES) if d == 0 else range(NCORES - 1, -1, -1)):
                    sl = step % 2
                    step += 1
                    xg = xg_t[:, sl * 528:(sl + 1) * 528]
                    kx = ("xg", sl)
                    tr.dma("sp", xg, xall[cp * 128:(cp + 1) * 128, :], reads=[], writes=[kx])
                    w = rmask_t[:, d * 8 + cp:d * 8 + cp + 1]
                    for q in range(2):
                        dcol = xg[:, 512 + d * 2 + q:512 + d * 2 + q + 1]
                        dp = dp_t[:, q:q + 1]
                        tr.op("dve", lambda e, dp=dp, dcol=dcol, w=w: e.tensor_scalar(dp, dcol, -1.0, w, ALU.add, ALU.mult),
                              reads=[kx, "rmask"], writes=[("dp", q)])
                        tr.op("dve", lambda e, dp=dp: e.tensor_scalar_add(dp, dp, 1.0), reads=[("dp", q)], writes=[("dp", q)])
                        S = srun[:, d, q, :]
                        tr.op("dve", lambda e, S=S, dp=dp: e.tensor_scalar_mul(S, S, dp),
                              reads=[("dp", q), ("srun", d, q)], writes=[("srun", d, q)])
                        Lp = xg[:, d * 256 + q * 128:d * 256 + (q + 1) * 128]
                        tr.op("dve", lambda e, S=S, Lp=Lp, w=w: e.scalar_tensor_tensor(S, Lp, w, S, ALU.mult, ALU.add),
                              reads=[kx, "rmask", ("srun", d, q)], writes=[("srun", d, q)])
                    if d == 0:
                        tr.op("dve", lambda e, xg=xg, cp=cp: e.scalar_tensor_tensor(
                            hal_t[:, 0:4], xg[:, 520:524], rmask_t[:, 16 + cp:17 + cp], hal_t[:, 0:4], ALU.mult, ALU.add),
                            reads=[kx, "rmask", "hal"], writes=["hal"])
                        tr.op("dve", lambda e, xg=xg, cp=cp: e.scalar_tensor_tensor(
                            hal_t[:, 4:8], xg[:, 516:520], rmask_t[:, 24 + cp:25 + cp], hal_t[:, 4:8], ALU.mult, ALU.add),
                            reads=[kx, "rmask", "hal"], writes=["hal"])
            tr.op("dve", lambda e: e.tensor_copy(z3[:, :, 0:1], hal_t[:, 0:4].rearrange("p (c o) -> p c o", o=1)),
                  reads=["hal"], writes=[("zh", 0)])
            tr.op("dve", lambda e: e.tensor_copy(z3[:, :, L + 1:L + 2], hal_t[:, 4:8].rearrange("p (c o) -> p c o", o=1)),
                  reads=["hal"], writes=[("zh", 1)])

        def phase2(seg, zero_init=True):
            fence(H_ALL + SGT_ALL + XN_ALL)
            if zero_init:
                tr.op("pool", lambda e: e.memset(srun_t[:, :], 0.0), writes=SRUN_ALL)
            def GA(blk):
                gate_sp(1, blk)
                gate_tail(1, blk, blk % 2)
                gate_dec_ci(1, blk, blk % 2)

            def HA(blk):
                for ch in (1, 0):
                    n = blk * 2 + ch
                    tr.op("act", lambda e, n=n: e.copy(sbsave[:, n, :, :], srun[:, 1, :, :]),
                          reads=[("srun", 1, 0), ("srun", 1, 1)], writes=[("sbs", n)])
                    state_update(1, blk, ch, blk % 2)
            GA(15)
            for blk in range(15, -1, -1):
                g = staged("G", GA, blk - 1) if blk > 0 else []
                h = staged("H", HA, blk)
                tr.emit_merged(g, h)

            def GB(blk):
                par = blk % 2
                ti = blk // 4
                tok = slice(blk * 128, (blk + 1) * 128)
                SP2, SPH2, SPL2 = sg_t[:, 0:512], sph_t[:, 0:512], spl_t[:, 0:512]
                EQ2, EI2 = sg_t[:, 768:1280], sg_t[:, 1280:1792]
                SPK = [("sp", 0), ("sp", 1)]
                SPHK = [("sph", 0), ("sph", 1)]
                SPLK = [("spl", 0), ("spl", 1)]
                b = ps_next()

                tr.op("pe", lambda e, b=b: e.matmul(ps[b][:, :], rT_t[0:64, tok], gw_t[0:64, :], start=True, stop=True),
                      reads=[("rT", ti), "gw"], writes=[("ps", b)])
                tr.op("dve", lambda e, b=b: e.tensor_tensor(SP2, ps[b][:, :], gb_t[:, :], ALU.add),
                      reads=[("ps", b), "gb"], writes=SPK)
                tr.op("act", lambda e: e.activation(SP2, SP2, AF.Exp, scale=-1.0), reads=SPK, writes=SPK)
                tr.op("act", lambda e: e.activation(SP2, SP2, AF.Ln, bias=1.0), reads=SPK, writes=SPK)
                tr.op("act", lambda e: e.copy(SPH2, SP2), reads=SPK, writes=SPHK)
                tr.op("pool", lambda e: e.tensor_tensor(SPL2, SP2, SPH2, ALU.subtract), reads=SPK + SPHK, writes=SPLK)
                gate_tail(0, blk, par)
                b2 = ps_next()

                def mmc(e, b2=b2):
                    ins = None
                    for d in range(2):
                        Mc = M_CF if d == 0 else M_CB
                        for q in range(2):
                            o = ps[b2][:, d * 256 + q * 128:d * 256 + (q + 1) * 128]
                            e.matmul(o, SPH[d][:, q * 128:(q + 1) * 128], Mc, start=True, stop=False)
                            ins = e.matmul(o, SPL[d][:, q * 128:(q + 1) * 128], Mc, start=False, stop=True)
                    return ins
                tr.op("pe", mmc, reads=SPHK + SPLK + ["cmat"], writes=[("ps", b2)])
                tr.op("act", lambda e, b2=b2: e.activation(EQ2, ps[b2][:, :], AF.Exp),
                      reads=[("ps", b2)], writes=[("eq", 0), ("eq", 1)])
                tr.op("act", lambda e, b2=b2: e.activation(EI2, ps[b2][:, :], AF.Exp, scale=-1.0),
                      reads=[("ps", b2)], writes=[("ei", 0), ("ei", 1)])
                QD2 = xn_t[:, 1024 + par * 512:1024 + (par + 1) * 512].rearrange("p (d q v) -> p d q v", d=2, v=128)
                KI2 = xn_t[:, 2048 + par * 512:2048 + (par + 1) * 512].rearrange("p (d q v) -> p d q v", d=2, v=128)
                tr.op("dve", lambda e: e.tensor_tensor(
                    QD2, bcast_mid(qT3[:, :, tok], 2), EQ2.rearrange("p (d q v) -> p d q v", d=2, v=128), ALU.mult),
                    reads=[("eq", 0), ("eq", 1), ("qT", 0, ti), ("qT", 1, ti)], writes=[("qd", 0, par), ("qd", 1, par)])
                tr.op("pool", lambda e: e.tensor_tensor(
                    KI2, bcast_mid(kT3[:, :, tok], 2), EI2.rearrange("p (d q v) -> p d q v", d=2, v=128), ALU.mult),
                    reads=[("ei", 0), ("ei", 1), ("kT", 0, ti), ("kT", 1, ti)], writes=[("ki", 0, par), ("ki", 1, par)])
                base = DECB[0][par]
                tr.op("dve", lambda e, base=base: e.tensor_copy(
                    dec_t[:, base:base + 4], EQ[0].rearrange("p (j s) -> p j s", s=64)[:, :, 63]),
                    reads=[("eq", 0)], writes=[("dec", 0, par)])

            def HB(blk):
                par = blk % 2
                ti = blk // 4
                tok = slice(blk * 128, (blk + 1) * 128)
                ob = 6 + blk % 2
                QD = [QDB[0][par], QDB[1][par]]
                KI = [KIB[0][par], KIB[1][par]]
                for ch in range(2):
                    tr.op("act", lambda e, ch=ch: e.copy(SFB[ch], srun[:, 0, :, :]),
                          reads=[("srun", 0, 0), ("srun", 0, 1)], writes=[("sfb", ch)])
                    state_update(0, blk, ch, par)
                PMD = [sq_t[:, 1024:1536], sq_t[:, 1536:2048]]
                for hh_ in range(2):
                    p0 = hh_ * 64
                    for d in range(2):
                        b = ps_next()

                        def mms(e, b=b, d=d, p0=p0):
                            ins = None
                            for q in range(2):
                                ins = e.matmul(ps[b][:, q * 128:(q + 1) * 128], KI[d][p0:p0 + 64, q * 128:(q + 1) * 128],
                                               QD[d][p0:p0 + 64, q * 128:(q + 1) * 128], start=True, stop=True)
                            return ins
                        tr.op("pe", mms, reads=[("ki", d, par), ("qd", d, par)], writes=[("ps", b)])
                        mk = MASK_F if d == 0 else MASK_B
                        tr.op("dve", lambda e, b=b, d=d, mk=mk, hh_=hh_: e.tensor_tensor(
                            PMD[d].rearrange("p (q h v) -> p q h v", h=2, v=128)[:, :, hh_, :],
                            ps[b][:, 0:256].rearrange("p (q v) -> p q v", v=128),
                            bcast_mid(mk, 2), ALU.mult),
                            reads=[("ps", b), "cmat", ("sq", 2 + d)], writes=[("sq", 2 + d)])
                for h in range(4):
                    q, hh_ = h // 2, h % 2
                    p0 = hh_ * 64

                    def mo(e, h=h, q=q, p0=p0, ob=ob, blk=blk):
                        o = ps[ob][:, h * 128:(h + 1) * 128]
                        vh = vtok3[:, blk, h * 128:(h + 1) * 128]
                        e.matmul(o, vh, PMD[0][:, h * 128:(h + 1) * 128], start=True, stop=False)
                        e.matmul(o, vh, PMD[1][:, h * 128:(h + 1) * 128], start=False, stop=False)
                        ins = None
                        for ch in range(2):
                            oc = ps[ob][:, h * 128 + ch * 64:h * 128 + ch * 64 + 64]
                            e.matmul(oc, SFB[ch][p0:p0 + 64, q, :], QD[0][p0:p0 + 64, q * 128 + ch * 64:q * 128 + ch * 64 + 64],
                                     start=False, stop=False)
                            ins = e.matmul(oc, sbsave[p0:p0 + 64, blk * 2 + ch, q, :],
                                           QD[1][p0:p0 + 64, q * 128 + ch * 64:q * 128 + ch * 64 + 64],
                                           start=False, stop=(ch == 1))
                        return ins
                    tr.op("pe", mo, reads=[("sq", 2), ("sq", 3), ("vtok", blk, h // 2),
                                           ("sfb", 0), ("sfb", 1), ("sbs", blk * 2), ("sbs", blk * 2 + 1),
                                           ("qd", 0, par), ("qd", 1, par)],
                          writes=[("ps", ob)] if h == 0 else [("pso", ob, h)])
                OKEYS = [("ps", ob)] + [("pso", ob, h) for h in range(1, 4)]
                tr.op("act", lambda e, ob=ob: e.activation(sq_t[:, 0:512], ps[ob][:, :], AF.Square),
                      reads=OKEYS, writes=[("sq", 0)])
                b = ps_next()
                tr.op("pe", lambda e, b=b: e.matmul(ps[b][:, :], ones128_t[:, :], sq_t[:, 0:512], start=True, stop=True),
                      reads=[("sq", 0), "ones128"], writes=[("ps", b)])
                act_rsqrt(rs_t[:, :], ps[b][:, :], [("ps", b)], ["rs"])
                tr.op("dve", lambda e, ob=ob: e.tensor_tensor(tt_t[:, :], ps[ob][:, :], rs_t[:, :], ALU.mult),
                      reads=OKEYS + ["rs"], writes=["tt"])
                tr.op("pool", lambda e, tok=tok: e.tensor_tensor(
                    m3[:, 0:4, tok], tt_t[:, :].rearrange("p (h v) -> p h v", v=128), m3[:, 0:4, tok], ALU.mult),
                    reads=["tt"] + [("m", c, ti) for c in range(4)], writes=[("m", c, ti) for c in range(4)])
            def conv_item(ti, c):
                tl = ti * T
                a = cv_t[:, (c % 2) * T:(c % 2 + 1) * T]
                ka = ("cv", c % 2)
                zr = [("z", c, ti)]
                zr.append(("z", c, ti - 1) if ti > 0 else ("zh", 0))
                zr.append(("z", c, ti + 1) if ti < TPS - 1 else ("zh", 1))
                tr.op("dve", lambda e: e.tensor_scalar_mul(a, z3[:, c, tl:tl + T], vecs_t[:, c * 3:c * 3 + 1]),
                      reads=zr + ["vecs"], writes=[ka])
                for i in (1, 2):
                    tr.op("dve", lambda e, i=i: e.scalar_tensor_tensor(
                        a, z3[:, c, tl + i:tl + i + T], vecs_t[:, c * 3 + i:c * 3 + i + 1], a, ALU.mult, ALU.add),
                        reads=zr + ["vecs", ka], writes=[ka])
                tr.op("pool", lambda e: e.tensor_tensor(a, a, m3[:, 4 + c, tl:tl + T], ALU.mult),
                      reads=[ka, ("m", 4 + c, ti)], writes=[ka])
                tr.op("act", lambda e: e.activation(sq_t[:, 512:1024], a, AF.Square), reads=[ka], writes=[("sq", 1)])
                b = ps_next()
                tr.op("pe", lambda e: e.matmul(ps[b][:, :], g64_t[:, :], sq_t[:, 512:1024], start=True, stop=True),
                      reads=[("sq", 1), "g64"], writes=[("ps", b)])
                act_rsqrt(rstd_t[:, :], ps[b][:, :], [("ps", b)], ["rstd"])
                tr.op("dve", lambda e: e.scalar_tensor_tensor(
                    m3[:, 4 + c, tl:tl + T], a, vecs_t[:, 12 + c:13 + c], rstd_t[:, :], ALU.mult, ALU.mult),
                    reads=[ka, "rstd", "vecs"], writes=[("m", 4 + c, ti)])

            GB(0)
            for blk in range(16):
                g = staged("G", GB, blk + 1) if blk + 1 < 16 else []
                h = staged("H3", HB, blk)
                cst = staged("C", conv_item, blk // 4, blk % 4)
                tr.emit_merged(g, h, cst)
            fence(H_ALL + SGT_ALL + XN_ALL + [("sbs", n) for n in range(32)])

        def wout(ti):
            tl = ti * T
            for j in range(4):
                w, kw = ring_get()
                w3 = w.rearrange("p (c n) -> p c n", n=256)
                for c2 in range(2):
                    mo_ = j * 2 + c2
                    b = ps_next()

                    def mm(e, w3=w3, c2=c2, b=b):
                        ins = None
                        for k in range(KC):
                            ins = e.matmul(ps[b][:, :], w3[:, k, c2 * 128:(c2 + 1) * 128], m3[:, k, tl:tl + T],
                                           start=(k == 0), stop=(k == KC - 1))
                        return ins
                    tr.op("pe", mm, reads=[("m", k, ti) for k in range(KC)] + [kw], writes=[("ps", b)])
                    tr.op("dve", lambda e, b=b, mo_=mo_: e.tensor_tensor(xt[:, mo_, :], xt[:, mo_, :], ps[b][:, :], ALU.add),
                          reads=[("ps", b), ("xt", mo_)], writes=[("xt", mo_)])
                    stats_sq(mo_)
                ring_advance(pi[0])
            for mo_ in range(KC):
                stats_mm(mo_)
            stats_finish()

        SEG_KEYS = {
            "sQ": [("qT", c2, ti) for c2 in range(2) for ti in range(TPS)],
            "sK": [("kT", c2, ti) for c2 in range(2) for ti in range(TPS)],
            "sKtok": [("ktok", blk) for blk in range(16)],
            "sVtok": [("vtok", blk, hv) for blk in range(16) for hv in range(2)],
            "sR": [("rT", ti) for ti in range(TPS)],
            "sM": [("m", c, ti) for c in range(8) for ti in range(TPS)],
            "sZ": [("z", c, ti) for c in range(4) for ti in range(TPS)] + [("zh", 0), ("zh", 1)],
        }
        SEG_SB = {"sQ": qT_t, "sK": kT_t, "sKtok": ktok_t, "sVtok": vtok_t, "sR": rT_t, "sM": m_t, "sZ": z_t}
        ring_advance(0)

        def phase1(x1_dst):
            for ti in range(TPS):
                t0 = ti * T
                tr.dma("pool", xt, xT[:, :, t0:t0 + T], reads=[], writes=XT_ALL)
                rmsnorm_stats()
                rmsnorm_apply(0, xn, XN_ALL)
                ffn()
                tr.dma("pool", x1_dst[:, :, t0:t0 + T], xt, reads=XT_ALL, writes=[("x1s", t0)], is_output=(mode == "A"))
                rmsnorm_apply(1, xn, XN_ALL)
                projection(ti)

        def phase3(x1_src, x1_keyed, yoff, prefetch=None):
            for ti in range(TPS):
                t0 = ti * T
                tr.dma("pool", xt, x1_src[:, :, t0:t0 + T], reads=([("x1s", t0)] if x1_keyed else []), writes=XT_ALL)
                if prefetch is not None:
                    for nm in prefetch.get(ti, []):
                        tr.dma("act", SEG_SB[nm][:, :], segd[nm], reads=[], writes=SEG_KEYS[nm])
                wout(ti)
                rmsnorm_apply(2, xn, XN_ALL)
                ffn()
                if prefetch is not None:
                    sm3 = segd["sM"].rearrange("p (c n) -> p c n", n=L)
                    tr.dma("act", m3[:, :, t0:t0 + T], sm3[:, :, t0:t0 + T], reads=[],
                           writes=[("m", c, ti) for c in range(8)])
                yst = h_t[:, 0:2 * KC * T].bitcast(F32).rearrange("p (c n) -> p c n", n=T)
                YK = [("h", 2 * c) for c in range(KC)] + [("h", 2 * c + 1) for c in range(KC)]
                rmsnorm_apply(3, yst, [("h", 2 * c) for c in range(KC)])
                tr.dma("sp", yT[:, :, yoff + t0:yoff + t0 + T], yst, reads=YK, writes=[("yT", yoff + t0)] + [("h", 2 * c + 1) for c in range(KC)],
                       is_output=True)

        if mode == "A":
            phase1(x1p)
            local_scan_and_exchange(True, False)
            for nm, pp, n in SEGBUFS:
                tr.dma("pool", segd[nm], SEG_SB[nm][:, :], reads=SEG_KEYS[nm], writes=[("segd", nm)], is_output=True)
        else:
            tr.op("pool", lambda e: e.memset(z3[:, :, 0:1], 0.0), writes=[("zh", 0)])
            tr.op("pool", lambda e: e.memset(z3[:, :, L + 1:L + 2], 0.0), writes=[("zh", 1)])
            phase1(x1s)
            phase2(0)
            phase3(x1s, True, 0, prefetch={0: ["sQ", "sK", "sR"], 1: ["sKtok", "sVtok"], 2: ["sZ"]})
            local_scan_and_exchange(False, True)
            phase2(1, zero_init=False)
            phase3(x1p, False, L)
        assert pi[0] == len(panels), (pi[0], len(panels))
        print('TCNT', tr.tcnt, max(tr.dcnt))
        tr.finish()
        tr.emit()
    return nc


def _cmat():
    a = np.arange(128)[:, None]
    b = np.arange(128)[None, :]
    same = (a // 64) == (b // 64)
    f32 = np.float32
    u1 = (same & (a <= b)).astype(f32)
    l1 = (same & (a >= b)).astype(f32)
    su = (same & (a < b)).astype(f32)
    sl = (same & (a > b)).astype(f32)
    ci = ((np.arange(128)[:, None] // 64) == np.arange(2)[None, :]).astype(f32)
    s = f32(-1.0 / 16.0)
    return np.ascontiguousarray(np.concatenate([s * u1, s * l1, s * su, s * sl, u1, l1, s * ci], axis=1))


def _panelize(inputs):
    f32 = np.float32
    out = np.zeros((NPAN, 128, SLOT), f32)

    def kmajor(w):
        K, n = w.shape
        return w.reshape(K // 128, 128, n).transpose(1, 0, 2).reshape(128, (K // 128) * n)
    for f, (pg, pu, pd) in ((1, (P_G1, P_U1, P_D1)), (2, (P_G2, P_U2, P_D2))):
        wg = np.asarray(inputs["ffn%d_w_gate" % f], f32)[0]
        wu = np.asarray(inputs["ffn%d_w_up" % f], f32)[0]
        wd = np.asarray(inputs["ffn%d_w_down" % f], f32)[0]
        for j in range(11):
            out[pg + j, :, :2048] = kmajor(wg[:, 256 * j:256 * j + 256])
            out[pu + j, :, :2048] = kmajor(wu[:, 256 * j:256 * j + 256])
        for m in range(8):
            out[pd + m, :, :2816] = kmajor(wd[:, 128 * m:128 * m + 128])
    w_in = np.asarray(inputs["w_in"], f32)[0]
    w_in_p = np.zeros((D, N_IN_PAD), f32)
    w_in_p[:, 0:1536] = w_in[:, 0:1536]
    w_in_p[:, 1536:1552] = w_in[:, 1536:1552]
    w_in_p[:, 1568:1584] = w_in[:, 1552:1568]
    w_in_p[:, 1600:] = w_in[:, 1568:]
    for pj, (c0, n) in enumerate(IN_PANELS):
        out[P_IN + pj, :, :8 * n] = kmajor(w_in_p[:, c0:c0 + n])
    wo = np.asarray(inputs["w_out"], f32)[0]
    for j in range(4):
        out[P_OUT + j, :, :2048] = kmajor(wo[:, 256 * j:256 * j + 256])
    scale = np.ones((NPAN, 128, SLOT), f32)

    def rowgain(g, n):
        gv = np.asarray(g, f32).reshape(KC, 128).T
        return np.repeat(gv[:, :, None], n, axis=2).reshape(128, KC * n)
    for j in range(11):
        scale[P_G1 + j, :, :2048] = rowgain(inputs["ffn1_norm"][0], 256)
        scale[P_U1 + j, :, :2048] = rowgain(inputs["ffn1_norm"][0], 256)
        scale[P_G2 + j, :, :2048] = rowgain(inputs["ffn2_norm"][0], 256)
        scale[P_U2 + j, :, :2048] = rowgain(inputs["ffn2_norm"][0], 256)
    for pj, (c0, n) in enumerate(IN_PANELS):
        scale[P_IN + pj, :, :8 * n] = rowgain(inputs["mix_norm"][0], n)
    return out, scale


def _small_inputs(inputs):
    f32 = np.float32

    def vec(v):
        return np.ascontiguousarray(np.asarray(v, f32).reshape(KC, 128).T)
    gains = np.concatenate([vec(inputs["ffn1_norm"][0]), vec(inputs["mix_norm"][0]),
                            vec(inputs["ffn2_norm"][0]), vec(inputs["final_norm"])], axis=1)
    gw = np.zeros((64, 512), f32)
    gw[0:16, 0:256] = np.asarray(inputs["gate_fwd_w"], f32)[0]
    gw[32:48, 256:512] = np.asarray(inputs["gate_bwd_w"], f32)[0]
    gb = np.concatenate([np.asarray(inputs["gate_fwd_b"], f32)[0], np.asarray(inputs["gate_bwd_b"], f32)[0]])[None, :]
    gb = np.ascontiguousarray(np.broadcast_to(gb, (128, 512)))
    cw = np.asarray(inputs["conv_w"], f32)[0]
    vecs = np.zeros((128, 20), f32)
    for c in range(4):
        for i in range(3):
            vecs[:, c * 3 + i] = cw[i, c * 128:(c + 1) * 128]
        vecs[:, 12 + c] = np.asarray(inputs["conv_group_norm"], f32)[0, c * 128:(c + 1) * 128]
        vecs[:, 16 + c] = np.asarray(inputs["gla_head_norm"], f32)[0, c * 128:(c + 1) * 128]
    return {"gains": np.ascontiguousarray(gains), "gw": gw, "gb": gb, "vecs": vecs, "cmat": _cmat()}


def _tmajor(x):
    return np.ascontiguousarray(x.reshape(x.shape[0], KC, 128).transpose(2, 1, 0))


_NC_CACHE = {}


def kernel(**inputs):
    f32 = np.float32
    if "W" not in _NC_CACHE:
        _NC_CACHE["W"] = build_w()
        _NC_CACHE["A"] = build_nc(mode="A")
        _NC_CACHE["B"] = build_nc(mode="B")
    cores = list(range(NCORES))
    pan32, scale32 = _panelize(inputs)
    resW = run_bass_kernel_spmd(_NC_CACHE["W"], [{"wsrc": np.ascontiguousarray(pan32[i * NPC:(i + 1) * NPC]),
                                                  "wscale": np.ascontiguousarray(scale32[i * NPC:(i + 1) * NPC])}
                                                 for i in cores], core_ids=cores)
    wpan = np.ascontiguousarray(np.concatenate([np.asarray(resW.results[i]["wdst"]) for i in cores], axis=0))
    small = _small_inputs(inputs)
    xp = np.asarray(inputs["x_prompt"], f32)[0]
    xs = np.asarray(inputs["x_sample"], f32)
    mapsA = []
    for i in cores:
        m = dict(small)
        m["wpan"] = wpan
        m["xT"] = _tmajor(xp[i * L:(i + 1) * L])
        mapsA.append(m)
    resA = run_bass_kernel_spmd(_NC_CACHE["A"], mapsA, core_ids=cores)
    xall = np.ascontiguousarray(np.concatenate([np.asarray(resA.results[i]["xout"]) for i in cores], axis=0))
    mapsB = []
    for i in cores:
        m = dict(small)
        m["wpan"] = wpan
        m["xT"] = _tmajor(xs[i])
        m["xall"] = xall
        m["x1p"] = np.asarray(resA.results[i]["x1p"])
        for nm, pp, n in SEGBUFS:
            m[nm] = np.asarray(resA.results[i][nm])
        rm = np.zeros((128, 32), f32)
        for cp in range(NCORES):
            rm[:, cp] = 1.0 if cp < i else 0.0
            rm[:, 8 + cp] = 1.0 if cp > i else 0.0
            rm[:, 16 + cp] = 1.0 if cp == i - 1 else 0.0
            rm[:, 24 + cp] = 1.0 if cp == i + 1 else 0.0
        m["rmask"] = rm
        mapsB.append(m)
    res = run_bass_kernel_spmd(_NC_CACHE["B"], mapsB, core_ids=cores)
    yp = np.empty((1, NCORES * L, D), np.float32)
    ys = np.empty((NCORES, L, D), np.float32)
    for i in range(NCORES):
        y = np.asarray(res.results[i]["yT"])
        yc = y.transpose(2, 1, 0).reshape(TOK, D)
        ys[i] = yc[:L]
        yp[0, i * L:(i + 1) * L] = yc[L:]
    return (yp, ys)
```

```python
import os
import numpy as np
from contextlib import ExitStack
import concourse.bass as bass
import concourse.mybir as mybir
from concourse.bass_utils import run_bass_kernel_spmd

F32 = mybir.dt.float32
BF16 = mybir.dt.bfloat16
ALU = mybir.AluOpType
AF = mybir.ActivationFunctionType

NCORES = 8
D = 1024
DFF = 2816
KC = 8
FC = 22
TOK = 4096
T = 512
L = 2048
NSEG = 2
TPS = L // T
EPS = 1e-6
NSLOT = 4
SLOT = 2816
NDS = 40

N_IN_PAD = 3136
IN_PANELS = [(0, 256), (256, 256), (512, 256), (768, 256), (1024, 256), (1280, 256), (1536, 64),
             (1600, 256), (1856, 256), (2112, 256), (2368, 256), (2624, 256), (2880, 256)]
NCM = 770
P_G1, P_U1, P_D1, P_IN, P_OUT, P_G2, P_U2, P_D2 = 0, 11, 22, 30, 43, 47, 58, 69
NPAN = 80
NPC = NPAN // NCORES


class TR:
    def __init__(self, nc, stack):
        self.nc = nc
        self.eng = {"pe": nc.tensor, "act": nc.scalar, "dve": nc.vector,
                    "pool": nc.gpsimd, "sp": nc.sync}
        self.q = {k: [] for k in self.eng}
        self.tsem = {k: stack.enter_context(nc.semaphore("t_" + k))
                     for k in ("pe", "act", "dve", "pool")}
        self.tcnt = {k: 0 for k in self.tsem}
        self.dsem = [stack.enter_context(nc.semaphore("d%d" % i)) for i in range(NDS)]
        self.dcnt = [0] * NDS
        self.drr = 0
        self.seen = {k: {} for k in self.eng}
        self.lastw = {}
        self.readers = {}
        self.out_tickets = []
        self.defer = None

    def record(self, stage_fn):
        assert self.defer is None
        self.defer = []
        try:
            stage_fn()
            return self.defer
        finally:
            self.defer = None

    def emit_merged(self, *lists):
        lists = [l for l in lists if l]
        pos = [0] * len(lists)
        total = sum(len(l) for l in lists)
        for _ in range(total):
            best, bi = None, -1
            for i, l in enumerate(lists):
                if pos[i] < len(l):
                    frac = pos[i] / len(l)
                    if best is None or frac < best:
                        best, bi = frac, i
            e, fn, reads, writes = lists[bi][pos[bi]]
            pos[bi] += 1
            self.op(e, fn, reads, writes)

    def _sem(self, k):
        return self.tsem[k] if isinstance(k, str) else self.dsem[k[1]]

    def _deps(self, e, reads, writes):
        need = {}

        def add(t):
            if t is None:
                return
            k, v = t
            if need.get(k, 0) < v:
                need[k] = v
        for r in reads:
            add(self.lastw.get(r))
        for w in writes:
            add(self.lastw.get(w))
            for k, v in self.readers.get(w, {}).items():
                add((k, v))
        out = []
        for k, v in need.items():
            if k == e and e == "pe":
                continue
            if self.seen[e].get(k, 0) >= v:
                continue
            self.seen[e][k] = v
            out.append((k, v))
        return out

    def _commit(self, ticket, reads, writes):
        for w in writes:
            self.lastw[w] = ticket
            self.readers[w] = {}
        k, v = ticket
        for r in reads:
            d = self.readers.setdefault(r, {})
            if d.get(k, 0) < v:
                d[k] = v

    def op(self, e, fn, reads=(), writes=()):
        reads = list(reads)
        writes = list(writes)
        if self.defer is not None:
            self.defer.append((e, fn, reads, writes))
            return None
        waits = self._deps(e, reads, writes)
        self.tcnt[e] += 1
        ticket = (e, self.tcnt[e])
        sem = self.tsem[e]

        def run(eng):
            for k, v in waits:
                eng.wait_ge(self._sem(k), v)
            ins = fn(eng)
            ins.then_inc(sem, 1)
        self.q[e].append(run)
        self._commit(ticket, reads, writes)
        return ticket

    def dma(self, e, out, in_, reads=(), writes=(), is_output=False, fn=None):
        reads = list(reads)
        writes = list(writes)
        i = self.drr
        self.drr = (self.drr + 1) % NDS
        waits = self._deps(e, reads, writes)
        prev = self.dcnt[i]
        if prev > 0 and self.seen[e].get(("d", i), 0) < prev:
            waits.append((("d", i), prev))
            self.seen[e][("d", i)] = prev
        self.dcnt[i] += 16
        ticket = (("d", i), self.dcnt[i])
        sem = self.dsem[i]

        def run(eng):
            for k, v in waits:
                eng.wait_ge(self._sem(k), v)
            if fn is None:
                eng.dma_start(out=out, in_=in_).then_inc(sem, 16)
            else:
                fn(eng).then_inc(sem, 16)
        self.q[e].append(run)
        self._commit(ticket, reads, writes)
        if is_output:
            self.out_tickets.append(ticket)
        return ticket

    def barrier(self):
        snap = {k: v for k, v in self.tcnt.items() if v > 0}
        for e in ("pe", "act", "dve", "pool"):
            waits = []
            for k, v in snap.items():
                if k == e and e == "pe":
                    continue
                if self.seen[e].get(k, 0) >= v:
                    continue
                self.seen[e][k] = v
                waits.append((k, v))

            def run(eng, waits=waits):
                for k, v in waits:
                    eng.wait_ge(self._sem(k), v)
            self.q[e].append(run)

    def finish(self):
        need = {}
        for k, v in self.out_tickets:
            if need.get(k, 0) < v:
                need[k] = v
        waits = list(need.items())

        def run(eng):
            for k, v in waits:
                eng.wait_ge(self._sem(k), v)
        self.q["sp"].append(run)

    def emit(self):
        nc = self.nc
        with nc.Block() as block:
            @block.tensor
            def _(eng):
                for f in self.q["pe"]:
                    f(eng)

            @block.scalar
            def _(eng):
                for f in self.q["act"]:
                    f(eng)

            @block.vector
            def _(eng):
                for f in self.q["dve"]:
                    f(eng)

            @block.gpsimd
            def _(eng):
                for f in self.q["pool"]:
                    f(eng)

            @block.sync
            def _(eng):
                for f in self.q["sp"]:
                    f(eng)


def build_w():
    nc = bass.Bass("TRN2", target_bir_lowering=False)
    src = nc.dram_tensor("wsrc", [NPC, 128, SLOT], F32, kind="ExternalInput").ap()
    scl = nc.dram_tensor("wscale", [NPC, 128, SLOT], F32, kind="ExternalInput").ap()
    dst = nc.dram_tensor("wdst", [NPC, 128, SLOT], BF16, kind="ExternalOutput").ap()
    stack = ExitStack()
    with stack:
        tr = TR(nc, stack)
        sin = [stack.enter_context(nc.sbuf_tensor("sin%d" % i, [128, SLOT], F32)) for i in range(3)]
        ssc = [stack.enter_context(nc.sbuf_tensor("ssc%d" % i, [128, SLOT], F32)) for i in range(3)]
        sout = [stack.enter_context(nc.sbuf_tensor("sout%d" % i, [128, SLOT], BF16)) for i in range(3)]
        engs = ["dve", "pool", "dve"]
        for j in range(NPC):
            b = j % 3
            tr.dma("sp", sin[b][:, :], src[j], writes=[("sin", b)])
            tr.dma("act", ssc[b][:, :], scl[j], writes=[("ssc", b)])
            tr.op(engs[b], lambda e, b=b: e.tensor_tensor(sout[b][:, :], sin[b][:, :], ssc[b][:, :], ALU.mult),
                  reads=[("sin", b), ("ssc", b)], writes=[("sout", b)])
            tr.dma("sp", dst[j], sout[b][:, :], reads=[("sout", b)], writes=[("dst", j)], is_output=True)
        tr.finish()
        tr.emit()
    return nc


def bcast_mid(ap_, n):
    from concourse.ap import AP
    a = ap_.ap
    return AP(ap_.tensor, ap_.offset, [list(a[0]), [0, n]] + [list(x) for x in a[1:]])


SEGBUFS = [("sQ", 128, 2 * L), ("sK", 128, 2 * L), ("sKtok", 128, 16 * 256), ("sVtok", 128, 16 * 512),
           ("sR", 64, L), ("sM", 128, 8 * L), ("sZ", 128, 4 * (L + 2))]


def build_nc(stage=None, mode="B"):
    stage = int(os.environ.get('KSTAGE', '99')) if stage is None else stage
    nc = bass.Bass("TRN2", target_bir_lowering=False)
    dt = nc.dram_tensor
    xT = dt("xT", [128, KC, L], F32, kind="ExternalInput").ap()
    wpan = dt("wpan", [NPAN, 128, SLOT], BF16, kind="ExternalInput").ap()
    gains = dt("gains", [128, 4 * KC], F32, kind="ExternalInput").ap()
    gw_d = dt("gw", [64, 512], F32, kind="ExternalInput").ap()
    gb_d = dt("gb", [128, 512], F32, kind="ExternalInput").ap()
    vecs_d = dt("vecs", [128, 20], F32, kind="ExternalInput").ap()
    cmat_d = dt("cmat", [128, NCM], F32, kind="ExternalInput").ap()
    segd = {}
    if mode == "A":
        xout = dt("xout", [128, 528], F32, kind="ExternalOutput").ap()
        x1p = dt("x1p", [128, KC, L], F32, kind="ExternalOutput").ap()
        for nm, pp, n in SEGBUFS:
            segd[nm] = dt(nm, [pp, n], BF16, kind="ExternalOutput").ap()
    else:
        yT = dt("yT", [128, KC, TOK], F32, kind="ExternalOutput").ap()
        xall = dt("xall", [NCORES * 128, 528], F32, kind="ExternalInput").ap()
        x1p = dt("x1p", [128, KC, L], F32, kind="ExternalInput").ap()
        for nm, pp, n in SEGBUFS:
            segd[nm] = dt(nm, [pp, n], BF16, kind="ExternalInput").ap()
        x1s = dt("x1s", [128, KC, L], F32).ap()
    rmask_d = dt("rmask", [128, 32], F32, kind="ExternalInput").ap() if mode == "B" else None
    KDBG = int(os.environ.get('KDBG', '0'))
    if KDBG:
        dbg = dt("dbg", [128, 8 * L], BF16, kind="ExternalOutput").ap()

    stack = ExitStack()
    with stack:
        tr = TR(nc, stack)

        def sb(name, n, dtype, parts=128):
            return stack.enter_context(nc.sbuf_tensor(name, [parts, n], dtype))

        xt_t = sb("xt", KC * T, F32)
        xn_t = sb("xn", KC * T, BF16)
        h_t = sb("h", FC * T, BF16)
        ring_t = sb("ring", NSLOT * SLOT, BF16)
        sq_t = sb("sq", KC * T, BF16)
        rstd_t = sb("rstd", T, F32)
        sg_t = sb("sgt", 4 * T, F32)
        gains_t = sb("gains_sb", 4 * KC, F32)
        ones_t = sb("ones", 128, BF16)
        ones128_t = sb("ones128", 128, BF16)
        g64_t = sb("g64", 128, BF16)
        vecs_t = sb("vecs_sb", 20, F32)
        gwst_t = sb("gwst", 512, F32, parts=64)
        gw_t = sb("gw_sb", 512, BF16, parts=64)
        gb_t = sb("gb_sb", 512, F32)
        onesrow_t = sb("onesrow", 128, BF16, parts=1)
        cmb_t = sb("cmat_bf", NCM, BF16)
        sph_t = sb("sp_hi", 512, BF16)
        spl_t = sb("sp_lo", 512, BF16)
        qT_t = sb("qT", 2 * L, BF16)
        kT_t = sb("kT", 2 * L, BF16)
        ktok_t = sb("ktok", 16 * 256, BF16)
        vtok_t = sb("vtok", 16 * 512, BF16)
        rT_t = sb("rT", L, BF16, parts=64)
        m_t = sb("m", 8 * L, BF16)
        z_t = sb("z", 4 * (L + 2), BF16)
        srun_t = sb("srun", 2 * 2 * 128, F32)
        rs_t = sb("rs", T, F32)
        tt_t = sb("tt", T, F32)
        cv_t = sb("cv", 2 * T, F32)
        dec_t = sb("dec", 16, F32)
        ctmp_t = sb("ctmp", T, F32)
        rmask_t = sb("rmask_sb", 32, F32)
        dacc_t = sb("dacc", 4, F32)
        xst_t = sb("xst", 528, F32)
        xg_t = sb("xg", 2 * 528, F32)
        hal_t = sb("hal", 8, F32)
        dp_t = sb("dp", 4, F32)

        ps = [stack.enter_context(nc.psum_tensor("ps%d" % i, [128, 512], F32)) for i in range(8)]
        ps_pool = {"all": [0, 1, 2, 3, 4, 5], "G": [0, 1], "H": [2, 3, 4, 5], "H3": [2, 3, 4], "C": [5],
                   "G0": [0], "G1": [1], "H0": [2, 3], "H1": [4, 5]}
        ps_rr = {k: 0 for k in ps_pool}
        ps_cur = ["all"]

        def ps_next():
            pool = ps_cur[0]
            lst = ps_pool[pool]
            i = ps_rr[pool]
            ps_rr[pool] = (i + 1) % len(lst)
            return lst[i]

        def staged(pool, fn, *a):
            def run():
                ps_cur[0] = pool
                try:
                    fn(*a)
                finally:
                    ps_cur[0] = "all"
            return tr.record(run)

        xt = xt_t[:, :].rearrange("p (c n) -> p c n", n=T)
        xn = xn_t[:, :].rearrange("p (c n) -> p c n", n=T)
        hh = h_t[:, :].rearrange("p (c n) -> p c n", n=T)
        sq = sq_t[:, :].rearrange("p (c n) -> p c n", n=T)
        qT3 = qT_t[:, :].rearrange("p (c n) -> p c n", n=L)
        kT3 = kT_t[:, :].rearrange("p (c n) -> p c n", n=L)
        ktok3 = ktok_t[:, :].rearrange("p (b n) -> p b n", n=256)
        vtok3 = vtok_t[:, :].rearrange("p (b n) -> p b n", n=512)
        m3 = m_t[:, :].rearrange("p (c n) -> p c n", n=L)
        z3 = z_t[:, :].rearrange("p (c n) -> p c n", n=L + 2)
        sbsave = h_t[:, 0:32 * 256].rearrange("p (n q v) -> p n q v", q=2, v=128)
        srun = srun_t[:, :].rearrange("p (d q v) -> p d q v", q=2, v=128)
        XT_ALL = [("xt", c) for c in range(KC)]
        XN_ALL = [("xn", c) for c in range(KC)]
        SQ_ALL = [("sq", c) for c in range(KC)]
        H_ALL = [("h", k) for k in range(FC)]
        SGT_ALL = [("sgt", i) for i in range(4)]

        def cm(i):
            return cmb_t[:, i * 128:(i + 1) * 128]
        M_CF, M_CB, M_TB, M_TF, MASK_F, MASK_B = (cm(i) for i in range(6))
        CI = cmb_t[:, 768:770]

        tr.op("pool", lambda e: e.memset(ones_t[:, :], 1.0 / D), writes=["ones"])
        tr.op("pool", lambda e: e.memset(ones128_t[:, :], 1.0 / 128), writes=["ones128"])
        tr.op("pool", lambda e: e.memset(g64_t[:, :], 0.0), writes=["g64"])
        tr.op("pool", lambda e: e.memset(g64_t[0:64, 0:64], 1.0 / 64), reads=["g64"], writes=["g64"])
        tr.op("pool", lambda e: e.memset(g64_t[64:128, 64:128], 1.0 / 64), reads=["g64"], writes=["g64"])
        tr.op("pool", lambda e: e.memset(onesrow_t[:, :], 1.0), writes=["onesrow"])
        tr.dma("sp", gains_t[:, :], gains, writes=["gains"])
        cmat_t = xg_t[:, 0:NCM]
        tr.dma("sp", cmat_t, cmat_d, writes=[("xg", 0), ("xg", 1)])
        tr.dma("sp", vecs_t[:, :], vecs_d, writes=["vecs"])
        tr.dma("sp", gwst_t[:, :], gw_d, writes=["gwst"])
        tr.dma("sp", gb_t[:, :], gb_d, writes=["gb"])
        if mode == "B":
            tr.dma("sp", rmask_t[:, :], rmask_d, writes=["rmask"])
        tr.op("dve", lambda e: e.tensor_copy(gw_t[:, :], gwst_t[:, :]), reads=["gwst"], writes=["gw"])
        tr.op("dve", lambda e: e.tensor_copy(cmb_t[:, :], cmat_t), reads=[("xg", 0), ("xg", 1)], writes=["cmat"])


        KFAST = int(os.environ.get('KFAST', '0'))
        def PW(idx, n):
            return (wpan[idx][:, 0:n], n)
        panels = []

        def add_ffn(f):
            pg, pu, pd = (P_G1, P_U1, P_D1) if f == 1 else (P_G2, P_U2, P_D2)
            for j in range(11):
                panels.append(PW(pg + j, 2048))
                panels.append(PW(pu + j, 2048))
            for m in range(8):
                panels.append(PW(pd + m, 2816))

        def add_phase1():
            for ti in range(TPS):
                add_ffn(1)
                for pj, (c0, nc_) in enumerate(IN_PANELS):
                    panels.append(PW(P_IN + pj, 8 * nc_))

        def add_phase3():
            for ti in range(TPS):
                for j in range(4):
                    panels.append(PW(P_OUT + j, 2048))
                add_ffn(2)
        if mode == "A":
            add_phase1()
        else:
            add_phase1()
            add_phase3()
            add_phase3()
        ring_issued = [0]
        pi = [0]

        def ring_advance(n_done):
            lim = min(len(panels), n_done + NSLOT)
            while ring_issued[0] < lim:
                i = ring_issued[0]
                ap, n = panels[i]
                s = i % NSLOT
                tr.dma("sp", ring_t[:, s * SLOT:s * SLOT + n], ap, reads=[], writes=[("ring", s)])
                ring_issued[0] += 1

        def ring_get():
            i = pi[0]
            assert i < ring_issued[0], (i, ring_issued[0])
            s = i % NSLOT
            n = panels[i][1]
            pi[0] += 1
            return ring_t[:, s * SLOT:s * SLOT + n], ("ring", s)

        def act_rsqrt(out_ap, in_ap, rd, wr):
            tr.op("act", lambda e: e.activation(out_ap, in_ap, AF.Ln, bias=EPS), reads=rd, writes=wr)
            tr.op("act", lambda e: e.activation(out_ap, out_ap, AF.Exp, scale=-0.5), reads=wr, writes=wr)

        def rmsnorm_stats():
            tr.op("act", lambda e: e.activation(sq_t[:, :], xt_t[:, :], AF.Square),
                  reads=XT_ALL, writes=SQ_ALL)
            b = ps_next()

            def mm(e, b=b):
                ins = None
                for c in range(KC):
                    ins = e.matmul(ps[b][:, :], ones_t[:, :], sq[:, c, :], start=(c == 0), stop=(c == KC - 1))
                return ins
            tr.op("pe", mm, reads=SQ_ALL + ["ones"], writes=[("ps", b)])
            act_rsqrt(rstd_t[:, :], ps[b][:, :], [("ps", b)], ["rstd"])

        def stats_sq(m):
            tr.op("act", lambda e, m=m: e.activation(sq[:, m, :], xt[:, m, :], AF.Square),
                  reads=[("xt", m)], writes=[("sq", m)])

        def stats_mm(m):
            tr.op("pe", lambda e, m=m: e.matmul(ps[7][:, :], ones_t[:, :], sq[:, m, :], start=(m == 0), stop=(m == KC - 1)),
                  reads=[("sq", m), "ones"], writes=[("ps", 7)])

        def stats_chunk(m):
            stats_sq(m)
            if m > 0:
                stats_mm(m - 1)
            if m == KC - 1:
                stats_mm(m)

        def stats_finish():
            act_rsqrt(rstd_t[:, :], ps[7][:, :], [("ps", 7)], ["rstd"])

        def rmsnorm_apply(gi, out3, out_keys):
            if gi == 3:
                for c in range(KC):
                    tr.op("dve", lambda e, c=c: e.scalar_tensor_tensor(
                        out3[:, c, :], xt[:, c, :], gains_t[:, gi * KC + c:gi * KC + c + 1], rstd_t[:, :],
                        ALU.mult, ALU.mult),
                        reads=[("xt", c), "rstd", "gains"], writes=[out_keys[c]])
                return
            for lo, hi in ((0, 4), (4, 8)):
                tr.op("dve", lambda e, lo=lo, hi=hi: e.tensor_tensor(
                    out3[:, lo:hi, :], xt[:, lo:hi, :], bcast_mid(rstd_t[:, :], hi - lo), ALU.mult),
                    reads=[("xt", c) for c in range(lo, hi)] + ["rstd"], writes=[out_keys[c] for c in range(lo, hi)])

        def mm_gu(e, w3, b, jj):
            ins = None
            for c in range(KC):
                ins = e.matmul(ps[b][:, :], w3[:, c, jj * 128:(jj + 1) * 128], xn[:, c, :],
                               start=(c == 0), stop=(c == KC - 1))
            return ins

        def mm_down(e, w3, b):
            ins = None
            for k in range(FC):
                ins = e.matmul(ps[b][:, :], w3[:, k, :], hh[:, k, :], start=(k == 0), stop=(k == FC - 1))
            return ins

        def sigmoid_chain(o, psb, bkey, okey):
            tr.op("act", lambda e: e.activation(o, psb, AF.Exp, scale=-1.0), reads=[bkey], writes=[okey])
            tr.op("act", lambda e: e.activation(o, o, AF.Ln, bias=1.0), reads=[okey], writes=[okey])
            tr.op("act", lambda e: e.activation(o, o, AF.Exp, scale=-1.0), reads=[okey], writes=[okey])

        def ffn():
            if KFAST:
                for _ in range(30):
                    ring_get()
                    ring_advance(pi[0])
                return
            for j in range(11):
                wg, kg = ring_get()
                wu, ku = ring_get()
                wg3 = wg.rearrange("p (c n) -> p c n", n=256)
                wu3 = wu.rearrange("p (c n) -> p c n", n=256)
                for jj in range(2):
                    fc = 2 * j + jj
                    bg = ps_next()
                    bu = ps_next()
                    tr.op("pe", lambda e, w3=wg3, b=bg, jj=jj: mm_gu(e, w3, b, jj), reads=XN_ALL + [kg], writes=[("ps", bg)])
                    tr.op("pe", lambda e, w3=wu3, b=bu, jj=jj: mm_gu(e, w3, b, jj), reads=XN_ALL + [ku], writes=[("ps", bu)])
                    sgv = sg_t[:, (fc % 2) * T:(fc % 2 + 1) * T]
                    sgw = sg_t[:, (2 + fc % 2) * T:(3 + fc % 2) * T]
                    ka = ("sgt", fc % 2)
                    kb = ("sgt", 2 + fc % 2)
                    sigmoid_chain(sgv, ps[bg][:, :], ("ps", bg), ka)
                    tr.op("dve", lambda e, b=bg, i=sgv, o=sgw: e.tensor_tensor(o, ps[b][:, :], i, ALU.mult),
                          reads=[("ps", bg), ka], writes=[kb])
                    tr.op("dve", lambda e, b=bu, i=sgw, fc=fc: e.tensor_tensor(hh[:, fc, :], ps[b][:, :], i, ALU.mult),
                          reads=[("ps", bu), kb], writes=[("h", fc)])
                ring_advance(pi[0])
            for m in range(8):
                wd, kd = ring_get()
                wd3 = wd.rearrange("p (k n) -> p k n", n=128)
                b = ps_next()
                tr.op("pe", lambda e, w3=wd3, b=b: mm_down(e, w3, b), reads=H_ALL + [kd], writes=[("ps", b)])
                tr.op("dve", lambda e, b=b, m=m: e.scalar_tensor_tensor(
                    xt[:, m, :], ps[b][:, :], 0.5, xt[:, m, :], ALU.mult, ALU.add),
                    reads=[("ps", b), ("xt", m)], writes=[("xt", m)])
                stats_chunk(m)
                ring_advance(pi[0])
            stats_finish()

        def mm_fm(e, w3, c0, M, b):
            ins = None
            for c in range(KC):
                ins = e.matmul(ps[b][0:M, :], w3[:, c, c0:c0 + M], xn[:, c, :], start=(c == 0), stop=(c == KC - 1))
            return ins

        def mm_tm(e, w3, blk, N, b):
            ins = None
            for c in range(KC):
                ins = e.matmul(ps[b][:, 0:N], xn[:, c, blk * 128:(blk + 1) * 128], w3[:, c, 0:N],
                               start=(c == 0), stop=(c == KC - 1))
            return ins

        evac_rr = [0]

        def evac(out_ap, in_ap, rd, wr, scale=None):
            e = "act" if evac_rr[0] % 2 == 0 else "dve"
            evac_rr[0] += 1
            if e == "act":
                if scale is None:
                    tr.op("act", lambda g: g.copy(out_ap, in_ap), reads=rd, writes=wr)
                else:
                    tr.op("act", lambda g: g.mul(out_ap, in_ap, scale), reads=rd, writes=wr)
            else:
                if scale is None:
                    tr.op("dve", lambda g: g.tensor_copy(out_ap, in_ap), reads=rd, writes=wr)
                else:
                    tr.op("dve", lambda g: g.tensor_scalar_mul(out_ap, in_ap, scale), reads=rd, writes=wr)

        def projection(ti):
            tl = ti * T
            w, kw = ring_get()
            w3 = w.rearrange("p (c n) -> p c n", n=256)
            for c2 in range(2):
                b = ps_next()
                tr.op("pe", lambda e, w3=w3, c2=c2, b=b: mm_fm(e, w3, c2 * 128, 128, b), reads=XN_ALL + [kw], writes=[("ps", b)])
                evac(qT3[:, c2, tl:tl + T], ps[b][:, :], [("ps", b)], [("qT", c2, ti)], scale=0.125)
            ring_advance(pi[0])
            w, kw = ring_get()
            w3 = w.rearrange("p (c n) -> p c n", n=256)
            for c2 in range(2):
                b = ps_next()
                tr.op("pe", lambda e, w3=w3, c2=c2, b=b: mm_fm(e, w3, c2 * 128, 128, b), reads=XN_ALL + [kw], writes=[("ps", b)])
                evac(kT3[:, c2, tl:tl + T], ps[b][:, :], [("ps", b)], [("kT", c2, ti)])
            for bl in range(4):
                blk = ti * 4 + bl
                b = ps_next()
                tr.op("pe", lambda e, w3=w3, bl=bl, b=b: mm_tm(e, w3, bl, 256, b), reads=XN_ALL + [kw], writes=[("ps", b)])
                evac(ktok3[:, blk, :], ps[b][:, 0:256], [("ps", b)], [("ktok", blk)])
            ring_advance(pi[0])
            for hv in range(2):
                w, kw = ring_get()
                w3 = w.rearrange("p (c n) -> p c n", n=256)
                for bl in range(4):
                    blk = ti * 4 + bl
                    b = ps_next()
                    tr.op("pe", lambda e, w3=w3, bl=bl, b=b: mm_tm(e, w3, bl, 256, b), reads=XN_ALL + [kw], writes=[("ps", b)])
                    evac(vtok3[:, blk, hv * 256:(hv + 1) * 256], ps[b][:, 0:256], [("ps", b)], [("vtok", blk, hv)])
                ring_advance(pi[0])
            for hg in range(2):
                w, kw = ring_get()
                w3 = w.rearrange("p (c n) -> p c n", n=256)
                for c2 in range(2):
                    c = hg * 2 + c2
                    b = ps_next()
                    tr.op("pe", lambda e, w3=w3, c2=c2, b=b: mm_fm(e, w3, c2 * 128, 128, b), reads=XN_ALL + [kw], writes=[("ps", b)])
                    sgv = sg_t[:, (c % 2) * T:(c % 2 + 1) * T]
                    ka = ("sgt", c % 2)
                    sigmoid_chain(sgv, ps[b][:, :], ("ps", b), ka)
                    tr.op("dve", lambda e, b=b, c=c, i=sgv: e.scalar_tensor_tensor(
                        m3[:, c, tl:tl + T], ps[b][:, :], vecs_t[:, 16 + c:17 + c], i, ALU.mult, ALU.mult),
                        reads=[("ps", b), ka, "vecs"], writes=[("m", c, ti)])
                ring_advance(pi[0])
            w, kw = ring_get()
            w3 = w.rearrange("p (c n) -> p c n", n=64)
            b = ps_next()
            tr.op("pe", lambda e, w3=w3, b=b: mm_fm(e, w3, 0, 64, b), reads=XN_ALL + [kw], writes=[("ps", b)])
            evac(rT_t[:, tl:tl + T], ps[b][0:64, :], [("ps", b)], [("rT", ti)])
            ring_advance(pi[0])
            for hb in range(2):
                w, kw = ring_get()
                w3 = w.rearrange("p (c n) -> p c n", n=256)
                for c2 in range(2):
                    c = hb * 2 + c2
                    b = ps_next()
                    tr.op("pe", lambda e, w3=w3, c2=c2, b=b: mm_fm(e, w3, c2 * 128, 128, b), reads=XN_ALL + [kw], writes=[("ps", b)])
                    evac(m3[:, 4 + c, tl:tl + T], ps[b][:, :], [("ps", b)], [("m", 4 + c, ti)])
                ring_advance(pi[0])
            cb = []
            for hc in range(2):
                w, kw = ring_get()
                w3 = w.rearrange("p (c n) -> p c n", n=256)
                for c2 in range(2):
                    b = ps_next()
                    cb.append(b)
                    tr.op("pe", lambda e, w3=w3, c2=c2, b=b: mm_fm(e, w3, c2 * 128, 128, b), reads=XN_ALL + [kw], writes=[("ps", b)])
                ring_advance(pi[0])
            cst = [cv_t[:, 0:T], cv_t[:, T:2 * T], ctmp_t[:, :], tt_t[:, :]]
            ckey = [("cv", 0), ("cv", 1), "ctmp", "tt"]
            for c in range(4):
                evac(cst[c], ps[cb[c]][:, :], [("ps", cb[c])], [ckey[c]])
            for hc in range(2):
                w, kw = ring_get()
                w3 = w.rearrange("p (c n) -> p c n", n=256)
                for c2 in range(2):
                    c = hc * 2 + c2
                    b = ps_next()
                    tr.op("pe", lambda e, w3=w3, c2=c2, b=b: mm_fm(e, w3, c2 * 128, 128, b), reads=XN_ALL + [kw], writes=[("ps", b)])
                    tr.op("dve", lambda e, b=b, c=c: e.tensor_tensor(
                        z3[:, c, 1 + tl:1 + tl + T], ps[b][:, :], cst[c], ALU.mult),
                        reads=[("ps", b), ckey[c]], writes=[("z", c, ti)])
                ring_advance(pi[0])

        def f32tmp(i):
            return sg_t[:, i * 256:(i + 1) * 256]
        SP = [f32tmp(0), f32tmp(1)]
        ET = f32tmp(2)
        ETD = [f32tmp(2), f32tmp(7)]
        SPH = [sph_t[:, 0:256], sph_t[:, 256:512]]
        SPL = [spl_t[:, 0:256], spl_t[:, 256:512]]
        EQ = [f32tmp(3), f32tmp(4)]
        EI = [f32tmp(5), f32tmp(6)]

        def b16tmp(i):
            return xn_t[:, i * 256:(i + 1) * 256]
        KTB = [[b16tmp(0), b16tmp(1)], [b16tmp(2), b16tmp(3)]]
        QDB = [[xn_t[:, 1024 + par * 512 + d * 256:1024 + par * 512 + (d + 1) * 256] for par in range(2)] for d in range(2)]
        KIB = [[xn_t[:, 2048 + par * 512 + d * 256:2048 + par * 512 + (d + 1) * 256] for par in range(2)] for d in range(2)]
        PM = [xn_t[:, 3072 + i * 128:3072 + (i + 1) * 128] for i in range(4)]
        SFB = [xn_t[:, 3584 + i * 256:3584 + (i + 1) * 256].rearrange("p (q v) -> p q v", v=128) for i in range(2)]
        DECB = [[0, 8], [4, 12]]

        def fence(keys):
            tr.barrier()

        def gate_sp(d, blk):
            b = ps_next()
            r0 = 0 if d == 0 else 32

            tr.op("pe", lambda e, b=b: e.matmul(ps[b][:, 0:256], rT_t[r0:r0 + 16, blk * 128:(blk + 1) * 128],
                                                gw_t[r0:r0 + 16, d * 256:(d + 1) * 256], start=True, stop=True),
                  reads=[("rT", blk // 4), "gw"], writes=[("ps", b)])
            tr.op("dve", lambda e, b=b: e.tensor_tensor(SP[d], ps[b][:, 0:256], gb_t[:, d * 256:(d + 1) * 256], ALU.add),
                  reads=[("ps", b), "gb"], writes=[("sp", d)])
            tr.op("act", lambda e: e.activation(SP[d], SP[d], AF.Exp, scale=-1.0), reads=[("sp", d)], writes=[("sp", d)])
            tr.op("act", lambda e: e.activation(SP[d], SP[d], AF.Ln, bias=1.0), reads=[("sp", d)], writes=[("sp", d)])
            tr.op("act", lambda e: e.copy(SPH[d], SP[d]), reads=[("sp", d)], writes=[("sph", d)])
            tr.op("dve", lambda e: e.tensor_tensor(SPL[d], SP[d], SPH[d], ALU.subtract),
                  reads=[("sp", d), ("sph", d)], writes=[("spl", d)])

        def gate_tail(d, blk, par):
            b = ps_next()
            Mt = M_TF if d == 0 else M_TB

            def mm(e, b=b):
                e.matmul(ps[b][:, 0:256], Mt, SPH[d], start=True, stop=False)
                return e.matmul(ps[b][:, 0:256], Mt, SPL[d], start=False, stop=True)
            tr.op("pe", mm, reads=[("sph", d), ("spl", d), "cmat"], writes=[("ps", b)])
            tr.op("act", lambda e, b=b: e.activation(ETD[d], ps[b][:, 0:256], AF.Exp), reads=[("ps", b)], writes=[("et", d)])
            tr.op("pool", lambda e: e.tensor_tensor(KTB[d][par], ktok3[:, blk, :], ETD[d], ALU.mult),
                  reads=[("et", d), ("ktok", blk)], writes=[("kt", d, par)])

        def gate_dec_ci(d, blk, par):
            b = ps_next()
            base = DECB[d][par]

            def mm(e, b=b):
                ins = None
                for q in range(2):
                    e.matmul(ps[b][:, q * 2:q * 2 + 2], SPH[d][:, q * 128:(q + 1) * 128], CI, start=True, stop=False)
                    ins = e.matmul(ps[b][:, q * 2:q * 2 + 2], SPL[d][:, q * 128:(q + 1) * 128], CI, start=False, stop=True)
                return ins
            tr.op("pe", mm, reads=[("sph", d), ("spl", d), "cmat"], writes=[("ps", b)])
            tr.op("act", lambda e, b=b: e.activation(dec_t[:, base:base + 4], ps[b][:, 0:4], AF.Exp),
                  reads=[("ps", b)], writes=[("dec", d, par)])

        def state_update(d, blk, ch, par):
            for q in range(2):
                b = ps_next()
                r0 = ch * 64
                tr.op("pe", lambda e, b=b, q=q, r0=r0: e.matmul(
                    ps[b][:, 0:256], KTB[d][par][r0:r0 + 64, q * 128:(q + 1) * 128],
                    vtok3[r0:r0 + 64, blk, q * 256:(q + 1) * 256], start=True, stop=True),
                    reads=[("kt", d, par), ("vtok", blk, q)], writes=[("ps", b)])
                for hh_ in range(2):
                    p0 = hh_ * 64
                    col = DECB[d][par] + q * 2 + ch
                    tr.op("dve", lambda e, b=b, q=q, p0=p0, col=col, hh_=hh_: e.scalar_tensor_tensor(
                        srun[p0:p0 + 64, d, q, :], srun[p0:p0 + 64, d, q, :], dec_t[p0:p0 + 64, col:col + 1],
                        ps[b][p0:p0 + 64, hh_ * 128:(hh_ + 1) * 128], ALU.mult, ALU.add),
                        reads=[("ps", b), ("dec", d, par), ("srun", d, q)], writes=[("srun", d, q)])

        SRUN_ALL = [("srun", d, q) for d in range(2) for q in range(2)]

        def local_scan_and_exchange(do_scan, do_fold):
            tr.barrier()
            tr.op("pool", lambda e: e.memset(srun_t[:, :], 0.0), writes=SRUN_ALL)
            tr.op("pool", lambda e: e.memset(dacc_t[:, :], 1.0), writes=[("dacc", 0), ("dacc", 1)])
            tr.op("pool", lambda e: e.memset(hal_t[:, :], 0.0), writes=["hal"])
            if do_scan:
                local_scan()
            if do_fold:
                fold_exchange()

        def local_scan():
            orders = {1: list(range(15, -1, -1)), 0: list(range(16))}

            def G(d, blk):
                gate_sp(d, blk)
                gate_tail(d, blk, blk % 2)
                gate_dec_ci(d, blk, blk % 2)

            def H(d, blk):
                par = blk % 2
                for ch in ((1, 0) if d == 1 else (0, 1)):
                    state_update(d, blk, ch, par)
                base = DECB[d][par]
                dv = dec_t[:, base:base + 4].rearrange("p (q c) -> p q c", c=2)
                for ch in range(2):
                    tr.op("dve", lambda e, d=d, dv=dv, ch=ch: e.tensor_tensor(
                        dacc_t[:, d * 2:d * 2 + 2], dacc_t[:, d * 2:d * 2 + 2], dv[:, :, ch], ALU.mult),
                        reads=[("dec", d, par), ("dacc", d)], writes=[("dacc", d)])
            G(1, orders[1][0])
            G(0, orders[0][0])
            for i in range(16):
                streams = []
                for d in (1, 0):
                    if i + 1 < 16:
                        streams.append(staged("G%d" % d, G, d, orders[d][i + 1]))
                    streams.append(staged("H%d" % d, H, d, orders[d][i]))
                tr.emit_merged(*streams)
            tr.op("dve", lambda e: e.tensor_copy(xst_t[:, 0:512], srun_t[:, :]), reads=SRUN_ALL, writes=["xst"])
            tr.op("dve", lambda e: e.tensor_copy(xst_t[:, 512:516], dacc_t[:, :]), reads=[("dacc", 0), ("dacc", 1), "xst"], writes=["xst"])
            tr.op("dve", lambda e: e.tensor_copy(xst_t[:, 516:520].rearrange("p (c o) -> p c o", o=1), z3[:, :, 1:2]),
                  reads=[("z", c, 0) for c in range(4)] + ["xst"], writes=["xst"])
            tr.op("dve", lambda e: e.tensor_copy(xst_t[:, 520:524].rearrange("p (c o) -> p c o", o=1), z3[:, :, L:L + 1]),
                  reads=[("z", c, TPS - 1) for c in range(4)] + ["xst"], writes=["xst"])
            tr.op("pool", lambda e: e.memset(xst_t[:, 524:528], 0.0), reads=["xst"], writes=["xst"])
            tr.dma("sp", xout, xst_t[:, :], reads=["xst"], writes=["xout"], is_output=True)

        def fold_exchange():
            step = 0
            for d in (0, 1):
                for cp in (range(NCORES) if d == 0 else range(NCORES - 1, -1, -1)):
                    sl = step % 2
                    step += 1
                    xg = xg_t[:, sl * 528:(sl + 1) * 528]
                    kx = ("xg", sl)
                    tr.dma("sp", xg, xall[cp * 128:(cp + 1) * 128, :], reads=[], writes=[kx])
                    w = rmask_t[:, d * 8 + cp:d * 8 + cp + 1]
                    for q in range(2):
                        dcol = xg[:, 512 + d * 2 + q:512 + d * 2 + q + 1]
                        dp = dp_t[:, q:q + 1]
                        tr.op("dve", lambda e, dp=dp, dcol=dcol, w=w: e.tensor_scalar(dp, dcol, -1.0, w, ALU.add, ALU.mult),
                              reads=[kx, "rmask"], writes=[("dp", q)])
                        tr.op("dve", lambda e, dp=dp: e.tensor_scalar_add(dp, dp, 1.0), reads=[("dp", q)], writes=[("dp", q)])
                        S = srun[:, d, q, :]
                        tr.op("dve", lambda e, S=S, dp=dp: e.tensor_scalar_mul(S, S, dp),
                              reads=[("dp", q), ("srun", d, q)], writes=[("srun", d, q)])
                        Lp = xg[:, d * 256 + q * 128:d * 256 + (q + 1) * 128]
                        tr.op("dve", lambda e, S=S, Lp=Lp, w=w: e.scalar_tensor_tensor(S, Lp, w, S, ALU.mult, ALU.add),
                              reads=[kx, "rmask", ("srun", d, q)], writes=[("srun", d, q)])
                    if d == 0:
                        tr.op("dve", lambda e, xg=xg, cp=cp: e.scalar_tensor_tensor(
                            hal_t[:, 0:4], xg[:, 520:524], rmask_t[:, 16 + cp:17 + cp], hal_t[:, 0:4], ALU.mult, ALU.add),
                            reads=[kx, "rmask", "hal"], writes=["hal"])
                        tr.op("dve", lambda e, xg=xg, cp=cp: e.scalar_tensor_tensor(
                            hal_t[:, 4:8], xg[:, 516:520], rmask_t[:, 24 + cp:25 + cp], hal_t[:, 4:8], ALU.mult, ALU.add),
                            reads=[kx, "rmask", "hal"], writes=["hal"])
            tr.op("dve", lambda e: e.tensor_copy(z3[:, :, 0:1], hal_t[:, 0:4].rearrange("p (c o) -> p c o", o=1)),
                  reads=["hal"], writes=[("zh", 0)])
            tr.op("dve", lambda e: e.tensor_copy(z3[:, :, L + 1:L + 2], hal_t[:, 4:8].rearrange("p (c o) -> p c o", o=1)),
                  reads=["hal"], writes=[("zh", 1)])

        def phase2(seg, zero_init=True):
            fence(H_ALL + SGT_ALL + XN_ALL)
            if zero_init:
                tr.op("pool", lambda e: e.memset(srun_t[:, :], 0.0), writes=SRUN_ALL)
            def GA(blk):
                gate_sp(1, blk)
                gate_tail(1, blk, blk % 2)
                gate_dec_ci(1, blk, blk % 2)

            def HA(blk):
                for ch in (1, 0):
                    n = blk * 2 + ch
                    tr.op("act", lambda e, n=n: e.copy(sbsave[:, n, :, :], srun[:, 1, :, :]),
                          reads=[("srun", 1, 0), ("srun", 1, 1)], writes=[("sbs", n)])
                    state_update(1, blk, ch, blk % 2)
            GA(15)
            for blk in range(15, -1, -1):
                g = staged("G", GA, blk - 1) if blk > 0 else []
                h = staged("H", HA, blk)
                tr.emit_merged(g, h)

            def GB(blk):
                par = blk % 2
                ti = blk // 4
                tok = slice(blk * 128, (blk + 1) * 128)
                SP2, SPH2, SPL2 = sg_t[:, 0:512], sph_t[:, 0:512], spl_t[:, 0:512]
                EQ2, EI2 = sg_t[:, 768:1280], sg_t[:, 1280:1792]
                SPK = [("sp", 0), ("sp", 1)]
                SPHK = [("sph", 0), ("sph", 1)]
                SPLK = [("spl", 0), ("spl", 1)]
                b = ps_next()

                tr.op("pe", lambda e, b=b: e.matmul(ps[b][:, :], rT_t[0:64, tok], gw_t[0:64, :], start=True, stop=True),
                      reads=[("rT", ti), "gw"], writes=[("ps", b)])
                tr.op("dve", lambda e, b=b: e.tensor_tensor(SP2, ps[b][:, :], gb_t[:, :], ALU.add),
                      reads=[("ps", b), "gb"], writes=SPK)
                tr.op("act", lambda e: e.activation(SP2, SP2, AF.Exp, scale=-1.0), reads=SPK, writes=SPK)
                tr.op("act", lambda e: e.activation(SP2, SP2, AF.Ln, bias=1.0), reads=SPK, writes=SPK)
                tr.op("act", lambda e: e.copy(SPH2, SP2), reads=SPK, writes=SPHK)
                tr.op("pool", lambda e: e.tensor_tensor(SPL2, SP2, SPH2, ALU.subtract), reads=SPK + SPHK, writes=SPLK)
                gate_tail(0, blk, par)
                b2 = ps_next()

                def mmc(e, b2=b2):
                    ins = None
                    for d in range(2):
                        Mc = M_CF if d == 0 else M_CB
                        for q in range(2):
                            o = ps[b2][:, d * 256 + q * 128:d * 256 + (q + 1) * 128]
                            e.matmul(o, SPH[d][:, q * 128:(q + 1) * 128], Mc, start=True, stop=False)
                            ins = e.matmul(o, SPL[d][:, q * 128:(q + 1) * 128], Mc, start=False, stop=True)
                    return ins
                tr.op("pe", mmc, reads=SPHK + SPLK + ["cmat"], writes=[("ps", b2)])
                tr.op("act", lambda e, b2=b2: e.activation(EQ2, ps[b2][:, :], AF.Exp),
                      reads=[("ps", b2)], writes=[("eq", 0), ("eq", 1)])
                tr.op("act", lambda e, b2=b2: e.activation(EI2, ps[b2][:, :], AF.Exp, scale=-1.0),
                      reads=[("ps", b2)], writes=[("ei", 0), ("ei", 1)])
                QD2 = xn_t[:, 1024 + par * 512:1024 + (par + 1) * 512].rearrange("p (d q v) -> p d q v", d=2, v=128)
                KI2 = xn_t[:, 2048 + par * 512:2048 + (par + 1) * 512].rearrange("p (d q v) -> p d q v", d=2, v=128)
                tr.op("dve", lambda e: e.tensor_tensor(
                    QD2, bcast_mid(qT3[:, :, tok], 2), EQ2.rearrange("p (d q v) -> p d q v", d=2, v=128), ALU.mult),
                    reads=[("eq", 0), ("eq", 1), ("qT", 0, ti), ("qT", 1, ti)], writes=[("qd", 0, par), ("qd", 1, par)])
                tr.op("pool", lambda e: e.tensor_tensor(
                    KI2, bcast_mid(kT3[:, :, tok], 2), EI2.rearrange("p (d q v) -> p d q v", d=2, v=128), ALU.mult),
                    reads=[("ei", 0), ("ei", 1), ("kT", 0, ti), ("kT", 1, ti)], writes=[("ki", 0, par), ("ki", 1, par)])
                base = DECB[0][par]
                tr.op("dve", lambda e, base=base: e.tensor_copy(
                    dec_t[:, base:base + 4], EQ[0].rearrange("p (j s) -> p j s", s=64)[:, :, 63]),
                    reads=[("eq", 0)], writes=[("dec", 0, par)])

            def HB(blk):
                par = blk % 2
                ti = blk // 4
                tok = slice(blk * 128, (blk + 1) * 128)
                ob = 6 + blk % 2
                QD = [QDB[0][par], QDB[1][par]]
                KI = [KIB[0][par], KIB[1][par]]
                for ch in range(2):
                    tr.op("act", lambda e, ch=ch: e.copy(SFB[ch], srun[:, 0, :, :]),
                          reads=[("srun", 0, 0), ("srun", 0, 1)], writes=[("sfb", ch)])
                    state_update(0, blk, ch, par)
                PMD = [sq_t[:, 1024:1536], sq_t[:, 1536:2048]]
                for hh_ in range(2):
                    p0 = hh_ * 64
                    for d in range(2):
                        b = ps_next()

                        def mms(e, b=b, d=d, p0=p0):
                            ins = None
                            for q in range(2):
                                ins = e.matmul(ps[b][:, q * 128:(q + 1) * 128], KI[d][p0:p0 + 64, q * 128:(q + 1) * 128],
                                               QD[d][p0:p0 + 64, q * 128:(q + 1) * 128], start=True, stop=True)
                            return ins
                        tr.op("pe", mms, reads=[("ki", d, par), ("qd", d, par)], writes=[("ps", b)])
                        mk = MASK_F if d == 0 else MASK_B
                        tr.op("dve", lambda e, b=b, d=d, mk=mk, hh_=hh_: e.tensor_tensor(
                            PMD[d].rearrange("p (q h v) -> p q h v", h=2, v=128)[:, :, hh_, :],
                            ps[b][:, 0:256].rearrange("p (q v) -> p q v", v=128),
                            bcast_mid(mk, 2), ALU.mult),
                            reads=[("ps", b), "cmat", ("sq", 2 + d)], writes=[("sq", 2 + d)])
                for h in range(4):
                    q, hh_ = h // 2, h % 2
                    p0 = hh_ * 64

                    def mo(e, h=h, q=q, p0=p0, ob=ob, blk=blk):
                        o = ps[ob][:, h * 128:(h + 1) * 128]
                        vh = vtok3[:, blk, h * 128:(h + 1) * 128]
                        e.matmul(o, vh, PMD[0][:, h * 128:(h + 1) * 128], start=True, stop=False)
                        e.matmul(o, vh, PMD[1][:, h * 128:(h + 1) * 128], start=False, stop=False)
                        ins = None
                        for ch in range(2):
                            oc = ps[ob][:, h * 128 + ch * 64:h * 128 + ch * 64 + 64]
                            e.matmul(oc, SFB[ch][p0:p0 + 64, q, :], QD[0][p0:p0 + 64, q * 128 + ch * 64:q * 128 + ch * 64 + 64],
                                     start=False, stop=False)
                            ins = e.matmul(oc, sbsave[p0:p0 + 64, blk * 2 + ch, q, :],
                                           QD[1][p0:p0 + 64, q * 128 + ch * 64:q * 128 + ch * 64 + 64],
                                           start=False, stop=(ch == 1))
                        return ins
                    tr.op("pe", mo, reads=[("sq", 2), ("sq", 3), ("vtok", blk, h // 2),
                                           ("sfb", 0), ("sfb", 1), ("sbs", blk * 2), ("sbs", blk * 2 + 1),
                                           ("qd", 0, par), ("qd", 1, par)],
                          writes=[("ps", ob)] if h == 0 else [("pso", ob, h)])
                OKEYS = [("ps", ob)] + [("pso", ob, h) for h in range(1, 4)]
                tr.op("act", lambda e, ob=ob: e.activation(sq_t[:, 0:512], ps[ob][:, :], AF.Square),
                      reads=OKEYS, writes=[("sq", 0)])
                b = ps_next()
                tr.op("pe", lambda e, b=b: e.matmul(ps[b][:, :], ones128_t[:, :], sq_t[:, 0:512], start=True, stop=True),
                      reads=[("sq", 0), "ones128"], writes=[("ps", b)])
                act_rsqrt(rs_t[:, :], ps[b][:, :], [("ps", b)], ["rs"])
                tr.op("dve", lambda e, ob=ob: e.tensor_tensor(tt_t[:, :], ps[ob][:, :], rs_t[:, :], ALU.mult),
                      reads=OKEYS + ["rs"], writes=["tt"])
                tr.op("pool", lambda e, tok=tok: e.tensor_tensor(
                    m3[:, 0:4, tok], tt_t[:, :].rearrange("p (h v) -> p h v", v=128), m3[:, 0:4, tok], ALU.mult),
                    reads=["tt"] + [("m", c, ti) for c in range(4)], writes=[("m", c, ti) for c in range(4)])
            def conv_item(ti, c):
                tl = ti * T
                a = cv_t[:, (c % 2) * T:(c % 2 + 1) * T]
                ka = ("cv", c % 2)
                zr = [("z", c, ti)]
                zr.append(("z", c, ti - 1) if ti > 0 else ("zh", 0))
                zr.append(("z", c, ti + 1) if ti < TPS - 1 else ("zh", 1))
                tr.op("dve", lambda e: e.tensor_scalar_mul(a, z3[:, c, tl:tl + T], vecs_t[:, c * 3:c * 3 + 1]),
                      reads=zr + ["vecs"], writes=[ka])
                for i in (1, 2):
                    tr.op("dve", lambda e, i=i: e.scalar_tensor_tensor(
                        a, z3[:, c, tl + i:tl + i + T], vecs_t[:, c * 3 + i:c * 3 + i + 1], a, ALU.mult, ALU.add),
                        reads=zr + ["vecs", ka], writes=[ka])
                tr.op("pool", lambda e: e.tensor_tensor(a, a, m3[:, 4 + c, tl:tl + T], ALU.mult),
                      reads=[ka, ("m", 4 + c, ti)], writes=[ka])
                tr.op("act", lambda e: e.activation(sq_t[:, 512:1024], a, AF.Square), reads=[ka], writes=[("sq", 1)])
                b = ps_next()
                tr.op("pe", lambda e: e.matmul(ps[b][:, :], g64_t[:, :], sq_t[:, 512:1024], start=True, stop=True),
                      reads=[("sq", 1), "g64"], writes=[("ps", b)])
                act_rsqrt(rstd_t[:, :], ps[b][:, :], [("ps", b)], ["rstd"])
                tr.op("dve", lambda e: e.scalar_tensor_tensor(
                    m3[:, 4 + c, tl:tl + T], a, vecs_t[:, 12 + c:13 + c], rstd_t[:, :], ALU.mult, ALU.mult),
                    reads=[ka, "rstd", "vecs"], writes=[("m", 4 + c, ti)])

            GB(0)
            for blk in range(16):
                g = staged("G", GB, blk + 1) if blk + 1 < 16 else []
                h = staged("H3", HB, blk)
                cst = staged("C", conv_item, blk // 4, blk % 4)
                tr.emit_merged(g, h, cst)
            fence(H_ALL + SGT_ALL + XN_ALL + [("sbs", n) for n in range(32)])

        def wout(ti):
            tl = ti * T
            for j in range(4):
                w, kw = ring_get()
                w3 = w.rearrange("p (c n) -> p c n", n=256)
                for c2 in range(2):
                    mo_ = j * 2 + c2
                    b = ps_next()

                    def mm(e, w3=w3, c2=c2, b=b):
                        ins = None
                        for k in range(KC):
                            ins = e.matmul(ps[b][:, :], w3[:, k, c2 * 128:(c2 + 1) * 128], m3[:, k, tl:tl + T],
                                           start=(k == 0), stop=(k == KC - 1))
                        return ins
                    tr.op("pe", mm, reads=[("m", k, ti) for k in range(KC)] + [kw], writes=[("ps", b)])
                    tr.op("dve", lambda e, b=b, mo_=mo_: e.tensor_tensor(xt[:, mo_, :], xt[:, mo_, :], ps[b][:, :], ALU.add),
                          reads=[("ps", b), ("xt", mo_)], writes=[("xt", mo_)])
                    stats_sq(mo_)
                ring_advance(pi[0])
            for mo_ in range(KC):
                stats_mm(mo_)
            stats_finish()

        SEG_KEYS = {
            "sQ": [("qT", c2, ti) for c2 in range(2) for ti in range(TPS)],
            "sK": [("kT", c2, ti) for c2 in range(2) for ti in range(TPS)],
            "sKtok": [("ktok", blk) for blk in range(16)],
            "sVtok": [("vtok", blk, hv) for blk in range(16) for hv in range(2)],
            "sR": [("rT", ti) for ti in range(TPS)],
            "sM": [("m", c, ti) for c in range(8) for ti in range(TPS)],
            "sZ": [("z", c, ti) for c in range(4) for ti in range(TPS)] + [("zh", 0), ("zh", 1)],
        }
        SEG_SB = {"sQ": qT_t, "sK": kT_t, "sKtok": ktok_t, "sVtok": vtok_t, "sR": rT_t, "sM": m_t, "sZ": z_t}
        ring_advance(0)

        def phase1(x1_dst):
            for ti in range(TPS):
                t0 = ti * T
                tr.dma("pool", xt, xT[:, :, t0:t0 + T], reads=[], writes=XT_ALL)
                rmsnorm_stats()
                rmsnorm_apply(0, xn, XN_ALL)
                ffn()
                tr.dma("pool", x1_dst[:, :, t0:t0 + T], xt, reads=XT_ALL, writes=[("x1s", t0)], is_output=(mode == "A"))
                rmsnorm_apply(1, xn, XN_ALL)
                projection(ti)

        def phase3(x1_src, x1_keyed, yoff, prefetch=None):
            for ti in range(TPS):
                t0 = ti * T
                tr.dma("pool", xt, x1_src[:, :, t0:t0 + T], reads=([("x1s", t0)] if x1_keyed else []), writes=XT_ALL)
                if prefetch is not None:
                    for nm in prefetch.get(ti, []):
                        tr.dma("act", SEG_SB[nm][:, :], segd[nm], reads=[], writes=SEG_KEYS[nm])
                wout(ti)
                rmsnorm_apply(2, xn, XN_ALL)
                ffn()
                if prefetch is not None:
                    sm3 = segd["sM"].rearrange("p (c n) -> p c n", n=L)
                    tr.dma("act", m3[:, :, t0:t0 + T], sm3[:, :, t0:t0 + T], reads=[],
                           writes=[("m", c, ti) for c in range(8)])
                yst = h_t[:, 0:2 * KC * T].bitcast(F32).rearrange("p (c n) -> p c n", n=T)
                YK = [("h", 2 * c) for c in range(KC)] + [("h", 2 * c + 1) for c in range(KC)]
                rmsnorm_apply(3, yst, [("h", 2 * c) for c in range(KC)])
                tr.dma("sp", yT[:, :, yoff + t0:yoff + t0 + T], yst, reads=YK, writes=[("yT", yoff + t0)] + [("h", 2 * c + 1) for c in range(KC)],
                       is_output=True)

        if mode == "A":
            phase1(x1p)
            local_scan_and_exchange(True, False)
            for nm, pp, n in SEGBUFS:
                tr.dma("pool", segd[nm], SEG_SB[nm][:, :], reads=SEG_KEYS[nm], writes=[("segd", nm)], is_output=True)
        else:
            tr.op("pool", lambda e: e.memset(z3[:, :, 0:1], 0.0), writes=[("zh", 0)])
            tr.op("pool", lambda e: e.memset(z3[:, :, L + 1:L + 2], 0.0), writes=[("zh", 1)])
            phase1(x1s)
            phase2(0)
            phase3(x1s, True, 0, prefetch={0: ["sQ", "sK", "sR"], 1: ["sKtok", "sVtok"], 2: ["sZ"]})
            local_scan_and_exchange(False, True)
            phase2(1, zero_init=False)
            phase3(x1p, False, L)
        assert pi[0] == len(panels), (pi[0], len(panels))
        print('TCNT', tr.tcnt, max(tr.dcnt))
        tr.finish()
        tr.emit()
    return nc


def _cmat():
    a = np.arange(128)[:, None]
    b = np.arange(128)[None, :]
    same = (a // 64) == (b // 64)
    f32 = np.float32
    u1 = (same & (a <= b)).astype(f32)
    l1 = (same & (a >= b)).astype(f32)
    su = (same & (a < b)).astype(f32)
    sl = (same & (a > b)).astype(f32)
    ci = ((np.arange(128)[:, None] // 64) == np.arange(2)[None, :]).astype(f32)
    s = f32(-1.0 / 16.0)
    return np.ascontiguousarray(np.concatenate([s * u1, s * l1, s * su, s * sl, u1, l1, s * ci], axis=1))


def _panelize(inputs):
    f32 = np.float32
    out = np.zeros((NPAN, 128, SLOT), f32)

    def kmajor(w):
        K, n = w.shape
        return w.reshape(K // 128, 128, n).transpose(1, 0, 2).reshape(128, (K // 128) * n)
    for f, (pg, pu, pd) in ((1, (P_G1, P_U1, P_D1)), (2, (P_G2, P_U2, P_D2))):
        wg = np.asarray(inputs["ffn%d_w_gate" % f], f32)[0]
        wu = np.asarray(inputs["ffn%d_w_up" % f], f32)[0]
        wd = np.asarray(inputs["ffn%d_w_down" % f], f32)[0]
        for j in range(11):
            out[pg + j, :, :2048] = kmajor(wg[:, 256 * j:256 * j + 256])
            out[pu + j, :, :2048] = kmajor(wu[:, 256 * j:256 * j + 256])
        for m in range(8):
            out[pd + m, :, :2816] = kmajor(wd[:, 128 * m:128 * m + 128])
    w_in = np.asarray(inputs["w_in"], f32)[0]
    w_in_p = np.zeros((D, N_IN_PAD), f32)
    w_in_p[:, 0:1536] = w_in[:, 0:1536]
    w_in_p[:, 1536:1552] = w_in[:, 1536:1552]
    w_in_p[:, 1568:1584] = w_in[:, 1552:1568]
    w_in_p[:, 1600:] = w_in[:, 1568:]
    for pj, (c0, n) in enumerate(IN_PANELS):
        out[P_IN + pj, :, :8 * n] = kmajor(w_in_p[:, c0:c0 + n])
    wo = np.asarray(inputs["w_out"], f32)[0]
    for j in range(4):
        out[P_OUT + j, :, :2048] = kmajor(wo[:, 256 * j:256 * j + 256])
    scale = np.ones((NPAN, 128, SLOT), f32)

    def rowgain(g, n):
        gv = np.asarray(g, f32).reshape(KC, 128).T
        return np.repeat(gv[:, :, None], n, axis=2).reshape(128, KC * n)
    for j in range(11):
        scale[P_G1 + j, :, :2048] = rowgain(inputs["ffn1_norm"][0], 256)
        scale[P_U1 + j, :, :2048] = rowgain(inputs["ffn1_norm"][0], 256)
        scale[P_G2 + j, :, :2048] = rowgain(inputs["ffn2_norm"][0], 256)
        scale[P_U2 + j, :, :2048] = rowgain(inputs["ffn2_norm"][0], 256)
    for pj, (c0, n) in enumerate(IN_PANELS):
        scale[P_IN + pj, :, :8 * n] = rowgain(inputs["mix_norm"][0], n)
    return out, scale


def _small_inputs(inputs):
    f32 = np.float32

    def vec(v):
        return np.ascontiguousarray(np.asarray(v, f32).reshape(KC, 128).T)
    gains = np.concatenate([vec(inputs["ffn1_norm"][0]), vec(inputs["mix_norm"][0]),
                            vec(inputs["ffn2_norm"][0]), vec(inputs["final_norm"])], axis=1)
    gw = np.zeros((64, 512), f32)
    gw[0:16, 0:256] = np.asarray(inputs["gate_fwd_w"], f32)[0]
    gw[32:48, 256:512] = np.asarray(inputs["gate_bwd_w"], f32)[0]
    gb = np.concatenate([np.asarray(inputs["gate_fwd_b"], f32)[0], np.asarray(inputs["gate_bwd_b"], f32)[0]])[None, :]
    gb = np.ascontiguousarray(np.broadcast_to(gb, (128, 512)))
    cw = np.asarray(inputs["conv_w"], f32)[0]
    vecs = np.zeros((128, 20), f32)
    for c in range(4):
        for i in range(3):
            vecs[:, c * 3 + i] = cw[i, c * 128:(c + 1) * 128]
        vecs[:, 12 + c] = np.asarray(inputs["conv_group_norm"], f32)[0, c * 128:(c + 1) * 128]
        vecs[:, 16 + c] = np.asarray(inputs["gla_head_norm"], f32)[0, c * 128:(c + 1) * 128]
    return {"gains": np.ascontiguousarray(gains), "gw": gw, "gb": gb, "vecs": vecs, "cmat": _cmat()}


def _tmajor(x):
    return np.ascontiguousarray(x.reshape(x.shape[0], KC, 128).transpose(2, 1, 0))


_NC_CACHE = {}


def kernel(**inputs):
    f32 = np.float32
    if "W" not in _NC_CACHE:
        _NC_CACHE["W"] = build_w()
        _NC_CACHE["A"] = build_nc(mode="A")
        _NC_CACHE["B"] = build_nc(mode="B")
    cores = list(range(NCORES))
    pan32, scale32 = _panelize(inputs)
    resW = run_bass_kernel_spmd(_NC_CACHE["W"], [{"wsrc": np.ascontiguousarray(pan32[i * NPC:(i + 1) * NPC]),
                                                  "wscale": np.ascontiguousarray(scale32[i * NPC:(i + 1) * NPC])}
                                                 for i in cores], core_ids=cores)
    wpan = np.ascontiguousarray(np.concatenate([np.asarray(resW.results[i]["wdst"]) for i in cores], axis=0))
    small = _small_inputs(inputs)
    xp = np.asarray(inputs["x_prompt"], f32)[0]
    xs = np.asarray(inputs["x_sample"], f32)
    mapsA = []
    for i in cores:
        m = dict(small)
        m["wpan"] = wpan
        m["xT"] = _tmajor(xp[i * L:(i + 1) * L])
        mapsA.append(m)
    resA = run_bass_kernel_spmd(_NC_CACHE["A"], mapsA, core_ids=cores)
    xall = np.ascontiguousarray(np.concatenate([np.asarray(resA.results[i]["xout"]) for i in cores], axis=0))
    mapsB = []
    for i in cores:
        m = dict(small)
        m["wpan"] = wpan
        m["xT"] = _tmajor(xs[i])
        m["xall"] = xall
        m["x1p"] = np.asarray(resA.results[i]["x1p"])
        for nm, pp, n in SEGBUFS:
            m[nm] = np.asarray(resA.results[i][nm])
        rm = np.zeros((128, 32), f32)
        for cp in range(NCORES):
            rm[:, cp] = 1.0 if cp < i else 0.0
            rm[:, 8 + cp] = 1.0 if cp > i else 0.0
            rm[:, 16 + cp] = 1.0 if cp == i - 1 else 0.0
            rm[:, 24 + cp] = 1.0 if cp == i + 1 else 0.0
        m["rmask"] = rm
        mapsB.append(m)
    res = run_bass_kernel_spmd(_NC_CACHE["B"], mapsB, core_ids=cores)
    yp = np.empty((1, NCORES * L, D), np.float32)
    ys = np.empty((NCORES, L, D), np.float32)
    for i in range(NCORES):
        y = np.asarray(res.results[i]["yT"])
        yc = y.transpose(2, 1, 0).reshape(TOK, D)
        ys[i] = yc[:L]
        yp[0, i * L:(i + 1) * L] = yc[L:]
    return (yp, ys)
```

```python
import os
import numpy as np
from contextlib import ExitStack
import concourse.bass as bass
import concourse.mybir as mybir
from concourse.bass_utils import run_bass_kernel_spmd

F32 = mybir.dt.float32
BF16 = mybir.dt.bfloat16
ALU = mybir.AluOpType
AF = mybir.ActivationFunctionType

NCORES = 8
D = 1024
DFF = 2816
KC = 8
FC = 22
TOK = 4096
T = 512
L = 2048
NSEG = 2
TPS = L // T
EPS = 1e-6
NSLOT = 4
SLOT = 2816
NDS = 40

N_IN_PAD = 3136
IN_PANELS = [(0, 256), (256, 256), (512, 256), (768, 256), (1024, 256), (1280, 256), (1536, 64),
             (1600, 256), (1856, 256), (2112, 256), (2368, 256), (2624, 256), (2880, 256)]
NCM = 770
P_G1, P_U1, P_D1, P_IN, P_OUT, P_G2, P_U2, P_D2 = 0, 11, 22, 30, 43, 47, 58, 69
NPAN = 80
NPC = NPAN // NCORES


class TR:
    def __init__(self, nc, stack):
        self.nc = nc
        self.eng = {"pe": nc.tensor, "act": nc.scalar, "dve": nc.vector,
                    "pool": nc.gpsimd, "sp": nc.sync}
        self.q = {k: [] for k in self.eng}
        self.tsem = {k: stack.enter_context(nc.semaphore("t_" + k))
                     for k in ("pe", "act", "dve", "pool")}
        self.tcnt = {k: 0 for k in self.tsem}
        self.dsem = [stack.enter_context(nc.semaphore("d%d" % i)) for i in range(NDS)]
        self.dcnt = [0] * NDS
        self.drr = 0
        self.seen = {k: {} for k in self.eng}
        self.lastw = {}
        self.readers = {}
        self.out_tickets = []
        self.defer = None

    def record(self, stage_fn):
        assert self.defer is None
        self.defer = []
        try:
            stage_fn()
            return self.defer
        finally:
            self.defer = None

    def emit_merged(self, *lists):
        lists = [l for l in lists if l]
        pos = [0] * len(lists)
        total = sum(len(l) for l in lists)
        for _ in range(total):
            best, bi = None, -1
            for i, l in enumerate(lists):
                if pos[i] < len(l):
                    frac = pos[i] / len(l)
                    if best is None or frac < best:
                        best, bi = frac, i
            e, fn, reads, writes = lists[bi][pos[bi]]
            pos[bi] += 1
            self.op(e, fn, reads, writes)

    def _sem(self, k):
        return self.tsem[k] if isinstance(k, str) else self.dsem[k[1]]

    def _deps(self, e, reads, writes):
        need = {}

        def add(t):
            if t is None:
                return
            k, v = t
            if need.get(k, 0) < v:
                need[k] = v
        for r in reads:
            add(self.lastw.get(r))
        for w in writes:
            add(self.lastw.get(w))
            for k, v in self.readers.get(w, {}).items():
                add((k, v))
        out = []
        for k, v in need.items():
            if k == e and e == "pe":
                continue
            if self.seen[e].get(k, 0) >= v:
                continue
            self.seen[e][k] = v
            out.append((k, v))
        return out

    def _commit(self, ticket, reads, writes):
        for w in writes:
            self.lastw[w] = ticket
            self.readers[w] = {}
        k, v = ticket
        for r in reads:
            d = self.readers.setdefault(r, {})
            if d.get(k, 0) < v:
                d[k] = v

    def op(self, e, fn, reads=(), writes=()):
        reads = list(reads)
        writes = list(writes)
        if self.defer is not None:
            self.defer.append((e, fn, reads, writes))
            return None
        waits = self._deps(e, reads, writes)
        self.tcnt[e] += 1
        ticket = (e, self.tcnt[e])
        sem = self.tsem[e]

        def run(eng):
            for k, v in waits:
                eng.wait_ge(self._sem(k), v)
            ins = fn(eng)
            ins.then_inc(sem, 1)
        self.q[e].append(run)
        self._commit(ticket, reads, writes)
        return ticket

    def dma(self, e, out, in_, reads=(), writes=(), is_output=False, fn=None):
        reads = list(reads)
        writes = list(writes)
        i = self.drr
        self.drr = (self.drr + 1) % NDS
        waits = self._deps(e, reads, writes)
        prev = self.dcnt[i]
        if prev > 0 and self.seen[e].get(("d", i), 0) < prev:
            waits.append((("d", i), prev))
            self.seen[e][("d", i)] = prev
        self.dcnt[i] += 16
        ticket = (("d", i), self.dcnt[i])
        sem = self.dsem[i]

        def run(eng):
            for k, v in waits:
                eng.wait_ge(self._sem(k), v)
            if fn is None:
                eng.dma_start(out=out, in_=in_).then_inc(sem, 16)
            else:
                fn(eng).then_inc(sem, 16)
        self.q[e].append(run)
        self._commit(ticket, reads, writes)
        if is_output:
            self.out_tickets.append(ticket)
        return ticket

    def barrier(self):
        snap = {k: v for k, v in self.tcnt.items() if v > 0}
        for e in ("pe", "act", "dve", "pool"):
            waits = []
            for k, v in snap.items():
                if k == e and e == "pe":
                    continue
                if self.seen[e].get(k, 0) >= v:
                    continue
                self.seen[e][k] = v
                waits.append((k, v))

            def run(eng, waits=waits):
                for k, v in waits:
                    eng.wait_ge(self._sem(k), v)
            self.q[e].append(run)

    def finish(self):
        need = {}
        for k, v in self.out_tickets:
            if need.get(k, 0) < v:
                need[k] = v
        waits = list(need.items())

        def run(eng):
            for k, v in waits:
                eng.wait_ge(self._sem(k), v)
        self.q["sp"].append(run)

    def emit(self):
        nc = self.nc
        with nc.Block() as block:
            @block.tensor
            def _(eng):
                for f in self.q["pe"]:
                    f(eng)

            @block.scalar
            def _(eng):
                for f in self.q["act"]:
                    f(eng)

            @block.vector
            def _(eng):
                for f in self.q["dve"]:
                    f(eng)

            @block.gpsimd
            def _(eng):
                for f in self.q["pool"]:
                    f(eng)

            @block.sync
            def _(eng):
                for f in self.q["sp"]:
                    f(eng)


def build_w():
    nc = bass.Bass("TRN2", target_bir_lowering=False)
    src = nc.dram_tensor("wsrc", [NPC, 128, SLOT], F32, kind="ExternalInput").ap()
    dst = nc.dram_tensor("wdst", [NPC, 128, SLOT], BF16, kind="ExternalOutput").ap()
    stack = ExitStack()
    with stack:
        tr = TR(nc, stack)
        sin = [stack.enter_context(nc.sbuf_tensor("sin%d" % i, [128, SLOT], F32)) for i in range(3)]
        sout = [stack.enter_context(nc.sbuf_tensor("sout%d" % i, [128, SLOT], BF16)) for i in range(3)]
        engs = ["dve", "act", "pool"]
        for j in range(NPC):
            b = j % 3
            tr.dma("sp", sin[b][:, :], src[j], writes=[("sin", b)])
            if engs[b] == "act":
                tr.op("act", lambda e, b=b: e.copy(sout[b][:, :], sin[b][:, :]), reads=[("sin", b)], writes=[("sout", b)])
            else:
                tr.op(engs[b], lambda e, b=b: e.tensor_copy(sout[b][:, :], sin[b][:, :]), reads=[("sin", b)], writes=[("sout", b)])
            tr.dma("sp", dst[j], sout[b][:, :], reads=[("sout", b)], writes=[("dst", j)], is_output=True)
        tr.finish()
        tr.emit()
    return nc


def bcast_mid(ap_, n):
    from concourse.ap import AP
    a = ap_.ap
    return AP(ap_.tensor, ap_.offset, [list(a[0]), [0, n]] + [list(x) for x in a[1:]])


SEGBUFS = [("sQ", 128, 2 * L), ("sK", 128, 2 * L), ("sKtok", 128, 16 * 256), ("sVtok", 128, 16 * 512),
           ("sR", 64, L), ("sM", 128, 8 * L), ("sZ", 128, 4 * (L + 2))]


def build_nc(stage=None, mode="B"):
    stage = int(os.environ.get('KSTAGE', '99')) if stage is None else stage
    nc = bass.Bass("TRN2", target_bir_lowering=False)
    dt = nc.dram_tensor
    xT = dt("xT", [128, KC, L], F32, kind="ExternalInput").ap()
    wpan = dt("wpan", [NPAN, 128, SLOT], BF16, kind="ExternalInput").ap()
    gains = dt("gains", [128, 4 * KC], F32, kind="ExternalInput").ap()
    gw_d = dt("gw", [64, 512], F32, kind="ExternalInput").ap()
    gb_d = dt("gb", [128, 512], F32, kind="ExternalInput").ap()
    vecs_d = dt("vecs", [128, 20], F32, kind="ExternalInput").ap()
    cmat_d = dt("cmat", [128, NCM], F32, kind="ExternalInput").ap()
    segd = {}
    if mode == "A":
        xout = dt("xout", [128, 528], F32, kind="ExternalOutput").ap()
        x1p = dt("x1p", [128, KC, L], F32, kind="ExternalOutput").ap()
        for nm, pp, n in SEGBUFS:
            segd[nm] = dt(nm, [pp, n], BF16, kind="ExternalOutput").ap()
    else:
        yT = dt("yT", [128, KC, TOK], F32, kind="ExternalOutput").ap()
        xall = dt("xall", [NCORES * 128, 528], F32, kind="ExternalInput").ap()
        x1p = dt("x1p", [128, KC, L], F32, kind="ExternalInput").ap()
        for nm, pp, n in SEGBUFS:
            segd[nm] = dt(nm, [pp, n], BF16, kind="ExternalInput").ap()
        x1s = dt("x1s", [128, KC, L], F32).ap()
    rmask_d = dt("rmask", [128, 32], F32, kind="ExternalInput").ap() if mode == "B" else None
    KDBG = int(os.environ.get('KDBG', '0'))
    if KDBG:
        dbg = dt("dbg", [128, 8 * L], BF16, kind="ExternalOutput").ap()

    stack = ExitStack()
    with stack:
        tr = TR(nc, stack)

        def sb(name, n, dtype, parts=128):
            return stack.enter_context(nc.sbuf_tensor(name, [parts, n], dtype))

        xt_t = sb("xt", KC * T, F32)
        xn_t = sb("xn", KC * T, BF16)
        h_t = sb("h", FC * T, BF16)
        ring_t = sb("ring", NSLOT * SLOT, BF16)
        sq_t = sb("sq", KC * T, BF16)
        rstd_t = sb("rstd", T, F32)
        sg_t = sb("sgt", 4 * T, F32)
        gains_t = sb("gains_sb", 4 * KC, F32)
        ones_t = sb("ones", 128, BF16)
        ones128_t = sb("ones128", 128, BF16)
        g64_t = sb("g64", 128, BF16)
        vecs_t = sb("vecs_sb", 20, F32)
        gwst_t = sb("gwst", 512, F32, parts=64)
        gw_t = sb("gw_sb", 512, BF16, parts=64)
        gb_t = sb("gb_sb", 512, F32)
        onesrow_t = sb("onesrow", 128, BF16, parts=1)
        cmb_t = sb("cmat_bf", NCM, BF16)
        sph_t = sb("sp_hi", 512, BF16)
        spl_t = sb("sp_lo", 512, BF16)
        qT_t = sb("qT", 2 * L, BF16)
        kT_t = sb("kT", 2 * L, BF16)
        ktok_t = sb("ktok", 16 * 256, BF16)
        vtok_t = sb("vtok", 16 * 512, BF16)
        rT_t = sb("rT", L, BF16, parts=64)
        m_t = sb("m", 8 * L, BF16)
        z_t = sb("z", 4 * (L + 2), BF16)
        srun_t = sb("srun", 2 * 2 * 128, F32)
        rs_t = sb("rs", T, F32)
        tt_t = sb("tt", T, F32)
        cv_t = sb("cv", 2 * T, F32)
        dec_t = sb("dec", 16, F32)
        ctmp_t = sb("ctmp", T, F32)
        rmask_t = sb("rmask_sb", 32, F32)
        dacc_t = sb("dacc", 4, F32)
        xst_t = sb("xst", 528, F32)
        xg_t = sb("xg", 2 * 528, F32)
        hal_t = sb("hal", 8, F32)
        dp_t = sb("dp", 4, F32)

        ps = [stack.enter_context(nc.psum_tensor("ps%d" % i, [128, 512], F32)) for i in range(8)]
        ps_pool = {"all": [0, 1, 2, 3, 4, 5], "G": [0, 1], "H": [2, 3, 4, 5], "H3": [2, 3, 4], "C": [5],
                   "G0": [0], "G1": [1], "H0": [2, 3], "H1": [4, 5]}
        ps_rr = {k: 0 for k in ps_pool}
        ps_cur = ["all"]

        def ps_next():
            pool = ps_cur[0]
            lst = ps_pool[pool]
            i = ps_rr[pool]
            ps_rr[pool] = (i + 1) % len(lst)
            return lst[i]

        def staged(pool, fn, *a):
            def run():
                ps_cur[0] = pool
                try:
                    fn(*a)
                finally:
                    ps_cur[0] = "all"
            return tr.record(run)

        xt = xt_t[:, :].rearrange("p (c n) -> p c n", n=T)
        xn = xn_t[:, :].rearrange("p (c n) -> p c n", n=T)
        hh = h_t[:, :].rearrange("p (c n) -> p c n", n=T)
        sq = sq_t[:, :].rearrange("p (c n) -> p c n", n=T)
        qT3 = qT_t[:, :].rearrange("p (c n) -> p c n", n=L)
        kT3 = kT_t[:, :].rearrange("p (c n) -> p c n", n=L)
        ktok3 = ktok_t[:, :].rearrange("p (b n) -> p b n", n=256)
        vtok3 = vtok_t[:, :].rearrange("p (b n) -> p b n", n=512)
        m3 = m_t[:, :].rearrange("p (c n) -> p c n", n=L)
        z3 = z_t[:, :].rearrange("p (c n) -> p c n", n=L + 2)
        sbsave = h_t[:, 0:32 * 256].rearrange("p (n q v) -> p n q v", q=2, v=128)
        srun = srun_t[:, :].rearrange("p (d q v) -> p d q v", q=2, v=128)
        XT_ALL = [("xt", c) for c in range(KC)]
        XN_ALL = [("xn", c) for c in range(KC)]
        SQ_ALL = [("sq", c) for c in range(KC)]
        H_ALL = [("h", k) for k in range(FC)]
        SGT_ALL = [("sgt", i) for i in range(4)]

        def cm(i):
            return cmb_t[:, i * 128:(i + 1) * 128]
        M_CF, M_CB, M_TB, M_TF, MASK_F, MASK_B = (cm(i) for i in range(6))
        CI = cmb_t[:, 768:770]

        tr.op("pool", lambda e: e.memset(ones_t[:, :], 1.0 / D), writes=["ones"])
        tr.op("pool", lambda e: e.memset(ones128_t[:, :], 1.0 / 128), writes=["ones128"])
        tr.op("pool", lambda e: e.memset(g64_t[:, :], 0.0), writes=["g64"])
        tr.op("pool", lambda e: e.memset(g64_t[0:64, 0:64], 1.0 / 64), reads=["g64"], writes=["g64"])
        tr.op("pool", lambda e: e.memset(g64_t[64:128, 64:128], 1.0 / 64), reads=["g64"], writes=["g64"])
        tr.op("pool", lambda e: e.memset(onesrow_t[:, :], 1.0), writes=["onesrow"])
        tr.dma("sp", gains_t[:, :], gains, writes=["gains"])
        cmat_t = xg_t[:, 0:NCM]
        tr.dma("sp", cmat_t, cmat_d, writes=[("xg", 0), ("xg", 1)])
        tr.dma("sp", vecs_t[:, :], vecs_d, writes=["vecs"])
        tr.dma("sp", gwst_t[:, :], gw_d, writes=["gwst"])
        tr.dma("sp", gb_t[:, :], gb_d, writes=["gb"])
        if mode == "B":
            tr.dma("sp", rmask_t[:, :], rmask_d, writes=["rmask"])
        tr.op("dve", lambda e: e.tensor_copy(gw_t[:, :], gwst_t[:, :]), reads=["gwst"], writes=["gw"])
        tr.op("dve", lambda e: e.tensor_copy(cmb_t[:, :], cmat_t), reads=[("xg", 0), ("xg", 1)], writes=["cmat"])


        KFAST = int(os.environ.get('KFAST', '0'))
        def PW(idx, n):
            return (wpan[idx][:, 0:n], n)
        panels = []

        def add_ffn(f):
            pg, pu, pd = (P_G1, P_U1, P_D1) if f == 1 else (P_G2, P_U2, P_D2)
            for j in range(11):
                panels.append(PW(pg + j, 2048))
                panels.append(PW(pu + j, 2048))
            for m in range(8):
                panels.append(PW(pd + m, 2816))

        def add_phase1():
            for ti in range(TPS):
                add_ffn(1)
                for pj, (c0, nc_) in enumerate(IN_PANELS):
                    panels.append(PW(P_IN + pj, 8 * nc_))

        def add_phase3():
            for ti in range(TPS):
                for j in range(4):
                    panels.append(PW(P_OUT + j, 2048))
                add_ffn(2)
        if mode == "A":
            add_phase1()
        else:
            add_phase1()
            add_phase3()
            add_phase3()
        ring_issued = [0]
        pi = [0]

        def ring_advance(n_done):
            lim = min(len(panels), n_done + NSLOT)
            while ring_issued[0] < lim:
                i = ring_issued[0]
                ap, n = panels[i]
                s = i % NSLOT
                tr.dma("sp", ring_t[:, s * SLOT:s * SLOT + n], ap, reads=[], writes=[("ring", s)])
                ring_issued[0] += 1

        def ring_get():
            i = pi[0]
            assert i < ring_issued[0], (i, ring_issued[0])
            s = i % NSLOT
            n = panels[i][1]
            pi[0] += 1
            return ring_t[:, s * SLOT:s * SLOT + n], ("ring", s)

        def act_rsqrt(out_ap, in_ap, rd, wr):
            tr.op("act", lambda e: e.activation(out_ap, in_ap, AF.Ln, bias=EPS), reads=rd, writes=wr)
            tr.op("act", lambda e: e.activation(out_ap, out_ap, AF.Exp, scale=-0.5), reads=wr, writes=wr)

        def rmsnorm_stats():
            tr.op("act", lambda e: e.activation(sq_t[:, :], xt_t[:, :], AF.Square),
                  reads=XT_ALL, writes=SQ_ALL)
            b = ps_next()

            def mm(e, b=b):
                ins = None
                for c in range(KC):
                    ins = e.matmul(ps[b][:, :], ones_t[:, :], sq[:, c, :], start=(c == 0), stop=(c == KC - 1))
                return ins
            tr.op("pe", mm, reads=SQ_ALL + ["ones"], writes=[("ps", b)])
            act_rsqrt(rstd_t[:, :], ps[b][:, :], [("ps", b)], ["rstd"])

        def stats_sq(m):
            tr.op("act", lambda e, m=m: e.activation(sq[:, m, :], xt[:, m, :], AF.Square),
                  reads=[("xt", m)], writes=[("sq", m)])

        def stats_mm(m):
            tr.op("pe", lambda e, m=m: e.matmul(ps[7][:, :], ones_t[:, :], sq[:, m, :], start=(m == 0), stop=(m == KC - 1)),
                  reads=[("sq", m), "ones"], writes=[("ps", 7)])

        def stats_chunk(m):
            stats_sq(m)
            if m > 0:
                stats_mm(m - 1)
            if m == KC - 1:
                stats_mm(m)

        def stats_finish():
            act_rsqrt(rstd_t[:, :], ps[7][:, :], [("ps", 7)], ["rstd"])

        def rmsnorm_apply(gi, out3, out_keys):
            for c in range(KC):
                tr.op("dve", lambda e, c=c: e.scalar_tensor_tensor(
                    out3[:, c, :], xt[:, c, :], gains_t[:, gi * KC + c:gi * KC + c + 1], rstd_t[:, :],
                    ALU.mult, ALU.mult),
                    reads=[("xt", c), "rstd", "gains"], writes=[out_keys[c]])

        def mm_gu(e, w3, b, jj):
            ins = None
            for c in range(KC):
                ins = e.matmul(ps[b][:, :], w3[:, c, jj * 128:(jj + 1) * 128], xn[:, c, :],
                               start=(c == 0), stop=(c == KC - 1))
            return ins

        def mm_down(e, w3, b):
            ins = None
            for k in range(FC):
                ins = e.matmul(ps[b][:, :], w3[:, k, :], hh[:, k, :], start=(k == 0), stop=(k == FC - 1))
            return ins

        def sigmoid_chain(o, psb, bkey, okey):
            tr.op("act", lambda e: e.activation(o, psb, AF.Exp, scale=-1.0), reads=[bkey], writes=[okey])
            tr.op("act", lambda e: e.activation(o, o, AF.Ln, bias=1.0), reads=[okey], writes=[okey])
            tr.op("act", lambda e: e.activation(o, o, AF.Exp, scale=-1.0), reads=[okey], writes=[okey])

        def ffn():
            if KFAST:
                for _ in range(30):
                    ring_get()
                    ring_advance(pi[0])
                return
            for j in range(11):
                wg, kg = ring_get()
                wu, ku = ring_get()
                wg3 = wg.rearrange("p (c n) -> p c n", n=256)
                wu3 = wu.rearrange("p (c n) -> p c n", n=256)
                for jj in range(2):
                    fc = 2 * j + jj
                    bg = ps_next()
                    bu = ps_next()
                    if fc == 0:
                        for c in range(KC):
                            for w3_, b_, k_ in ((wg3, bg, kg), (wu3, bu, ku)):
                                tr.op("pe", lambda e, w3_=w3_, b_=b_, c=c: e.matmul(
                                    ps[b_][:, :], w3_[:, c, 0:128], xn[:, c, :], start=(c == 0), stop=(c == KC - 1)),
                                    reads=[("xn", c), k_], writes=[("ps", b_)])
                    else:
                        tr.op("pe", lambda e, w3=wg3, b=bg, jj=jj: mm_gu(e, w3, b, jj), reads=XN_ALL + [kg], writes=[("ps", bg)])
                        tr.op("pe", lambda e, w3=wu3, b=bu, jj=jj: mm_gu(e, w3, b, jj), reads=XN_ALL + [ku], writes=[("ps", bu)])
                    sgv = sg_t[:, (fc % 2) * T:(fc % 2 + 1) * T]
                    sgw = sg_t[:, (2 + fc % 2) * T:(3 + fc % 2) * T]
                    ka = ("sgt", fc % 2)
                    kb = ("sgt", 2 + fc % 2)
                    sigmoid_chain(sgv, ps[bg][:, :], ("ps", bg), ka)
                    tr.op("dve", lambda e, b=bg, i=sgv, o=sgw: e.tensor_tensor(o, ps[b][:, :], i, ALU.mult),
                          reads=[("ps", bg), ka], writes=[kb])
                    tr.op("dve", lambda e, b=bu, i=sgw, fc=fc: e.tensor_tensor(hh[:, fc, :], ps[b][:, :], i, ALU.mult),
                          reads=[("ps", bu), kb], writes=[("h", fc)])
                ring_advance(pi[0])
            for m in range(8):
                wd, kd = ring_get()
                wd3 = wd.rearrange("p (k n) -> p k n", n=128)
                b = ps_next()
                if m == 0:
                    for lo, hi in ((0, 16), (16, 20), (20, 22)):
                        def mmd(e, w3=wd3, b=b, lo=lo, hi=hi):
                            ins = None
                            for k in range(lo, hi):
                                ins = e.matmul(ps[b][:, :], w3[:, k, :], hh[:, k, :], start=(k == 0), stop=(k == FC - 1))
                            return ins
                        tr.op("pe", mmd, reads=[("h", k) for k in range(lo, hi)] + [kd], writes=[("ps", b)])
                else:
                    tr.op("pe", lambda e, w3=wd3, b=b: mm_down(e, w3, b), reads=H_ALL + [kd], writes=[("ps", b)])
                tr.op("dve", lambda e, b=b, m=m: e.scalar_tensor_tensor(
                    xt[:, m, :], ps[b][:, :], 0.5, xt[:, m, :], ALU.mult, ALU.add),
                    reads=[("ps", b), ("xt", m)], writes=[("xt", m)])
                stats_chunk(m)
                ring_advance(pi[0])
            stats_finish()

        def mm_fm(e, w3, c0, M, b):
            ins = None
            for c in range(KC):
                ins = e.matmul(ps[b][0:M, :], w3[:, c, c0:c0 + M], xn[:, c, :], start=(c == 0), stop=(c == KC - 1))
            return ins

        def mm_tm(e, w3, blk, N, b):
            ins = None
            for c in range(KC):
                ins = e.matmul(ps[b][:, 0:N], xn[:, c, blk * 128:(blk + 1) * 128], w3[:, c, 0:N],
                               start=(c == 0), stop=(c == KC - 1))
            return ins

        evac_rr = [0]

        def evac(out_ap, in_ap, rd, wr, scale=None):
            e = "act" if evac_rr[0] % 2 == 0 else "dve"
            evac_rr[0] += 1
            if e == "act":
                if scale is None:
                    tr.op("act", lambda g: g.copy(out_ap, in_ap), reads=rd, writes=wr)
                else:
                    tr.op("act", lambda g: g.mul(out_ap, in_ap, scale), reads=rd, writes=wr)
            else:
                if scale is None:
                    tr.op("dve", lambda g: g.tensor_copy(out_ap, in_ap), reads=rd, writes=wr)
                else:
                    tr.op("dve", lambda g: g.tensor_scalar_mul(out_ap, in_ap, scale), reads=rd, writes=wr)

        def projection(ti):
            tl = ti * T
            w, kw = ring_get()
            w3 = w.rearrange("p (c n) -> p c n", n=256)
            qb = [ps_next(), ps_next()]
            for c in range(KC):
                for c2 in range(2):
                    tr.op("pe", lambda e, w3=w3, c2=c2, c=c: e.matmul(
                        ps[qb[c2]][:, :], w3[:, c, c2 * 128:(c2 + 1) * 128], xn[:, c, :], start=(c == 0), stop=(c == KC - 1)),
                        reads=[("xn", c), kw], writes=[("ps", qb[c2])])
            for c2 in range(2):
                evac(qT3[:, c2, tl:tl + T], ps[qb[c2]][:, :], [("ps", qb[c2])], [("qT", c2, ti)], scale=0.125)
            ring_advance(pi[0])
            w, kw = ring_get()
            w3 = w.rearrange("p (c n) -> p c n", n=256)
            for c2 in range(2):
                b = ps_next()
                tr.op("pe", lambda e, w3=w3, c2=c2, b=b: mm_fm(e, w3, c2 * 128, 128, b), reads=XN_ALL + [kw], writes=[("ps", b)])
                evac(kT3[:, c2, tl:tl + T], ps[b][:, :], [("ps", b)], [("kT", c2, ti)])
            for bl in range(4):
                blk = ti * 4 + bl
                b = ps_next()
                tr.op("pe", lambda e, w3=w3, bl=bl, b=b: mm_tm(e, w3, bl, 256, b), reads=XN_ALL + [kw], writes=[("ps", b)])
                evac(ktok3[:, blk, :], ps[b][:, 0:256], [("ps", b)], [("ktok", blk)])
            ring_advance(pi[0])
            for hv in range(2):
                w, kw = ring_get()
                w3 = w.rearrange("p (c n) -> p c n", n=256)
                for bl in range(4):
                    blk = ti * 4 + bl
                    b = ps_next()
                    tr.op("pe", lambda e, w3=w3, bl=bl, b=b: mm_tm(e, w3, bl, 256, b), reads=XN_ALL + [kw], writes=[("ps", b)])
                    evac(vtok3[:, blk, hv * 256:(hv + 1) * 256], ps[b][:, 0:256], [("ps", b)], [("vtok", blk, hv)])
                ring_advance(pi[0])
            for hg in range(2):
                w, kw = ring_get()
                w3 = w.rearrange("p (c n) -> p c n", n=256)
                for c2 in range(2):
                    c = hg * 2 + c2
                    b = ps_next()
                    tr.op("pe", lambda e, w3=w3, c2=c2, b=b: mm_fm(e, w3, c2 * 128, 128, b), reads=XN_ALL + [kw], writes=[("ps", b)])
                    sgv = sg_t[:, (c % 2) * T:(c % 2 + 1) * T]
                    ka = ("sgt", c % 2)
                    sigmoid_chain(sgv, ps[b][:, :], ("ps", b), ka)
                    tr.op("dve", lambda e, b=b, c=c, i=sgv: e.scalar_tensor_tensor(
                        m3[:, c, tl:tl + T], ps[b][:, :], vecs_t[:, 16 + c:17 + c], i, ALU.mult, ALU.mult),
                        reads=[("ps", b), ka, "vecs"], writes=[("m", c, ti)])
                ring_advance(pi[0])
            w, kw = ring_get()
            w3 = w.rearrange("p (c n) -> p c n", n=64)
            b = ps_next()
            tr.op("pe", lambda e, w3=w3, b=b: mm_fm(e, w3, 0, 64, b), reads=XN_ALL + [kw], writes=[("ps", b)])
            evac(rT_t[:, tl:tl + T], ps[b][0:64, :], [("ps", b)], [("rT", ti)])
            ring_advance(pi[0])
            for hb in range(2):
                w, kw = ring_get()
                w3 = w.rearrange("p (c n) -> p c n", n=256)
                for c2 in range(2):
                    c = hb * 2 + c2
                    b = ps_next()
                    tr.op("pe", lambda e, w3=w3, c2=c2, b=b: mm_fm(e, w3, c2 * 128, 128, b), reads=XN_ALL + [kw], writes=[("ps", b)])
                    evac(m3[:, 4 + c, tl:tl + T], ps[b][:, :], [("ps", b)], [("m", 4 + c, ti)])
                ring_advance(pi[0])
            cb = []
            for hc in range(2):
                w, kw = ring_get()
                w3 = w.rearrange("p (c n) -> p c n", n=256)
                for c2 in range(2):
                    b = ps_next()
                    cb.append(b)
                    tr.op("pe", lambda e, w3=w3, c2=c2, b=b: mm_fm(e, w3, c2 * 128, 128, b), reads=XN_ALL + [kw], writes=[("ps", b)])
                ring_advance(pi[0])
            cst = [cv_t[:, 0:T], cv_t[:, T:2 * T], ctmp_t[:, :], tt_t[:, :]]
            ckey = [("cv", 0), ("cv", 1), "ctmp", "tt"]
            for c in range(4):
                evac(cst[c], ps[cb[c]][:, :], [("ps", cb[c])], [ckey[c]])
            for hc in range(2):
                w, kw = ring_get()
                w3 = w.rearrange("p (c n) -> p c n", n=256)
                for c2 in range(2):
                    c = hc * 2 + c2
                    b = ps_next()
                    tr.op("pe", lambda e, w3=w3, c2=c2, b=b: mm_fm(e, w3, c2 * 128, 128, b), reads=XN_ALL + [kw], writes=[("ps", b)])
                    tr.op("dve", lambda e, b=b, c=c: e.tensor_tensor(
                        z3[:, c, 1 + tl:1 + tl + T], ps[b][:, :], cst[c], ALU.mult),
                        reads=[("ps", b), ckey[c]], writes=[("z", c, ti)])
                ring_advance(pi[0])

        def f32tmp(i):
            return sg_t[:, i * 256:(i + 1) * 256]
        SP = [f32tmp(0), f32tmp(1)]
        ET = f32tmp(2)
        ETD = [f32tmp(2), f32tmp(7)]
        SPH = [sph_t[:, 0:256], sph_t[:, 256:512]]
        SPL = [spl_t[:, 0:256], spl_t[:, 256:512]]
        EQ = [f32tmp(3), f32tmp(4)]
        EI = [f32tmp(5), f32tmp(6)]

        def b16tmp(i):
            return xn_t[:, i * 256:(i + 1) * 256]
        KTB = [[b16tmp(0), b16tmp(1)], [b16tmp(2), b16tmp(3)]]
        QDB = [[xn_t[:, 1024 + par * 512 + d * 256:1024 + par * 512 + (d + 1) * 256] for par in range(2)] for d in range(2)]
        KIB = [[xn_t[:, 2048 + par * 512 + d * 256:2048 + par * 512 + (d + 1) * 256] for par in range(2)] for d in range(2)]
        PM = [xn_t[:, 3072 + i * 128:3072 + (i + 1) * 128] for i in range(4)]
        SFB = [xn_t[:, 3584 + i * 256:3584 + (i + 1) * 256].rearrange("p (q v) -> p q v", v=128) for i in range(2)]
        DECB = [[0, 8], [4, 12]]

        def fence(keys):
            tr.barrier()

        def gate_sp(d, blk):
            b = ps_next()
            r0 = 0 if d == 0 else 32

            tr.op("pe", lambda e, b=b: e.matmul(ps[b][:, 0:256], rT_t[r0:r0 + 16, blk * 128:(blk + 1) * 128],
                                                gw_t[r0:r0 + 16, d * 256:(d + 1) * 256], start=True, stop=True),
                  reads=[("rT", blk // 4), "gw"], writes=[("ps", b)])
            tr.op("dve", lambda e, b=b: e.tensor_tensor(SP[d], ps[b][:, 0:256], gb_t[:, d * 256:(d + 1) * 256], ALU.add),
                  reads=[("ps", b), "gb"], writes=[("sp", d)])
            tr.op("act", lambda e: e.activation(SP[d], SP[d], AF.Exp, scale=-1.0), reads=[("sp", d)], writes=[("sp", d)])
            tr.op("act", lambda e: e.activation(SP[d], SP[d], AF.Ln, bias=1.0), reads=[("sp", d)], writes=[("sp", d)])
            tr.op("act", lambda e: e.copy(SPH[d], SP[d]), reads=[("sp", d)], writes=[("sph", d)])
            tr.op("dve", lambda e: e.tensor_tensor(SPL[d], SP[d], SPH[d], ALU.subtract),
                  reads=[("sp", d), ("sph", d)], writes=[("spl", d)])

        def gate_tail(d, blk, par):
            b = ps_next()
            Mt = M_TF if d == 0 else M_TB

            def mm(e, b=b):
                e.matmul(ps[b][:, 0:256], Mt, SPH[d], start=True, stop=False)
                return e.matmul(ps[b][:, 0:256], Mt, SPL[d], start=False, stop=True)
            tr.op("pe", mm, reads=[("sph", d), ("spl", d), "cmat"], writes=[("ps", b)])
            tr.op("act", lambda e, b=b: e.activation(ETD[d], ps[b][:, 0:256], AF.Exp), reads=[("ps", b)], writes=[("et", d)])
            tr.op("pool", lambda e: e.tensor_tensor(KTB[d][par], ktok3[:, blk, :], ETD[d], ALU.mult),
                  reads=[("et", d), ("ktok", blk)], writes=[("kt", d, par)])

        def gate_dec_ci(d, blk, par):
            b = ps_next()
            base = DECB[d][par]

            def mm(e, b=b):
                ins = None
                for q in range(2):
                    e.matmul(ps[b][:, q * 2:q * 2 + 2], SPH[d][:, q * 128:(q + 1) * 128], CI, start=True, stop=False)
                    ins = e.matmul(ps[b][:, q * 2:q * 2 + 2], SPL[d][:, q * 128:(q + 1) * 128], CI, start=False, stop=True)
                return ins
            tr.op("pe", mm, reads=[("sph", d), ("spl", d), "cmat"], writes=[("ps", b)])
            tr.op("act", lambda e, b=b: e.activation(dec_t[:, base:base + 4], ps[b][:, 0:4], AF.Exp),
                  reads=[("ps", b)], writes=[("dec", d, par)])

        def state_update(d, blk, ch, par):
            for q in range(2):
                b = ps_next()
                r0 = ch * 64
                tr.op("pe", lambda e, b=b, q=q, r0=r0: e.matmul(
                    ps[b][:, 0:256], KTB[d][par][r0:r0 + 64, q * 128:(q + 1) * 128],
                    vtok3[r0:r0 + 64, blk, q * 256:(q + 1) * 256], start=True, stop=True),
                    reads=[("kt", d, par), ("vtok", blk, q)], writes=[("ps", b)])
                for hh_ in range(2):
                    p0 = hh_ * 64
                    col = DECB[d][par] + q * 2 + ch
                    tr.op("dve", lambda e, b=b, q=q, p0=p0, col=col, hh_=hh_: e.scalar_tensor_tensor(
                        srun[p0:p0 + 64, d, q, :], srun[p0:p0 + 64, d, q, :], dec_t[p0:p0 + 64, col:col + 1],
                        ps[b][p0:p0 + 64, hh_ * 128:(hh_ + 1) * 128], ALU.mult, ALU.add),
                        reads=[("ps", b), ("dec", d, par), ("srun", d, q)], writes=[("srun", d, q)])

        SRUN_ALL = [("srun", d, q) for d in range(2) for q in range(2)]

        def local_scan_and_exchange(do_scan, do_fold):
            tr.barrier()
            tr.op("pool", lambda e: e.memset(srun_t[:, :], 0.0), writes=SRUN_ALL)
            tr.op("pool", lambda e: e.memset(dacc_t[:, :], 1.0), writes=[("dacc", 0), ("dacc", 1)])
            tr.op("pool", lambda e: e.memset(hal_t[:, :], 0.0), writes=["hal"])
            if do_scan:
                local_scan()
            if do_fold:
                fold_exchange()

        def local_scan():
            orders = {1: list(range(15, -1, -1)), 0: list(range(16))}

            def G(d, blk):
                gate_sp(d, blk)
                gate_tail(d, blk, blk % 2)
                gate_dec_ci(d, blk, blk % 2)

            def H(d, blk):
                par = blk % 2
                for ch in ((1, 0) if d == 1 else (0, 1)):
                    state_update(d, blk, ch, par)
                base = DECB[d][par]
                dv = dec_t[:, base:base + 4].rearrange("p (q c) -> p q c", c=2)
                for ch in range(2):
                    tr.op("dve", lambda e, d=d, dv=dv, ch=ch: e.tensor_tensor(
                        dacc_t[:, d * 2:d * 2 + 2], dacc_t[:, d * 2:d * 2 + 2], dv[:, :, ch], ALU.mult),
                        reads=[("dec", d, par), ("dacc", d)], writes=[("dacc", d)])
            G(1, orders[1][0])
            G(0, orders[0][0])
            for i in range(16):
                streams = []
                for d in (1, 0):
                    if i + 1 < 16:
                        streams.append(staged("G%d" % d, G, d, orders[d][i + 1]))
                    streams.append(staged("H%d" % d, H, d, orders[d][i]))
                tr.emit_merged(*streams)
            tr.op("dve", lambda e: e.tensor_copy(xst_t[:, 0:512], srun_t[:, :]), reads=SRUN_ALL, writes=["xst"])
            tr.op("dve", lambda e: e.tensor_copy(xst_t[:, 512:516], dacc_t[:, :]), reads=[("dacc", 0), ("dacc", 1), "xst"], writes=["xst"])
            tr.op("dve", lambda e: e.tensor_copy(xst_t[:, 516:520].rearrange("p (c o) -> p c o", o=1), z3[:, :, 1:2]),
                  reads=[("z", c, 0) for c in range(4)] + ["xst"], writes=["xst"])
            tr.op("dve", lambda e: e.tensor_copy(xst_t[:, 520:524].rearrange("p (c o) -> p c o", o=1), z3[:, :, L:L + 1]),
                  reads=[("z", c, TPS - 1) for c in range(4)] + ["xst"], writes=["xst"])
            tr.op("pool", lambda e: e.memset(xst_t[:, 524:528], 0.0), reads=["xst"], writes=["xst"])
            tr.dma("sp", xout, xst_t[:, :], reads=["xst"], writes=["xout"], is_output=True)

        def fold_exchange():
            step = 0
            for d in (0, 1):
                for cp in (range(NCORES) if d == 0 else range(NCORES - 1, -1, -1)):
                    sl = step % 2
                    step += 1
                    xg = xg_t[:, sl * 528:(sl + 1) * 528]
                    kx = ("xg", sl)
                    tr.dma("sp", xg, xall[cp * 128:(cp + 1) * 128, :], reads=[], writes=[kx])
                    w = rmask_t[:, d * 8 + cp:d * 8 + cp + 1]
                    for q in range(2):
                        dcol = xg[:, 512 + d * 2 + q:512 + d * 2 + q + 1]
                        dp = dp_t[:, q:q + 1]
                        tr.op("dve", lambda e, dp=dp, dcol=dcol, w=w: e.tensor_scalar(dp, dcol, -1.0, w, ALU.add, ALU.mult),
                              reads=[kx, "rmask"], writes=[("dp", q)])
                        tr.op("dve", lambda e, dp=dp: e.tensor_scalar_add(dp, dp, 1.0), reads=[("dp", q)], writes=[("dp", q)])
                        S = srun[:, d, q, :]
                        tr.op("dve", lambda e, S=S, dp=dp: e.tensor_scalar_mul(S, S, dp),
                              reads=[("dp", q), ("srun", d, q)], writes=[("srun", d, q)])
                        Lp = xg[:, d * 256 + q * 128:d * 256 + (q + 1) * 128]
                        tr.op("dve", lambda e, S=S, Lp=Lp, w=w: e.scalar_tensor_tensor(S, Lp, w, S, ALU.mult, ALU.add),
                              reads=[kx, "rmask", ("srun", d, q)], writes=[("srun", d, q)])
                    if d == 0:
                        tr.op("dve", lambda e, xg=xg, cp=cp: e.scalar_tensor_tensor(
                            hal_t[:, 0:4], xg[:, 520:524], rmask_t[:, 16 + cp:17 + cp], hal_t[:, 0:4], ALU.mult, ALU.add),
                            reads=[kx, "rmask", "hal"], writes=["hal"])
                        tr.op("dve", lambda e, xg=xg, cp=cp: e.scalar_tensor_tensor(
                            hal_t[:, 4:8], xg[:, 516:520], rmask_t[:, 24 + cp:25 + cp], hal_t[:, 4:8], ALU.mult, ALU.add),
                            reads=[kx, "rmask", "hal"], writes=["hal"])
            tr.op("dve", lambda e: e.tensor_copy(z3[:, :, 0:1], hal_t[:, 0:4].rearrange("p (c o) -> p c o", o=1)),
                  reads=["hal"], writes=[("zh", 0)])
            tr.op("dve", lambda e: e.tensor_copy(z3[:, :, L + 1:L + 2], hal_t[:, 4:8].rearrange("p (c o) -> p c o", o=1)),
                  reads=["hal"], writes=[("zh", 1)])

        def phase2(seg, zero_init=True):
            fence(H_ALL + SGT_ALL + XN_ALL)
            if zero_init:
                tr.op("pool", lambda e: e.memset(srun_t[:, :], 0.0), writes=SRUN_ALL)
            def GA(blk):
                gate_sp(1, blk)
                gate_tail(1, blk, blk % 2)
                gate_dec_ci(1, blk, blk % 2)

            def HA(blk):
                for ch in (1, 0):
                    n = blk * 2 + ch
                    tr.op("act", lambda e, n=n: e.copy(sbsave[:, n, :, :], srun[:, 1, :, :]),
                          reads=[("srun", 1, 0), ("srun", 1, 1)], writes=[("sbs", n)])
                    state_update(1, blk, ch, blk % 2)
            GA(15)
            for blk in range(15, -1, -1):
                g = staged("G", GA, blk - 1) if blk > 0 else []
                h = staged("H", HA, blk)
                tr.emit_merged(g, h)

            def GB(blk):
                par = blk % 2
                ti = blk // 4
                tok = slice(blk * 128, (blk + 1) * 128)
                SP2, SPH2, SPL2 = sg_t[:, 0:512], sph_t[:, 0:512], spl_t[:, 0:512]
                EQ2, EI2 = sg_t[:, 768:1280], sg_t[:, 1280:1792]
                SPK = [("sp", 0), ("sp", 1)]
                SPHK = [("sph", 0), ("sph", 1)]
                SPLK = [("spl", 0), ("spl", 1)]
                b = ps_next()

                tr.op("pe", lambda e, b=b: e.matmul(ps[b][:, :], rT_t[0:64, tok], gw_t[0:64, :], start=True, stop=True),
                      reads=[("rT", ti), "gw"], writes=[("ps", b)])
                tr.op("dve", lambda e, b=b: e.tensor_tensor(SP2, ps[b][:, :], gb_t[:, :], ALU.add),
                      reads=[("ps", b), "gb"], writes=SPK)
                tr.op("act", lambda e: e.activation(SP2, SP2, AF.Exp, scale=-1.0), reads=SPK, writes=SPK)
                tr.op("act", lambda e: e.activation(SP2, SP2, AF.Ln, bias=1.0), reads=SPK, writes=SPK)
                tr.op("act", lambda e: e.copy(SPH2, SP2), reads=SPK, writes=SPHK)
                tr.op("pool", lambda e: e.tensor_tensor(SPL2, SP2, SPH2, ALU.subtract), reads=SPK + SPHK, writes=SPLK)
                gate_tail(0, blk, par)
                b2 = ps_next()

                def mmc(e, b2=b2):
                    ins = None
                    for d in range(2):
                        Mc = M_CF if d == 0 else M_CB
                        for q in range(2):
                            o = ps[b2][:, d * 256 + q * 128:d * 256 + (q + 1) * 128]
                            e.matmul(o, SPH[d][:, q * 128:(q + 1) * 128], Mc, start=True, stop=False)
                            ins = e.matmul(o, SPL[d][:, q * 128:(q + 1) * 128], Mc, start=False, stop=True)
                    return ins
                tr.op("pe", mmc, reads=SPHK + SPLK + ["cmat"], writes=[("ps", b2)])
                tr.op("act", lambda e, b2=b2: e.activation(EQ2, ps[b2][:, :], AF.Exp),
                      reads=[("ps", b2)], writes=[("eq", 0), ("eq", 1)])
                tr.op("act", lambda e, b2=b2: e.activation(EI2, ps[b2][:, :], AF.Exp, scale=-1.0),
                      reads=[("ps", b2)], writes=[("ei", 0), ("ei", 1)])
                QD2 = xn_t[:, 1024 + par * 512:1024 + (par + 1) * 512].rearrange("p (d q v) -> p d q v", d=2, v=128)
                KI2 = xn_t[:, 2048 + par * 512:2048 + (par + 1) * 512].rearrange("p (d q v) -> p d q v", d=2, v=128)
                tr.op("dve", lambda e: e.tensor_tensor(
                    QD2, bcast_mid(qT3[:, :, tok], 2), EQ2.rearrange("p (d q v) -> p d q v", d=2, v=128), ALU.mult),
                    reads=[("eq", 0), ("eq", 1), ("qT", 0, ti), ("qT", 1, ti)], writes=[("qd", 0, par), ("qd", 1, par)])
                tr.op("pool", lambda e: e.tensor_tensor(
                    KI2, bcast_mid(kT3[:, :, tok], 2), EI2.rearrange("p (d q v) -> p d q v", d=2, v=128), ALU.mult),
                    reads=[("ei", 0), ("ei", 1), ("kT", 0, ti), ("kT", 1, ti)], writes=[("ki", 0, par), ("ki", 1, par)])
                base = DECB[0][par]
                tr.op("dve", lambda e, base=base: e.tensor_copy(
                    dec_t[:, base:base + 4], EQ[0].rearrange("p (j s) -> p j s", s=64)[:, :, 63]),
                    reads=[("eq", 0)], writes=[("dec", 0, par)])

            def HB(blk):
                par = blk % 2
                ti = blk // 4
                tok = slice(blk * 128, (blk + 1) * 128)
                ob = 6 + blk % 2
                QD = [QDB[0][par], QDB[1][par]]
                KI = [KIB[0][par], KIB[1][par]]
                for ch in range(2):
                    tr.op("act", lambda e, ch=ch: e.copy(SFB[ch], srun[:, 0, :, :]),
                          reads=[("srun", 0, 0), ("srun", 0, 1)], writes=[("sfb", ch)])
                    state_update(0, blk, ch, par)
                PMD = [sq_t[:, 1024:1536], sq_t[:, 1536:2048]]
                for hh_ in range(2):
                    p0 = hh_ * 64
                    for d in range(2):
                        b = ps_next()

                        def mms(e, b=b, d=d, p0=p0):
                            ins = None
                            for q in range(2):
                                ins = e.matmul(ps[b][:, q * 128:(q + 1) * 128], KI[d][p0:p0 + 64, q * 128:(q + 1) * 128],
                                               QD[d][p0:p0 + 64, q * 128:(q + 1) * 128], start=True, stop=True)
                            return ins
                        tr.op("pe", mms, reads=[("ki", d, par), ("qd", d, par)], writes=[("ps", b)])
                        mk = MASK_F if d == 0 else MASK_B
                        tr.op("dve", lambda e, b=b, d=d, mk=mk, hh_=hh_: e.tensor_tensor(
                            PMD[d].rearrange("p (q h v) -> p q h v", h=2, v=128)[:, :, hh_, :],
                            ps[b][:, 0:256].rearrange("p (q v) -> p q v", v=128),
                            bcast_mid(mk, 2), ALU.mult),
                            reads=[("ps", b), "cmat", ("sq", 2 + d)], writes=[("sq", 2 + d)])
                for h in range(4):
                    q, hh_ = h // 2, h % 2
                    p0 = hh_ * 64

                    def mo(e, h=h, q=q, p0=p0, ob=ob, blk=blk):
                        o = ps[ob][:, h * 128:(h + 1) * 128]
                        vh = vtok3[:, blk, h * 128:(h + 1) * 128]
                        e.matmul(o, vh, PMD[0][:, h * 128:(h + 1) * 128], start=True, stop=False)
                        e.matmul(o, vh, PMD[1][:, h * 128:(h + 1) * 128], start=False, stop=False)
                        ins = None
                        for ch in range(2):
                            oc = ps[ob][:, h * 128 + ch * 64:h * 128 + ch * 64 + 64]
                            e.matmul(oc, SFB[ch][p0:p0 + 64, q, :], QD[0][p0:p0 + 64, q * 128 + ch * 64:q * 128 + ch * 64 + 64],
                                     start=False, stop=False)
                            ins = e.matmul(oc, sbsave[p0:p0 + 64, blk * 2 + ch, q, :],
                                           QD[1][p0:p0 + 64, q * 128 + ch * 64:q * 128 + ch * 64 + 64],
                                           start=False, stop=(ch == 1))
                        return ins
                    tr.op("pe", mo, reads=[("sq", 2), ("sq", 3), ("vtok", blk, h // 2),
                                           ("sfb", 0), ("sfb", 1), ("sbs", blk * 2), ("sbs", blk * 2 + 1),
                                           ("qd", 0, par), ("qd", 1, par)],
                          writes=[("ps", ob)] if h == 0 else [("pso", ob, h)])
                OKEYS = [("ps", ob)] + [("pso", ob, h) for h in range(1, 4)]
                tr.op("act", lambda e, ob=ob: e.activation(sq_t[:, 0:512], ps[ob][:, :], AF.Square),
                      reads=OKEYS, writes=[("sq", 0)])
                b = ps_next()
                tr.op("pe", lambda e, b=b: e.matmul(ps[b][:, :], ones128_t[:, :], sq_t[:, 0:512], start=True, stop=True),
                      reads=[("sq", 0), "ones128"], writes=[("ps", b)])
                act_rsqrt(rs_t[:, :], ps[b][:, :], [("ps", b)], ["rs"])
                tr.op("dve", lambda e, ob=ob: e.tensor_tensor(tt_t[:, :], ps[ob][:, :], rs_t[:, :], ALU.mult),
                      reads=OKEYS + ["rs"], writes=["tt"])
                tr.op("pool", lambda e, tok=tok: e.tensor_tensor(
                    m3[:, 0:4, tok], tt_t[:, :].rearrange("p (h v) -> p h v", v=128), m3[:, 0:4, tok], ALU.mult),
                    reads=["tt"] + [("m", c, ti) for c in range(4)], writes=[("m", c, ti) for c in range(4)])
            def conv_item(ti, c):
                tl = ti * T
                a = cv_t[:, (c % 2) * T:(c % 2 + 1) * T]
                ka = ("cv", c % 2)
                zr = [("z", c, ti)]
                zr.append(("z", c, ti - 1) if ti > 0 else ("zh", 0))
                zr.append(("z", c, ti + 1) if ti < TPS - 1 else ("zh", 1))
                tr.op("dve", lambda e: e.tensor_scalar_mul(a, z3[:, c, tl:tl + T], vecs_t[:, c * 3:c * 3 + 1]),
                      reads=zr + ["vecs"], writes=[ka])
                for i in (1, 2):
                    tr.op("dve", lambda e, i=i: e.scalar_tensor_tensor(
                        a, z3[:, c, tl + i:tl + i + T], vecs_t[:, c * 3 + i:c * 3 + i + 1], a, ALU.mult, ALU.add),
                        reads=zr + ["vecs", ka], writes=[ka])
                tr.op("pool", lambda e: e.tensor_tensor(a, a, m3[:, 4 + c, tl:tl + T], ALU.mult),
                      reads=[ka, ("m", 4 + c, ti)], writes=[ka])
                tr.op("act", lambda e: e.activation(sq_t[:, 512:1024], a, AF.Square), reads=[ka], writes=[("sq", 1)])
                b = ps_next()
                tr.op("pe", lambda e: e.matmul(ps[b][:, :], g64_t[:, :], sq_t[:, 512:1024], start=True, stop=True),
                      reads=[("sq", 1), "g64"], writes=[("ps", b)])
                act_rsqrt(rstd_t[:, :], ps[b][:, :], [("ps", b)], ["rstd"])
                tr.op("dve", lambda e: e.scalar_tensor_tensor(
                    m3[:, 4 + c, tl:tl + T], a, vecs_t[:, 12 + c:13 + c], rstd_t[:, :], ALU.mult, ALU.mult),
                    reads=[ka, "rstd", "vecs"], writes=[("m", 4 + c, ti)])

            GB(0)
            for blk in range(16):
                g = staged("G", GB, blk + 1) if blk + 1 < 16 else []
                h = staged("H3", HB, blk)
                cst = staged("C", conv_item, blk // 4, blk % 4)
                tr.emit_merged(g, h, cst)
            fence(H_ALL + SGT_ALL + XN_ALL + [("sbs", n) for n in range(32)])

        def wout(ti):
            tl = ti * T
            for j in range(4):
                w, kw = ring_get()
                w3 = w.rearrange("p (c n) -> p c n", n=256)
                for c2 in range(2):
                    mo_ = j * 2 + c2
                    b = ps_next()

                    def mm(e, w3=w3, c2=c2, b=b):
                        ins = None
                        for k in range(KC):
                            ins = e.matmul(ps[b][:, :], w3[:, k, c2 * 128:(c2 + 1) * 128], m3[:, k, tl:tl + T],
                                           start=(k == 0), stop=(k == KC - 1))
                        return ins
                    tr.op("pe", mm, reads=[("m", k, ti) for k in range(KC)] + [kw], writes=[("ps", b)])
                    tr.op("dve", lambda e, b=b, mo_=mo_: e.tensor_tensor(xt[:, mo_, :], xt[:, mo_, :], ps[b][:, :], ALU.add),
                          reads=[("ps", b), ("xt", mo_)], writes=[("xt", mo_)])
                    stats_sq(mo_)
                ring_advance(pi[0])
            for mo_ in range(KC):
                stats_mm(mo_)
            stats_finish()

        SEG_KEYS = {
            "sQ": [("qT", c2, ti) for c2 in range(2) for ti in range(TPS)],
            "sK": [("kT", c2, ti) for c2 in range(2) for ti in range(TPS)],
            "sKtok": [("ktok", blk) for blk in range(16)],
            "sVtok": [("vtok", blk, hv) for blk in range(16) for hv in range(2)],
            "sR": [("rT", ti) for ti in range(TPS)],
            "sM": [("m", c, ti) for c in range(8) for ti in range(TPS)],
            "sZ": [("z", c, ti) for c in range(4) for ti in range(TPS)] + [("zh", 0), ("zh", 1)],
        }
        SEG_SB = {"sQ": qT_t, "sK": kT_t, "sKtok": ktok_t, "sVtok": vtok_t, "sR": rT_t, "sM": m_t, "sZ": z_t}
        ring_advance(0)

        def phase1(x1_dst):
            for ti in range(TPS):
                t0 = ti * T
                tr.dma("pool", xt, xT[:, :, t0:t0 + T], reads=[], writes=XT_ALL)
                rmsnorm_stats()
                rmsnorm_apply(0, xn, XN_ALL)
                ffn()
                tr.dma("pool", x1_dst[:, :, t0:t0 + T], xt, reads=XT_ALL, writes=[("x1s", t0)], is_output=(mode == "A"))
                rmsnorm_apply(1, xn, XN_ALL)
                projection(ti)

        def phase3(x1_src, x1_keyed, yoff, prefetch=None):
            for ti in range(TPS):
                t0 = ti * T
                tr.dma("pool", xt, x1_src[:, :, t0:t0 + T], reads=([("x1s", t0)] if x1_keyed else []), writes=XT_ALL)
                if prefetch is not None:
                    for nm in prefetch.get(ti, []):
                        tr.dma("act", SEG_SB[nm][:, :], segd[nm], reads=[], writes=SEG_KEYS[nm])
                wout(ti)
                rmsnorm_apply(2, xn, XN_ALL)
                ffn()
                if prefetch is not None:
                    sm3 = segd["sM"].rearrange("p (c n) -> p c n", n=L)
                    tr.dma("act", m3[:, :, t0:t0 + T], sm3[:, :, t0:t0 + T], reads=[],
                           writes=[("m", c, ti) for c in range(8)])
                yst = h_t[:, 0:2 * KC * T].bitcast(F32).rearrange("p (c n) -> p c n", n=T)
                YK = [("h", 2 * c) for c in range(KC)] + [("h", 2 * c + 1) for c in range(KC)]
                rmsnorm_apply(3, yst, [("h", 2 * c) for c in range(KC)])
                tr.dma("sp", yT[:, :, yoff + t0:yoff + t0 + T], yst, reads=YK, writes=[("yT", yoff + t0)] + [("h", 2 * c + 1) for c in range(KC)],
                       is_output=True)

        if mode == "A":
            phase1(x1p)
            local_scan_and_exchange(True, False)
            for nm, pp, n in SEGBUFS:
                tr.dma("pool", segd[nm], SEG_SB[nm][:, :], reads=SEG_KEYS[nm], writes=[("segd", nm)], is_output=True)
        else:
            tr.op("pool", lambda e: e.memset(z3[:, :, 0:1], 0.0), writes=[("zh", 0)])
            tr.op("pool", lambda e: e.memset(z3[:, :, L + 1:L + 2], 0.0), writes=[("zh", 1)])
            phase1(x1s)
            phase2(0)
            phase3(x1s, True, 0, prefetch={0: ["sQ", "sK", "sR"], 1: ["sKtok", "sVtok"], 2: ["sZ"]})
            local_scan_and_exchange(False, True)
            phase2(1, zero_init=False)
            phase3(x1p, False, L)
        assert pi[0] == len(panels), (pi[0], len(panels))
        print('TCNT', tr.tcnt, max(tr.dcnt))
        tr.finish()
        tr.emit()
    return nc


def _cmat():
    a = np.arange(128)[:, None]
    b = np.arange(128)[None, :]
    same = (a // 64) == (b // 64)
    f32 = np.float32
    u1 = (same & (a <= b)).astype(f32)
    l1 = (same & (a >= b)).astype(f32)
    su = (same & (a < b)).astype(f32)
    sl = (same & (a > b)).astype(f32)
    ci = ((np.arange(128)[:, None] // 64) == np.arange(2)[None, :]).astype(f32)
    s = f32(-1.0 / 16.0)
    return np.ascontiguousarray(np.concatenate([s * u1, s * l1, s * su, s * sl, u1, l1, s * ci], axis=1))


def _panelize(inputs):
    f32 = np.float32
    out = np.zeros((NPAN, 128, SLOT), f32)

    def kmajor(w):
        K, n = w.shape
        return w.reshape(K // 128, 128, n).transpose(1, 0, 2).reshape(128, (K // 128) * n)
    for f, (pg, pu, pd) in ((1, (P_G1, P_U1, P_D1)), (2, (P_G2, P_U2, P_D2))):
        wg = np.asarray(inputs["ffn%d_w_gate" % f], f32)[0]
        wu = np.asarray(inputs["ffn%d_w_up" % f], f32)[0]
        wd = np.asarray(inputs["ffn%d_w_down" % f], f32)[0]
        for j in range(11):
            out[pg + j, :, :2048] = kmajor(wg[:, 256 * j:256 * j + 256])
            out[pu + j, :, :2048] = kmajor(wu[:, 256 * j:256 * j + 256])
        for m in range(8):
            out[pd + m, :, :2816] = kmajor(wd[:, 128 * m:128 * m + 128])
    w_in = np.asarray(inputs["w_in"], f32)[0]
    w_in_p = np.zeros((D, N_IN_PAD), f32)
    w_in_p[:, 0:1536] = w_in[:, 0:1536]
    w_in_p[:, 1536:1552] = w_in[:, 1536:1552]
    w_in_p[:, 1568:1584] = w_in[:, 1552:1568]
    w_in_p[:, 1600:] = w_in[:, 1568:]
    for pj, (c0, n) in enumerate(IN_PANELS):
        out[P_IN + pj, :, :8 * n] = kmajor(w_in_p[:, c0:c0 + n])
    wo = np.asarray(inputs["w_out"], f32)[0]
    for j in range(4):
        out[P_OUT + j, :, :2048] = kmajor(wo[:, 256 * j:256 * j + 256])
    return out


def _small_inputs(inputs):
    f32 = np.float32

    def vec(v):
        return np.ascontiguousarray(np.asarray(v, f32).reshape(KC, 128).T)
    gains = np.concatenate([vec(inputs["ffn1_norm"][0]), vec(inputs["mix_norm"][0]),
                            vec(inputs["ffn2_norm"][0]), vec(inputs["final_norm"])], axis=1)
    gw = np.zeros((64, 512), f32)
    gw[0:16, 0:256] = np.asarray(inputs["gate_fwd_w"], f32)[0]
    gw[32:48, 256:512] = np.asarray(inputs["gate_bwd_w"], f32)[0]
    gb = np.concatenate([np.asarray(inputs["gate_fwd_b"], f32)[0], np.asarray(inputs["gate_bwd_b"], f32)[0]])[None, :]
    gb = np.ascontiguousarray(np.broadcast_to(gb, (128, 512)))
    cw = np.asarray(inputs["conv_w"], f32)[0]
    vecs = np.zeros((128, 20), f32)
    for c in range(4):
        for i in range(3):
            vecs[:, c * 3 + i] = cw[i, c * 128:(c + 1) * 128]
        vecs[:, 12 + c] = np.asarray(inputs["conv_group_norm"], f32)[0, c * 128:(c + 1) * 128]
        vecs[:, 16 + c] = np.asarray(inputs["gla_head_norm"], f32)[0, c * 128:(c + 1) * 128]
    return {"gains": np.ascontiguousarray(gains), "gw": gw, "gb": gb, "vecs": vecs, "cmat": _cmat()}


def _tmajor(x):
    return np.ascontiguousarray(x.reshape(x.shape[0], KC, 128).transpose(2, 1, 0))


_NC_CACHE = {}


def kernel(**inputs):
    f32 = np.float32
    if "W" not in _NC_CACHE:
        _NC_CACHE["W"] = build_w()
        _NC_CACHE["A"] = build_nc(mode="A")
        _NC_CACHE["B"] = build_nc(mode="B")
    cores = list(range(NCORES))
    pan32 = _panelize(inputs)
    resW = run_bass_kernel_spmd(_NC_CACHE["W"], [{"wsrc": np.ascontiguousarray(pan32[i * NPC:(i + 1) * NPC])}
                                                 for i in cores], core_ids=cores)
    wpan = np.ascontiguousarray(np.concatenate([np.asarray(resW.results[i]["wdst"]) for i in cores], axis=0))
    small = _small_inputs(inputs)
    xp = np.asarray(inputs["x_prompt"], f32)[0]
    xs = np.asarray(inputs["x_sample"], f32)
    mapsA = []
    for i in cores:
        m = dict(small)
        m["wpan"] = wpan
        m["xT"] = _tmajor(xp[i * L:(i + 1) * L])
        mapsA.append(m)
    resA = run_bass_kernel_spmd(_NC_CACHE["A"], mapsA, core_ids=cores)
    xall = np.ascontiguousarray(np.concatenate([np.asarray(resA.results[i]["xout"]) for i in cores], axis=0))
    mapsB = []
    for i in cores:
        m = dict(small)
        m["wpan"] = wpan
        m["xT"] = _tmajor(xs[i])
        m["xall"] = xall
        m["x1p"] = np.asarray(resA.results[i]["x1p"])
        for nm, pp, n in SEGBUFS:
            m[nm] = np.asarray(resA.results[i][nm])
        rm = np.zeros((128, 32), f32)
        for cp in range(NCORES):
            rm[:, cp] = 1.0 if cp < i else 0.0
            rm[:, 8 + cp] = 1.0 if cp > i else 0.0
            rm[:, 16 + cp] = 1.0 if cp == i - 1 else 0.0
            rm[:, 24 + cp] = 1.0 if cp == i + 1 else 0.0
        m["rmask"] = rm
        mapsB.append(m)
    res = run_bass_kernel_spmd(_NC_CACHE["B"], mapsB, core_ids=cores)
    yp = np.empty((1, NCORES * L, D), np.float32)
    ys = np.empty((NCORES, L, D), np.float32)
    for i in range(NCORES):
        y = np.asarray(res.results[i]["yT"])
        yc = y.transpose(2, 1, 0).reshape(TOK, D)
        ys[i] = yc[:L]
        yp[0, i * L:(i + 1) * L] = yc[L:]
    return (yp, ys)
```

```python
import os
import numpy as np
from contextlib import ExitStack
import concourse.bass as bass
import concourse.mybir as mybir
from concourse.bass_utils import run_bass_kernel_spmd

F32 = mybir.dt.float32
BF16 = mybir.dt.bfloat16
ALU = mybir.AluOpType
AF = mybir.ActivationFunctionType

NCORES = 8
D = 1024
DFF = 2816
KC = 8
FC = 22
TOK = 4096
T = 512
L = 2048
NSEG = 2
TPS = L // T
EPS = 1e-6
NSLOT = 4
SLOT = 2816
NDS = 40

N_IN_PAD = 3136
IN_PANELS = [(0, 256), (256, 256), (512, 256), (768, 256), (1024, 256), (1280, 256), (1536, 64),
             (1600, 256), (1856, 256), (2112, 256), (2368, 256), (2624, 256), (2880, 256)]
NCM = 770
P_G1, P_U1, P_D1, P_IN, P_OUT, P_G2, P_U2, P_D2 = 0, 11, 22, 30, 43, 47, 58, 69
NPAN = 80
NPC = NPAN // NCORES


class TR:
    def __init__(self, nc, stack):
        self.nc = nc
        self.eng = {"pe": nc.tensor, "act": nc.scalar, "dve": nc.vector,
                    "pool": nc.gpsimd, "sp": nc.sync}
        self.q = {k: [] for k in self.eng}
        self.tsem = {k: stack.enter_context(nc.semaphore("t_" + k))
                     for k in ("pe", "act", "dve", "pool")}
        self.tcnt = {k: 0 for k in self.tsem}
        self.dsem = [stack.enter_context(nc.semaphore("d%d" % i)) for i in range(NDS)]
        self.dcnt = [0] * NDS
        self.drr = 0
        self.seen = {k: {} for k in self.eng}
        self.lastw = {}
        self.readers = {}
        self.out_tickets = []
        self.defer = None

    def record(self, stage_fn):
        assert self.defer is None
        self.defer = []
        try:
            stage_fn()
            return self.defer
        finally:
            self.defer = None

    def emit_merged(self, *lists):
        lists = [l for l in lists if l]
        pos = [0] * len(lists)
        total = sum(len(l) for l in lists)
        for _ in range(total):
            best, bi = None, -1
            for i, l in enumerate(lists):
                if pos[i] < len(l):
                    frac = pos[i] / len(l)
                    if best is None or frac < best:
                        best, bi = frac, i
            e, fn, reads, writes = lists[bi][pos[bi]]
            pos[bi] += 1
            self.op(e, fn, reads, writes)

    def _sem(self, k):
        return self.tsem[k] if isinstance(k, str) else self.dsem[k[1]]

    def _deps(self, e, reads, writes):
        need = {}

        def add(t):
            if t is None:
                return
            k, v = t
            if need.get(k, 0) < v:
                need[k] = v
        for r in reads:
            add(self.lastw.get(r))
        for w in writes:
            add(self.lastw.get(w))
            for k, v in self.readers.get(w, {}).items():
                add((k, v))
        out = []
        for k, v in need.items():
            if k == e and e == "pe":
                continue
            if self.seen[e].get(k, 0) >= v:
                continue
            self.seen[e][k] = v
            out.append((k, v))
        return out

    def _commit(self, ticket, reads, writes):
        for w in writes:
            self.lastw[w] = ticket
            self.readers[w] = {}
        k, v = ticket
        for r in reads:
            d = self.readers.setdefault(r, {})
            if d.get(k, 0) < v:
                d[k] = v

    def op(self, e, fn, reads=(), writes=()):
        reads = list(reads)
        writes = list(writes)
        if self.defer is not None:
            self.defer.append((e, fn, reads, writes))
            return None
        waits = self._deps(e, reads, writes)
        self.tcnt[e] += 1
        ticket = (e, self.tcnt[e])
        sem = self.tsem[e]

        def run(eng):
            for k, v in waits:
                eng.wait_ge(self._sem(k), v)
            ins = fn(eng)
            ins.then_inc(sem, 1)
        self.q[e].append(run)
        self._commit(ticket, reads, writes)
        return ticket

    def dma(self, e, out, in_, reads=(), writes=(), is_output=False, fn=None):
        reads = list(reads)
        writes = list(writes)
        i = self.drr
        self.drr = (self.drr + 1) % NDS
        waits = self._deps(e, reads, writes)
        prev = self.dcnt[i]
        if prev > 0 and self.seen[e].get(("d", i), 0) < prev:
            waits.append((("d", i), prev))
            self.seen[e][("d", i)] = prev
        self.dcnt[i] += 16
        ticket = (("d", i), self.dcnt[i])
        sem = self.dsem[i]

        def run(eng):
            for k, v in waits:
                eng.wait_ge(self._sem(k), v)
            if fn is None:
                eng.dma_start(out=out, in_=in_).then_inc(sem, 16)
            else:
                fn(eng).then_inc(sem, 16)
        self.q[e].append(run)
        self._commit(ticket, reads, writes)
        if is_output:
            self.out_tickets.append(ticket)
        return ticket

    def barrier(self):
        snap = {k: v for k, v in self.tcnt.items() if v > 0}
        for e in ("pe", "act", "dve", "pool"):
            waits = []
            for k, v in snap.items():
                if k == e and e == "pe":
                    continue
                if self.seen[e].get(k, 0) >= v:
                    continue
                self.seen[e][k] = v
                waits.append((k, v))

            def run(eng, waits=waits):
                for k, v in waits:
                    eng.wait_ge(self._sem(k), v)
            self.q[e].append(run)

    def finish(self):
        need = {}
        for k, v in self.out_tickets:
            if need.get(k, 0) < v:
                need[k] = v
        waits = list(need.items())

        def run(eng):
            for k, v in waits:
                eng.wait_ge(self._sem(k), v)
        self.q["sp"].append(run)

    def emit(self):
        nc = self.nc
        with nc.Block() as block:
            @block.tensor
            def _(eng):
                for f in self.q["pe"]:
                    f(eng)

            @block.scalar
            def _(eng):
                for f in self.q["act"]:
                    f(eng)

            @block.vector
            def _(eng):
                for f in self.q["dve"]:
                    f(eng)

            @block.gpsimd
            def _(eng):
                for f in self.q["pool"]:
                    f(eng)

            @block.sync
            def _(eng):
                for f in self.q["sp"]:
                    f(eng)


def build_w():
    nc = bass.Bass("TRN2", target_bir_lowering=False)
    src = nc.dram_tensor("wsrc", [NPC, 128, SLOT], F32, kind="ExternalInput").ap()
    dst = nc.dram_tensor("wdst", [NPC, 128, SLOT], BF16, kind="ExternalOutput").ap()
    stack = ExitStack()
    with stack:
        tr = TR(nc, stack)
        sin = [stack.enter_context(nc.sbuf_tensor("sin%d" % i, [128, SLOT], F32)) for i in range(3)]
        sout = [stack.enter_context(nc.sbuf_tensor("sout%d" % i, [128, SLOT], BF16)) for i in range(3)]
        engs = ["dve", "act", "pool"]
        for j in range(NPC):
            b = j % 3
            tr.dma("sp", sin[b][:, :], src[j], writes=[("sin", b)])
            if engs[b] == "act":
                tr.op("act", lambda e, b=b: e.copy(sout[b][:, :], sin[b][:, :]), reads=[("sin", b)], writes=[("sout", b)])
            else:
                tr.op(engs[b], lambda e, b=b: e.tensor_copy(sout[b][:, :], sin[b][:, :]), reads=[("sin", b)], writes=[("sout", b)])
            tr.dma("sp", dst[j], sout[b][:, :], reads=[("sout", b)], writes=[("dst", j)], is_output=True)
        tr.finish()
        tr.emit()
    return nc


def bcast_mid(ap_, n):
    from concourse.ap import AP
    a = ap_.ap
    return AP(ap_.tensor, ap_.offset, [list(a[0]), [0, n]] + [list(x) for x in a[1:]])


SEGBUFS = [("sQ", 128, 2 * L), ("sK", 128, 2 * L), ("sKtok", 128, 16 * 256), ("sVtok", 128, 16 * 512),
           ("sR", 64, L), ("sM", 128, 8 * L), ("sZ", 128, 4 * (L + 2))]


def build_nc(stage=None, mode="B"):
    stage = int(os.environ.get('KSTAGE', '99')) if stage is None else stage
    nc = bass.Bass("TRN2", target_bir_lowering=False)
    dt = nc.dram_tensor
    xT = dt("xT", [128, KC, L], F32, kind="ExternalInput").ap()
    wpan = dt("wpan", [NPAN, 128, SLOT], BF16, kind="ExternalInput").ap()
    gains = dt("gains", [128, 4 * KC], F32, kind="ExternalInput").ap()
    gw_d = dt("gw", [64, 512], F32, kind="ExternalInput").ap()
    gb_d = dt("gb", [128, 512], F32, kind="ExternalInput").ap()
    vecs_d = dt("vecs", [128, 20], F32, kind="ExternalInput").ap()
    cmat_d = dt("cmat", [128, NCM], F32, kind="ExternalInput").ap()
    segd = {}
    if mode == "A":
        xout = dt("xout", [128, 528], F32, kind="ExternalOutput").ap()
        x1p = dt("x1p", [128, KC, L], F32, kind="ExternalOutput").ap()
        for nm, pp, n in SEGBUFS:
            segd[nm] = dt(nm, [pp, n], BF16, kind="ExternalOutput").ap()
    else:
        yT = dt("yT", [128, KC, TOK], F32, kind="ExternalOutput").ap()
        xall = dt("xall", [NCORES * 128, 528], F32, kind="ExternalInput").ap()
        x1p = dt("x1p", [128, KC, L], F32, kind="ExternalInput").ap()
        for nm, pp, n in SEGBUFS:
            segd[nm] = dt(nm, [pp, n], BF16, kind="ExternalInput").ap()
        x1s = dt("x1s", [128, KC, L], F32).ap()
    rmask_d = dt("rmask", [128, 32], F32, kind="ExternalInput").ap() if mode == "B" else None
    KDBG = int(os.environ.get('KDBG', '0'))
    if KDBG:
        dbg = dt("dbg", [128, 8 * L], BF16, kind="ExternalOutput").ap()

    stack = ExitStack()
    with stack:
        tr = TR(nc, stack)

        def sb(name, n, dtype, parts=128):
            return stack.enter_context(nc.sbuf_tensor(name, [parts, n], dtype))

        xt_t = sb("xt", KC * T, F32)
        xn_t = sb("xn", KC * T, BF16)
        h_t = sb("h", FC * T, BF16)
        ring_t = sb("ring", NSLOT * SLOT, BF16)
        sq_t = sb("sq", KC * T, BF16)
        rstd_t = sb("rstd", T, F32)
        sg_t = sb("sgt", 4 * T, F32)
        gains_t = sb("gains_sb", 4 * KC, F32)
        ones_t = sb("ones", 128, BF16)
        ones128_t = sb("ones128", 128, BF16)
        g64_t = sb("g64", 128, BF16)
        vecs_t = sb("vecs_sb", 20, F32)
        gwst_t = sb("gwst", 512, F32, parts=64)
        gw_t = sb("gw_sb", 512, BF16, parts=64)
        gb_t = sb("gb_sb", 512, F32)
        onesrow_t = sb("onesrow", 128, BF16, parts=1)
        cmb_t = sb("cmat_bf", NCM, BF16)
        sph_t = sb("sp_hi", 512, BF16)
        spl_t = sb("sp_lo", 512, BF16)
        qT_t = sb("qT", 2 * L, BF16)
        kT_t = sb("kT", 2 * L, BF16)
        ktok_t = sb("ktok", 16 * 256, BF16)
        vtok_t = sb("vtok", 16 * 512, BF16)
        rT_t = sb("rT", L, BF16, parts=64)
        m_t = sb("m", 8 * L, BF16)
        z_t = sb("z", 4 * (L + 2), BF16)
        srun_t = sb("srun", 2 * 2 * 128, F32)
        rs_t = sb("rs", T, F32)
        tt_t = sb("tt", T, F32)
        cv_t = sb("cv", 2 * T, F32)
        dec_t = sb("dec", 16, F32)
        ctmp_t = sb("ctmp", T, F32)
        rmask_t = sb("rmask_sb", 32, F32)
        dacc_t = sb("dacc", 4, F32)
        xst_t = sb("xst", 528, F32)
        xg_t = sb("xg", 2 * 528, F32)
        hal_t = sb("hal", 8, F32)
        dp_t = sb("dp", 4, F32)

        ps = [stack.enter_context(nc.psum_tensor("ps%d" % i, [128, 512], F32)) for i in range(8)]
        ps_pool = {"all": [0, 1, 2, 3, 4, 5], "G": [0, 1], "H": [2, 3, 4, 5], "H3": [2, 3, 4], "C": [5],
                   "G0": [0], "G1": [1], "H0": [2, 3], "H1": [4, 5]}
        ps_rr = {k: 0 for k in ps_pool}
        ps_cur = ["all"]

        def ps_next():
            pool = ps_cur[0]
            lst = ps_pool[pool]
            i = ps_rr[pool]
            ps_rr[pool] = (i + 1) % len(lst)
            return lst[i]

        def staged(pool, fn, *a):
            def run():
                ps_cur[0] = pool
                try:
                    fn(*a)
                finally:
                    ps_cur[0] = "all"
            return tr.record(run)

        xt = xt_t[:, :].rearrange("p (c n) -> p c n", n=T)
        xn = xn_t[:, :].rearrange("p (c n) -> p c n", n=T)
        hh = h_t[:, :].rearrange("p (c n) -> p c n", n=T)
        sq = sq_t[:, :].rearrange("p (c n) -> p c n", n=T)
        qT3 = qT_t[:, :].rearrange("p (c n) -> p c n", n=L)
        kT3 = kT_t[:, :].rearrange("p (c n) -> p c n", n=L)
        ktok3 = ktok_t[:, :].rearrange("p (b n) -> p b n", n=256)
        vtok3 = vtok_t[:, :].rearrange("p (b n) -> p b n", n=512)
        m3 = m_t[:, :].rearrange("p (c n) -> p c n", n=L)
        z3 = z_t[:, :].rearrange("p (c n) -> p c n", n=L + 2)
        sbsave = h_t[:, 0:32 * 256].rearrange("p (n q v) -> p n q v", q=2, v=128)
        srun = srun_t[:, :].rearrange("p (d q v) -> p d q v", q=2, v=128)
        XT_ALL = [("xt", c) for c in range(KC)]
        XN_ALL = [("xn", c) for c in range(KC)]
        SQ_ALL = [("sq", c) for c in range(KC)]
        H_ALL = [("h", k) for k in range(FC)]
        SGT_ALL = [("sgt", i) for i in range(4)]

        def cm(i):
            return cmb_t[:, i * 128:(i + 1) * 128]
        M_CF, M_CB, M_TB, M_TF, MASK_F, MASK_B = (cm(i) for i in range(6))
        CI = cmb_t[:, 768:770]

        tr.op("pool", lambda e: e.memset(ones_t[:, :], 1.0 / D), writes=["ones"])
        tr.op("pool", lambda e: e.memset(ones128_t[:, :], 1.0 / 128), writes=["ones128"])
        tr.op("pool", lambda e: e.memset(g64_t[:, :], 0.0), writes=["g64"])
        tr.op("pool", lambda e: e.memset(g64_t[0:64, 0:64], 1.0 / 64), reads=["g64"], writes=["g64"])
        tr.op("pool", lambda e: e.memset(g64_t[64:128, 64:128], 1.0 / 64), reads=["g64"], writes=["g64"])
        tr.op("pool", lambda e: e.memset(onesrow_t[:, :], 1.0), writes=["onesrow"])
        tr.dma("sp", gains_t[:, :], gains, writes=["gains"])
        cmat_t = xg_t[:, 0:NCM]
        tr.dma("sp", cmat_t, cmat_d, writes=[("xg", 0), ("xg", 1)])
        tr.dma("sp", vecs_t[:, :], vecs_d, writes=["vecs"])
        tr.dma("sp", gwst_t[:, :], gw_d, writes=["gwst"])
        tr.dma("sp", gb_t[:, :], gb_d, writes=["gb"])
        if mode == "B":
            tr.dma("sp", rmask_t[:, :], rmask_d, writes=["rmask"])
        tr.op("dve", lambda e: e.tensor_copy(gw_t[:, :], gwst_t[:, :]), reads=["gwst"], writes=["gw"])
        tr.op("dve", lambda e: e.tensor_copy(cmb_t[:, :], cmat_t), reads=[("xg", 0), ("xg", 1)], writes=["cmat"])


        KFAST = int(os.environ.get('KFAST', '0'))
        def PW(idx, n):
            return (wpan[idx][:, 0:n], n)
        panels = []

        def add_ffn(f):
            pg, pu, pd = (P_G1, P_U1, P_D1) if f == 1 else (P_G2, P_U2, P_D2)
            for j in range(11):
                panels.append(PW(pg + j, 2048))
                panels.append(PW(pu + j, 2048))
            for m in range(8):
                panels.append(PW(pd + m, 2816))

        def add_phase1():
            for ti in range(TPS):
                add_ffn(1)
                for pj, (c0, nc_) in enumerate(IN_PANELS):
                    panels.append(PW(P_IN + pj, 8 * nc_))

        def add_phase3():
            for ti in range(TPS):
                for j in range(4):
                    panels.append(PW(P_OUT + j, 2048))
                add_ffn(2)
        if mode == "A":
            add_phase1()
        else:
            add_phase1()
            add_phase3()
            add_phase3()
        ring_issued = [0]
        pi = [0]

        def ring_advance(n_done):
            lim = min(len(panels), n_done + NSLOT)
            while ring_issued[0] < lim:
                i = ring_issued[0]
                ap, n = panels[i]
                s = i % NSLOT
                tr.dma("sp", ring_t[:, s * SLOT:s * SLOT + n], ap, reads=[], writes=[("ring", s)])
                ring_issued[0] += 1

        def ring_get():
            i = pi[0]
            assert i < ring_issued[0], (i, ring_issued[0])
            s = i % NSLOT
            n = panels[i][1]
            pi[0] += 1
            return ring_t[:, s * SLOT:s * SLOT + n], ("ring", s)

        def act_rsqrt(out_ap, in_ap, rd, wr):
            tr.op("act", lambda e: e.activation(out_ap, in_ap, AF.Ln, bias=EPS), reads=rd, writes=wr)
            tr.op("act", lambda e: e.activation(out_ap, out_ap, AF.Exp, scale=-0.5), reads=wr, writes=wr)

        def rmsnorm_stats():
            tr.op("act", lambda e: e.activation(sq_t[:, :], xt_t[:, :], AF.Square),
                  reads=XT_ALL, writes=SQ_ALL)
            b = ps_next()

            def mm(e, b=b):
                ins = None
                for c in range(KC):
                    ins = e.matmul(ps[b][:, :], ones_t[:, :], sq[:, c, :], start=(c == 0), stop=(c == KC - 1))
                return ins
            tr.op("pe", mm, reads=SQ_ALL + ["ones"], writes=[("ps", b)])
            act_rsqrt(rstd_t[:, :], ps[b][:, :], [("ps", b)], ["rstd"])

        def stats_sq(m):
            tr.op("act", lambda e, m=m: e.activation(sq[:, m, :], xt[:, m, :], AF.Square),
                  reads=[("xt", m)], writes=[("sq", m)])

        def stats_mm(m):
            tr.op("pe", lambda e, m=m: e.matmul(ps[7][:, :], ones_t[:, :], sq[:, m, :], start=(m == 0), stop=(m == KC - 1)),
                  reads=[("sq", m), "ones"], writes=[("ps", 7)])

        def stats_chunk(m):
            stats_sq(m)
            if m > 0:
                stats_mm(m - 1)
            if m == KC - 1:
                stats_mm(m)

        def stats_finish():
            act_rsqrt(rstd_t[:, :], ps[7][:, :], [("ps", 7)], ["rstd"])

        def rmsnorm_apply(gi, out3, out_keys):
            for c in range(KC):
                tr.op("dve", lambda e, c=c: e.scalar_tensor_tensor(
                    out3[:, c, :], xt[:, c, :], gains_t[:, gi * KC + c:gi * KC + c + 1], rstd_t[:, :],
                    ALU.mult, ALU.mult),
                    reads=[("xt", c), "rstd", "gains"], writes=[out_keys[c]])

        def mm_gu(e, w3, b, jj):
            ins = None
            for c in range(KC):
                ins = e.matmul(ps[b][:, :], w3[:, c, jj * 128:(jj + 1) * 128], xn[:, c, :],
                               start=(c == 0), stop=(c == KC - 1))
            return ins

        def mm_down(e, w3, b):
            ins = None
            for k in range(FC):
                ins = e.matmul(ps[b][:, :], w3[:, k, :], hh[:, k, :], start=(k == 0), stop=(k == FC - 1))
            return ins

        def sigmoid_chain(o, psb, bkey, okey):
            tr.op("act", lambda e: e.activation(o, psb, AF.Exp, scale=-1.0), reads=[bkey], writes=[okey])
            tr.op("act", lambda e: e.activation(o, o, AF.Ln, bias=1.0), reads=[okey], writes=[okey])
            tr.op("act", lambda e: e.activation(o, o, AF.Exp, scale=-1.0), reads=[okey], writes=[okey])

        def ffn():
            if KFAST:
                for _ in range(30):
                    ring_get()
                    ring_advance(pi[0])
                return
            for j in range(11):
                wg, kg = ring_get()
                wu, ku = ring_get()
                wg3 = wg.rearrange("p (c n) -> p c n", n=256)
                wu3 = wu.rearrange("p (c n) -> p c n", n=256)
                banks = [(ps_next(), ps_next()) for _ in range(2)]
                if j == 0:
                    for c in range(KC):
                        for jj in range(2):
                            for w3_, b_, k_ in ((wg3, banks[jj][0], kg), (wu3, banks[jj][1], ku)):
                                tr.op("pe", lambda e, w3_=w3_, b_=b_, c=c, jj=jj: e.matmul(
                                    ps[b_][:, :], w3_[:, c, jj * 128:(jj + 1) * 128], xn[:, c, :],
                                    start=(c == 0), stop=(c == KC - 1)),
                                    reads=[("xn", c), k_], writes=[("ps", b_)])
                for jj in range(2):
                    fc = 2 * j + jj
                    bg, bu = banks[jj]
                    if j > 0:
                        tr.op("pe", lambda e, w3=wg3, b=bg, jj=jj: mm_gu(e, w3, b, jj), reads=XN_ALL + [kg], writes=[("ps", bg)])
                        tr.op("pe", lambda e, w3=wu3, b=bu, jj=jj: mm_gu(e, w3, b, jj), reads=XN_ALL + [ku], writes=[("ps", bu)])
                    sgv = sg_t[:, (fc % 2) * T:(fc % 2 + 1) * T]
                    sgw = sg_t[:, (2 + fc % 2) * T:(3 + fc % 2) * T]
                    ka = ("sgt", fc % 2)
                    kb = ("sgt", 2 + fc % 2)
                    sigmoid_chain(sgv, ps[bg][:, :], ("ps", bg), ka)
                    tr.op("dve", lambda e, b=bg, i=sgv, o=sgw: e.tensor_tensor(o, ps[b][:, :], i, ALU.mult),
                          reads=[("ps", bg), ka], writes=[kb])
                    tr.op("dve", lambda e, b=bu, i=sgw, fc=fc: e.tensor_tensor(hh[:, fc, :], ps[b][:, :], i, ALU.mult),
                          reads=[("ps", bu), kb], writes=[("h", fc)])
                ring_advance(pi[0])
            for m in range(8):
                wd, kd = ring_get()
                wd3 = wd.rearrange("p (k n) -> p k n", n=128)
                b = ps_next()
                if m == 0:
                    for lo, hi in ((0, 16), (16, 20), (20, 22)):
                        def mmd(e, w3=wd3, b=b, lo=lo, hi=hi):
                            ins = None
                            for k in range(lo, hi):
                                ins = e.matmul(ps[b][:, :], w3[:, k, :], hh[:, k, :], start=(k == 0), stop=(k == FC - 1))
                            return ins
                        tr.op("pe", mmd, reads=[("h", k) for k in range(lo, hi)] + [kd], writes=[("ps", b)])
                else:
                    tr.op("pe", lambda e, w3=wd3, b=b: mm_down(e, w3, b), reads=H_ALL + [kd], writes=[("ps", b)])
                tr.op("dve", lambda e, b=b, m=m: e.scalar_tensor_tensor(
                    xt[:, m, :], ps[b][:, :], 0.5, xt[:, m, :], ALU.mult, ALU.add),
                    reads=[("ps", b), ("xt", m)], writes=[("xt", m)])
                stats_chunk(m)
                ring_advance(pi[0])
            stats_finish()

        def mm_fm(e, w3, c0, M, b):
            ins = None
            for c in range(KC):
                ins = e.matmul(ps[b][0:M, :], w3[:, c, c0:c0 + M], xn[:, c, :], start=(c == 0), stop=(c == KC - 1))
            return ins

        def mm_tm(e, w3, blk, N, b):
            ins = None
            for c in range(KC):
                ins = e.matmul(ps[b][:, 0:N], xn[:, c, blk * 128:(blk + 1) * 128], w3[:, c, 0:N],
                               start=(c == 0), stop=(c == KC - 1))
            return ins

        evac_rr = [0]

        def evac(out_ap, in_ap, rd, wr, scale=None):
            e = "act" if evac_rr[0] % 2 == 0 else "dve"
            evac_rr[0] += 1
            if e == "act":
                if scale is None:
                    tr.op("act", lambda g: g.copy(out_ap, in_ap), reads=rd, writes=wr)
                else:
                    tr.op("act", lambda g: g.mul(out_ap, in_ap, scale), reads=rd, writes=wr)
            else:
                if scale is None:
                    tr.op("dve", lambda g: g.tensor_copy(out_ap, in_ap), reads=rd, writes=wr)
                else:
                    tr.op("dve", lambda g: g.tensor_scalar_mul(out_ap, in_ap, scale), reads=rd, writes=wr)

        def projection(ti):
            tl = ti * T
            wq, kwq = ring_get()
            wk, kwk = ring_get()
            wq3 = wq.rearrange("p (c n) -> p c n", n=256)
            wk3 = wk.rearrange("p (c n) -> p c n", n=256)
            qb = [ps_next(), ps_next()]
            kb_ = [ps_next(), ps_next()]
            for c in range(KC):
                for c2 in range(2):
                    tr.op("pe", lambda e, c2=c2, c=c: e.matmul(
                        ps[qb[c2]][:, :], wq3[:, c, c2 * 128:(c2 + 1) * 128], xn[:, c, :], start=(c == 0), stop=(c == KC - 1)),
                        reads=[("xn", c), kwq], writes=[("ps", qb[c2])])
                for c2 in range(2):
                    tr.op("pe", lambda e, c2=c2, c=c: e.matmul(
                        ps[kb_[c2]][:, :], wk3[:, c, c2 * 128:(c2 + 1) * 128], xn[:, c, :], start=(c == 0), stop=(c == KC - 1)),
                        reads=[("xn", c), kwk], writes=[("ps", kb_[c2])])
            for c2 in range(2):
                evac(qT3[:, c2, tl:tl + T], ps[qb[c2]][:, :], [("ps", qb[c2])], [("qT", c2, ti)], scale=0.125)
            for c2 in range(2):
                evac(kT3[:, c2, tl:tl + T], ps[kb_[c2]][:, :], [("ps", kb_[c2])], [("kT", c2, ti)])
            w3, kw = wk3, kwk
            for bl in range(4):
                blk = ti * 4 + bl
                b = ps_next()
                tr.op("pe", lambda e, w3=w3, bl=bl, b=b: mm_tm(e, w3, bl, 256, b), reads=XN_ALL + [kw], writes=[("ps", b)])
                evac(ktok3[:, blk, :], ps[b][:, 0:256], [("ps", b)], [("ktok", blk)])
            ring_advance(pi[0])
            for hv in range(2):
                w, kw = ring_get()
                w3 = w.rearrange("p (c n) -> p c n", n=256)
                for bl in range(4):
                    blk = ti * 4 + bl
                    b = ps_next()
                    tr.op("pe", lambda e, w3=w3, bl=bl, b=b: mm_tm(e, w3, bl, 256, b), reads=XN_ALL + [kw], writes=[("ps", b)])
                    evac(vtok3[:, blk, hv * 256:(hv + 1) * 256], ps[b][:, 0:256], [("ps", b)], [("vtok", blk, hv)])
                ring_advance(pi[0])
            for hg in range(2):
                w, kw = ring_get()
                w3 = w.rearrange("p (c n) -> p c n", n=256)
                for c2 in range(2):
                    c = hg * 2 + c2
                    b = ps_next()
                    tr.op("pe", lambda e, w3=w3, c2=c2, b=b: mm_fm(e, w3, c2 * 128, 128, b), reads=XN_ALL + [kw], writes=[("ps", b)])
                    sgv = sg_t[:, (c % 2) * T:(c % 2 + 1) * T]
                    ka = ("sgt", c % 2)
                    sigmoid_chain(sgv, ps[b][:, :], ("ps", b), ka)
                    tr.op("dve", lambda e, b=b, c=c, i=sgv: e.scalar_tensor_tensor(
                        m3[:, c, tl:tl + T], ps[b][:, :], vecs_t[:, 16 + c:17 + c], i, ALU.mult, ALU.mult),
                        reads=[("ps", b), ka, "vecs"], writes=[("m", c, ti)])
                ring_advance(pi[0])
            w, kw = ring_get()
            w3 = w.rearrange("p (c n) -> p c n", n=64)
            b = ps_next()
            tr.op("pe", lambda e, w3=w3, b=b: mm_fm(e, w3, 0, 64, b), reads=XN_ALL + [kw], writes=[("ps", b)])
            evac(rT_t[:, tl:tl + T], ps[b][0:64, :], [("ps", b)], [("rT", ti)])
            ring_advance(pi[0])
            for hb in range(2):
                w, kw = ring_get()
                w3 = w.rearrange("p (c n) -> p c n", n=256)
                for c2 in range(2):
                    c = hb * 2 + c2
                    b = ps_next()
                    tr.op("pe", lambda e, w3=w3, c2=c2, b=b: mm_fm(e, w3, c2 * 128, 128, b), reads=XN_ALL + [kw], writes=[("ps", b)])
                    evac(m3[:, 4 + c, tl:tl + T], ps[b][:, :], [("ps", b)], [("m", 4 + c, ti)])
                ring_advance(pi[0])
            cb = []
            for hc in range(2):
                w, kw = ring_get()
                w3 = w.rearrange("p (c n) -> p c n", n=256)
                for c2 in range(2):
                    b = ps_next()
                    cb.append(b)
                    tr.op("pe", lambda e, w3=w3, c2=c2, b=b: mm_fm(e, w3, c2 * 128, 128, b), reads=XN_ALL + [kw], writes=[("ps", b)])
                ring_advance(pi[0])
            cst = [cv_t[:, 0:T], cv_t[:, T:2 * T], ctmp_t[:, :], tt_t[:, :]]
            ckey = [("cv", 0), ("cv", 1), "ctmp", "tt"]
            for c in range(4):
                evac(cst[c], ps[cb[c]][:, :], [("ps", cb[c])], [ckey[c]])
            for hc in range(2):
                w, kw = ring_get()
                w3 = w.rearrange("p (c n) -> p c n", n=256)
                for c2 in range(2):
                    c = hc * 2 + c2
                    b = ps_next()
                    tr.op("pe", lambda e, w3=w3, c2=c2, b=b: mm_fm(e, w3, c2 * 128, 128, b), reads=XN_ALL + [kw], writes=[("ps", b)])
                    tr.op("dve", lambda e, b=b, c=c: e.tensor_tensor(
                        z3[:, c, 1 + tl:1 + tl + T], ps[b][:, :], cst[c], ALU.mult),
                        reads=[("ps", b), ckey[c]], writes=[("z", c, ti)])
                ring_advance(pi[0])

        def f32tmp(i):
            return sg_t[:, i * 256:(i + 1) * 256]
        SP = [f32tmp(0), f32tmp(1)]
        ET = f32tmp(2)
        ETD = [f32tmp(2), f32tmp(7)]
        SPH = [sph_t[:, 0:256], sph_t[:, 256:512]]
        SPL = [spl_t[:, 0:256], spl_t[:, 256:512]]
        EQ = [f32tmp(3), f32tmp(4)]
        EI = [f32tmp(5), f32tmp(6)]

        def b16tmp(i):
            return xn_t[:, i * 256:(i + 1) * 256]
        KTB = [[b16tmp(0), b16tmp(1)], [b16tmp(2), b16tmp(3)]]
        QDB = [[xn_t[:, 1024 + par * 512 + d * 256:1024 + par * 512 + (d + 1) * 256] for par in range(2)] for d in range(2)]
        KIB = [[xn_t[:, 2048 + par * 512 + d * 256:2048 + par * 512 + (d + 1) * 256] for par in range(2)] for d in range(2)]
        PM = [xn_t[:, 3072 + i * 128:3072 + (i + 1) * 128] for i in range(4)]
        SFB = [xn_t[:, 3584 + i * 256:3584 + (i + 1) * 256].rearrange("p (q v) -> p q v", v=128) for i in range(2)]
        DECB = [[0, 8], [4, 12]]

        def fence(keys):
            tr.barrier()

        def gate_sp(d, blk):
            b = ps_next()
            r0 = 0 if d == 0 else 32

            tr.op("pe", lambda e, b=b: e.matmul(ps[b][:, 0:256], rT_t[r0:r0 + 16, blk * 128:(blk + 1) * 128],
                                                gw_t[r0:r0 + 16, d * 256:(d + 1) * 256], start=True, stop=True),
                  reads=[("rT", blk // 4), "gw"], writes=[("ps", b)])
            tr.op("dve", lambda e, b=b: e.tensor_tensor(SP[d], ps[b][:, 0:256], gb_t[:, d * 256:(d + 1) * 256], ALU.add),
                  reads=[("ps", b), "gb"], writes=[("sp", d)])
            tr.op("act", lambda e: e.activation(SP[d], SP[d], AF.Exp, scale=-1.0), reads=[("sp", d)], writes=[("sp", d)])
            tr.op("act", lambda e: e.activation(SP[d], SP[d], AF.Ln, bias=1.0), reads=[("sp", d)], writes=[("sp", d)])
            tr.op("act", lambda e: e.copy(SPH[d], SP[d]), reads=[("sp", d)], writes=[("sph", d)])
            tr.op("dve", lambda e: e.tensor_tensor(SPL[d], SP[d], SPH[d], ALU.subtract),
                  reads=[("sp", d), ("sph", d)], writes=[("spl", d)])

        def gate_tail(d, blk, par):
            b = ps_next()
            Mt = M_TF if d == 0 else M_TB

            def mm(e, b=b):
                e.matmul(ps[b][:, 0:256], Mt, SPH[d], start=True, stop=False)
                return e.matmul(ps[b][:, 0:256], Mt, SPL[d], start=False, stop=True)
            tr.op("pe", mm, reads=[("sph", d), ("spl", d), "cmat"], writes=[("ps", b)])
            tr.op("act", lambda e, b=b: e.activation(ETD[d], ps[b][:, 0:256], AF.Exp), reads=[("ps", b)], writes=[("et", d)])
            tr.op("pool", lambda e: e.tensor_tensor(KTB[d][par], ktok3[:, blk, :], ETD[d], ALU.mult),
                  reads=[("et", d), ("ktok", blk)], writes=[("kt", d, par)])

        def gate_dec_ci(d, blk, par):
            b = ps_next()
            base = DECB[d][par]

            def mm(e, b=b):
                ins = None
                for q in range(2):
                    e.matmul(ps[b][:, q * 2:q * 2 + 2], SPH[d][:, q * 128:(q + 1) * 128], CI, start=True, stop=False)
                    ins = e.matmul(ps[b][:, q * 2:q * 2 + 2], SPL[d][:, q * 128:(q + 1) * 128], CI, start=False, stop=True)
                return ins
            tr.op("pe", mm, reads=[("sph", d), ("spl", d), "cmat"], writes=[("ps", b)])
            tr.op("act", lambda e, b=b: e.activation(dec_t[:, base:base + 4], ps[b][:, 0:4], AF.Exp),
                  reads=[("ps", b)], writes=[("dec", d, par)])

        def state_update(d, blk, ch, par):
            for q in range(2):
                b = ps_next()
                r0 = ch * 64
                tr.op("pe", lambda e, b=b, q=q, r0=r0: e.matmul(
                    ps[b][:, 0:256], KTB[d][par][r0:r0 + 64, q * 128:(q + 1) * 128],
                    vtok3[r0:r0 + 64, blk, q * 256:(q + 1) * 256], start=True, stop=True),
                    reads=[("kt", d, par), ("vtok", blk, q)], writes=[("ps", b)])
                for hh_ in range(2):
                    p0 = hh_ * 64
                    col = DECB[d][par] + q * 2 + ch
                    tr.op("dve", lambda e, b=b, q=q, p0=p0, col=col, hh_=hh_: e.scalar_tensor_tensor(
                        srun[p0:p0 + 64, d, q, :], srun[p0:p0 + 64, d, q, :], dec_t[p0:p0 + 64, col:col + 1],
                        ps[b][p0:p0 + 64, hh_ * 128:(hh_ + 1) * 128], ALU.mult, ALU.add),
                        reads=[("ps", b), ("dec", d, par), ("srun", d, q)], writes=[("srun", d, q)])

        SRUN_ALL = [("srun", d, q) for d in range(2) for q in range(2)]

        def local_scan_and_exchange(do_scan, do_fold):
            tr.barrier()
            tr.op("pool", lambda e: e.memset(srun_t[:, :], 0.0), writes=SRUN_ALL)
            tr.op("pool", lambda e: e.memset(dacc_t[:, :], 1.0), writes=[("dacc", 0), ("dacc", 1)])
            tr.op("pool", lambda e: e.memset(hal_t[:, :], 0.0), writes=["hal"])
            if do_scan:
                local_scan()
            if do_fold:
                fold_exchange()

        def local_scan():
            orders = {1: list(range(15, -1, -1)), 0: list(range(16))}

            def G(d, blk):
                gate_sp(d, blk)
                gate_tail(d, blk, blk % 2)
                gate_dec_ci(d, blk, blk % 2)

            def H(d, blk):
                par = blk % 2
                for ch in ((1, 0) if d == 1 else (0, 1)):
                    state_update(d, blk, ch, par)
                base = DECB[d][par]
                dv = dec_t[:, base:base + 4].rearrange("p (q c) -> p q c", c=2)
                for ch in range(2):
                    tr.op("dve", lambda e, d=d, dv=dv, ch=ch: e.tensor_tensor(
                        dacc_t[:, d * 2:d * 2 + 2], dacc_t[:, d * 2:d * 2 + 2], dv[:, :, ch], ALU.mult),
                        reads=[("dec", d, par), ("dacc", d)], writes=[("dacc", d)])
            G(1, orders[1][0])
            G(0, orders[0][0])
            for i in range(16):
                streams = []
                for d in (1, 0):
                    if i + 1 < 16:
                        streams.append(staged("G%d" % d, G, d, orders[d][i + 1]))
                    streams.append(staged("H%d" % d, H, d, orders[d][i]))
                tr.emit_merged(*streams)
            tr.op("dve", lambda e: e.tensor_copy(xst_t[:, 0:512], srun_t[:, :]), reads=SRUN_ALL, writes=["xst"])
            tr.op("dve", lambda e: e.tensor_copy(xst_t[:, 512:516], dacc_t[:, :]), reads=[("dacc", 0), ("dacc", 1), "xst"], writes=["xst"])
            tr.op("dve", lambda e: e.tensor_copy(xst_t[:, 516:520].rearrange("p (c o) -> p c o", o=1), z3[:, :, 1:2]),
                  reads=[("z", c, 0) for c in range(4)] + ["xst"], writes=["xst"])
            tr.op("dve", lambda e: e.tensor_copy(xst_t[:, 520:524].rearrange("p (c o) -> p c o", o=1), z3[:, :, L:L + 1]),
                  reads=[("z", c, TPS - 1) for c in range(4)] + ["xst"], writes=["xst"])
            tr.op("pool", lambda e: e.memset(xst_t[:, 524:528], 0.0), reads=["xst"], writes=["xst"])
            tr.dma("sp", xout, xst_t[:, :], reads=["xst"], writes=["xout"], is_output=True)

        def fold_exchange():
            step = 0
            for d in (0, 1):
                for cp in (range(NCORES) if d == 0 else range(NCORES - 1, -1, -1)):
                    sl = step % 2
                    step += 1
                    xg = xg_t[:, sl * 528:(sl + 1) * 528]
                    kx = ("xg", sl)
                    tr.dma("sp", xg, xall[cp * 128:(cp + 1) * 128, :], reads=[], writes=[kx])
                    w = rmask_t[:, d * 8 + cp:d * 8 + cp + 1]
                    for q in range(2):
                        dcol = xg[:, 512 + d * 2 + q:512 + d * 2 + q + 1]
                        dp = dp_t[:, q:q + 1]
                        tr.op("dve", lambda e, dp=dp, dcol=dcol, w=w: e.tensor_scalar(dp, dcol, -1.0, w, ALU.add, ALU.mult),
                              reads=[kx, "rmask"], writes=[("dp", q)])
                        tr.op("dve", lambda e, dp=dp: e.tensor_scalar_add(dp, dp, 1.0), reads=[("dp", q)], writes=[("dp", q)])
                        S = srun[:, d, q, :]
                        tr.op("dve", lambda e, S=S, dp=dp: e.tensor_scalar_mul(S, S, dp),
                              reads=[("dp", q), ("srun", d, q)], writes=[("srun", d, q)])
                        Lp = xg[:, d * 256 + q * 128:d * 256 + (q + 1) * 128]
                        tr.op("dve", lambda e, S=S, Lp=Lp, w=w: e.scalar_tensor_tensor(S, Lp, w, S, ALU.mult, ALU.add),
                              reads=[kx, "rmask", ("srun", d, q)], writes=[("srun", d, q)])
                    if d == 0:
                        tr.op("dve", lambda e, xg=xg, cp=cp: e.scalar_tensor_tensor(
                            hal_t[:, 0:4], xg[:, 520:524], rmask_t[:, 16 + cp:17 + cp], hal_t[:, 0:4], ALU.mult, ALU.add),
                            reads=[kx, "rmask", "hal"], writes=["hal"])
                        tr.op("dve", lambda e, xg=xg, cp=cp: e.scalar_tensor_tensor(
                            hal_t[:, 4:8], xg[:, 516:520], rmask_t[:, 24 + cp:25 + cp], hal_t[:, 4:8], ALU.mult, ALU.add),
                            reads=[kx, "rmask", "hal"], writes=["hal"])
            tr.op("dve", lambda e: e.tensor_copy(z3[:, :, 0:1], hal_t[:, 0:4].rearrange("p (c o) -> p c o", o=1)),
                  reads=["hal"], writes=[("zh", 0)])
            tr.op("dve", lambda e: e.tensor_copy(z3[:, :, L + 1:L + 2], hal_t[:, 4:8].rearrange("p (c o) -> p c o", o=1)),
                  reads=["hal"], writes=[("zh", 1)])

        def phase2(seg, zero_init=True):
            fence(H_ALL + SGT_ALL + XN_ALL)
            if zero_init:
                tr.op("pool", lambda e: e.memset(srun_t[:, :], 0.0), writes=SRUN_ALL)
            def GA(blk):
                gate_sp(1, blk)
                gate_tail(1, blk, blk % 2)
                gate_dec_ci(1, blk, blk % 2)

            def HA(blk):
                for ch in (1, 0):
                    n = blk * 2 + ch
                    tr.op("act", lambda e, n=n: e.copy(sbsave[:, n, :, :], srun[:, 1, :, :]),
                          reads=[("srun", 1, 0), ("srun", 1, 1)], writes=[("sbs", n)])
                    state_update(1, blk, ch, blk % 2)
            GA(15)
            for blk in range(15, -1, -1):
                g = staged("G", GA, blk - 1) if blk > 0 else []
                h = staged("H", HA, blk)
                tr.emit_merged(g, h)

            def GB(blk):
                par = blk % 2
                ti = blk // 4
                tok = slice(blk * 128, (blk + 1) * 128)
                SP2, SPH2, SPL2 = sg_t[:, 0:512], sph_t[:, 0:512], spl_t[:, 0:512]
                EQ2, EI2 = sg_t[:, 768:1280], sg_t[:, 1280:1792]
                SPK = [("sp", 0), ("sp", 1)]
                SPHK = [("sph", 0), ("sph", 1)]
                SPLK = [("spl", 0), ("spl", 1)]
                b = ps_next()

                tr.op("pe", lambda e, b=b: e.matmul(ps[b][:, :], rT_t[0:64, tok], gw_t[0:64, :], start=True, stop=True),
                      reads=[("rT", ti), "gw"], writes=[("ps", b)])
                tr.op("dve", lambda e, b=b: e.tensor_tensor(SP2, ps[b][:, :], gb_t[:, :], ALU.add),
                      reads=[("ps", b), "gb"], writes=SPK)
                tr.op("act", lambda e: e.activation(SP2, SP2, AF.Exp, scale=-1.0), reads=SPK, writes=SPK)
                tr.op("act", lambda e: e.activation(SP2, SP2, AF.Ln, bias=1.0), reads=SPK, writes=SPK)
                tr.op("act", lambda e: e.copy(SPH2, SP2), reads=SPK, writes=SPHK)
                tr.op("pool", lambda e: e.tensor_tensor(SPL2, SP2, SPH2, ALU.subtract), reads=SPK + SPHK, writes=SPLK)
                gate_tail(0, blk, par)
                b2 = ps_next()

                def mmc(e, b2=b2):
                    ins = None
                    for d in range(2):
                        Mc = M_CF if d == 0 else M_CB
                        for q in range(2):
                            o = ps[b2][:, d * 256 + q * 128:d * 256 + (q + 1) * 128]
                            e.matmul(o, SPH[d][:, q * 128:(q + 1) * 128], Mc, start=True, stop=False)
                            ins = e.matmul(o, SPL[d][:, q * 128:(q + 1) * 128], Mc, start=False, stop=True)
                    return ins
                tr.op("pe", mmc, reads=SPHK + SPLK + ["cmat"], writes=[("ps", b2)])
                tr.op("act", lambda e, b2=b2: e.activation(EQ2, ps[b2][:, :], AF.Exp),
                      reads=[("ps", b2)], writes=[("eq", 0), ("eq", 1)])
                tr.op("act", lambda e, b2=b2: e.activation(EI2, ps[b2][:, :], AF.Exp, scale=-1.0),
                      reads=[("ps", b2)], writes=[("ei", 0), ("ei", 1)])
                QD2 = xn_t[:, 1024 + par * 512:1024 + (par + 1) * 512].rearrange("p (d q v) -> p d q v", d=2, v=128)
                KI2 = xn_t[:, 2048 + par * 512:2048 + (par + 1) * 512].rearrange("p (d q v) -> p d q v", d=2, v=128)
                tr.op("dve", lambda e: e.tensor_tensor(
                    QD2, bcast_mid(qT3[:, :, tok], 2), EQ2.rearrange("p (d q v) -> p d q v", d=2, v=128), ALU.mult),
                    reads=[("eq", 0), ("eq", 1), ("qT", 0, ti), ("qT", 1, ti)], writes=[("qd", 0, par), ("qd", 1, par)])
                tr.op("pool", lambda e: e.tensor_tensor(
                    KI2, bcast_mid(kT3[:, :, tok], 2), EI2.rearrange("p (d q v) -> p d q v", d=2, v=128), ALU.mult),
                    reads=[("ei", 0), ("ei", 1), ("kT", 0, ti), ("kT", 1, ti)], writes=[("ki", 0, par), ("ki", 1, par)])
                base = DECB[0][par]
                tr.op("dve", lambda e, base=base: e.tensor_copy(
                    dec_t[:, base:base + 4], EQ[0].rearrange("p (j s) -> p j s", s=64)[:, :, 63]),
                    reads=[("eq", 0)], writes=[("dec", 0, par)])

            def HB(blk):
                par = blk % 2
                ti = blk // 4
                tok = slice(blk * 128, (blk + 1) * 128)
                ob = 6 + blk % 2
                QD = [QDB[0][par], QDB[1][par]]
                KI = [KIB[0][par], KIB[1][par]]
                for ch in range(2):
                    tr.op("act", lambda e, ch=ch: e.copy(SFB[ch], srun[:, 0, :, :]),
                          reads=[("srun", 0, 0), ("srun", 0, 1)], writes=[("sfb", ch)])
                    state_update(0, blk, ch, par)
                PMD = [sq_t[:, 1024:1536], sq_t[:, 1536:2048]]
                for hh_ in range(2):
                    p0 = hh_ * 64
                    for d in range(2):
                        b = ps_next()

                        def mms(e, b=b, d=d, p0=p0):
                            ins = None
                            for q in range(2):
                                ins = e.matmul(ps[b][:, q * 128:(q + 1) * 128], KI[d][p0:p0 + 64, q * 128:(q + 1) * 128],
                                               QD[d][p0:p0 + 64, q * 128:(q + 1) * 128], start=True, stop=True)
                            return ins
                        tr.op("pe", mms, reads=[("ki", d, par), ("qd", d, par)], writes=[("ps", b)])
                        mk = MASK_F if d == 0 else MASK_B
                        tr.op("dve", lambda e, b=b, d=d, mk=mk, hh_=hh_: e.tensor_tensor(
                            PMD[d].rearrange("p (q h v) -> p q h v", h=2, v=128)[:, :, hh_, :],
                            ps[b][:, 0:256].rearrange("p (q v) -> p q v", v=128),
                            bcast_mid(mk, 2), ALU.mult),
                            reads=[("ps", b), "cmat", ("sq", 2 + d)], writes=[("sq", 2 + d)])
                for h in range(4):
                    q, hh_ = h // 2, h % 2
                    p0 = hh_ * 64

                    def mo(e, h=h, q=q, p0=p0, ob=ob, blk=blk):
                        o = ps[ob][:, h * 128:(h + 1) * 128]
                        vh = vtok3[:, blk, h * 128:(h + 1) * 128]
                        e.matmul(o, vh, PMD[0][:, h * 128:(h + 1) * 128], start=True, stop=False)
                        e.matmul(o, vh, PMD[1][:, h * 128:(h + 1) * 128], start=False, stop=False)
                        ins = None
                        for ch in range(2):
                            oc = ps[ob][:, h * 128 + ch * 64:h * 128 + ch * 64 + 64]
                            e.matmul(oc, SFB[ch][p0:p0 + 64, q, :], QD[0][p0:p0 + 64, q * 128 + ch * 64:q * 128 + ch * 64 + 64],
                                     start=False, stop=False)
                            ins = e.matmul(oc, sbsave[p0:p0 + 64, blk * 2 + ch, q, :],
                                           QD[1][p0:p0 + 64, q * 128 + ch * 64:q * 128 + ch * 64 + 64],
                                           start=False, stop=(ch == 1))
                        return ins
                    tr.op("pe", mo, reads=[("sq", 2), ("sq", 3), ("vtok", blk, h // 2),
                                           ("sfb", 0), ("sfb", 1), ("sbs", blk * 2), ("sbs", blk * 2 + 1),
                                           ("qd", 0, par), ("qd", 1, par)],
                          writes=[("ps", ob)] if h == 0 else [("pso", ob, h)])
                OKEYS = [("ps", ob)] + [("pso", ob, h) for h in range(1, 4)]
                tr.op("act", lambda e, ob=ob: e.activation(sq_t[:, 0:512], ps[ob][:, :], AF.Square),
                      reads=OKEYS, writes=[("sq", 0)])
                b = ps_next()
                tr.op("pe", lambda e, b=b: e.matmul(ps[b][:, :], ones128_t[:, :], sq_t[:, 0:512], start=True, stop=True),
                      reads=[("sq", 0), "ones128"], writes=[("ps", b)])
                act_rsqrt(rs_t[:, :], ps[b][:, :], [("ps", b)], ["rs"])
                tr.op("dve", lambda e, ob=ob: e.tensor_tensor(tt_t[:, :], ps[ob][:, :], rs_t[:, :], ALU.mult),
                      reads=OKEYS + ["rs"], writes=["tt"])
                tr.op("pool", lambda e, tok=tok: e.tensor_tensor(
                    m3[:, 0:4, tok], tt_t[:, :].rearrange("p (h v) -> p h v", v=128), m3[:, 0:4, tok], ALU.mult),
                    reads=["tt"] + [("m", c, ti) for c in range(4)], writes=[("m", c, ti) for c in range(4)])
            def conv_item(ti, c):
                tl = ti * T
                a = cv_t[:, (c % 2) * T:(c % 2 + 1) * T]
                ka = ("cv", c % 2)
                zr = [("z", c, ti)]
                zr.append(("z", c, ti - 1) if ti > 0 else ("zh", 0))
                zr.append(("z", c, ti + 1) if ti < TPS - 1 else ("zh", 1))
                tr.op("dve", lambda e: e.tensor_scalar_mul(a, z3[:, c, tl:tl + T], vecs_t[:, c * 3:c * 3 + 1]),
                      reads=zr + ["vecs"], writes=[ka])
                for i in (1, 2):
                    tr.op("dve", lambda e, i=i: e.scalar_tensor_tensor(
                        a, z3[:, c, tl + i:tl + i + T], vecs_t[:, c * 3 + i:c * 3 + i + 1], a, ALU.mult, ALU.add),
                        reads=zr + ["vecs", ka], writes=[ka])
                tr.op("pool", lambda e: e.tensor_tensor(a, a, m3[:, 4 + c, tl:tl + T], ALU.mult),
                      reads=[ka, ("m", 4 + c, ti)], writes=[ka])
                tr.op("act", lambda e: e.activation(sq_t[:, 512:1024], a, AF.Square), reads=[ka], writes=[("sq", 1)])
                b = ps_next()
                tr.op("pe", lambda e: e.matmul(ps[b][:, :], g64_t[:, :], sq_t[:, 512:1024], start=True, stop=True),
                      reads=[("sq", 1), "g64"], writes=[("ps", b)])
                act_rsqrt(rstd_t[:, :], ps[b][:, :], [("ps", b)], ["rstd"])
                tr.op("dve", lambda e: e.scalar_tensor_tensor(
                    m3[:, 4 + c, tl:tl + T], a, vecs_t[:, 12 + c:13 + c], rstd_t[:, :], ALU.mult, ALU.mult),
                    reads=[ka, "rstd", "vecs"], writes=[("m", 4 + c, ti)])

            GB(0)
            for blk in range(16):
                g = staged("G", GB, blk + 1) if blk + 1 < 16 else []
                h = staged("H3", HB, blk)
                cst = staged("C", conv_item, blk // 4, blk % 4)
                tr.emit_merged(g, h, cst)
            fence(H_ALL + SGT_ALL + XN_ALL + [("sbs", n) for n in range(32)])

        def wout(ti):
            tl = ti * T
            for j in range(4):
                w, kw = ring_get()
                w3 = w.rearrange("p (c n) -> p c n", n=256)
                for c2 in range(2):
                    mo_ = j * 2 + c2
                    b = ps_next()

                    def mm(e, w3=w3, c2=c2, b=b):
                        ins = None
                        for k in range(KC):
                            ins = e.matmul(ps[b][:, :], w3[:, k, c2 * 128:(c2 + 1) * 128], m3[:, k, tl:tl + T],
                                           start=(k == 0), stop=(k == KC - 1))
                        return ins
                    tr.op("pe", mm, reads=[("m", k, ti) for k in range(KC)] + [kw], writes=[("ps", b)])
                    tr.op("dve", lambda e, b=b, mo_=mo_: e.tensor_tensor(xt[:, mo_, :], xt[:, mo_, :], ps[b][:, :], ALU.add),
                          reads=[("ps", b), ("xt", mo_)], writes=[("xt", mo_)])
                    stats_sq(mo_)
                ring_advance(pi[0])
            for mo_ in range(KC):
                stats_mm(mo_)
            stats_finish()

        SEG_KEYS = {
            "sQ": [("qT", c2, ti) for c2 in range(2) for ti in range(TPS)],
            "sK": [("kT", c2, ti) for c2 in range(2) for ti in range(TPS)],
            "sKtok": [("ktok", blk) for blk in range(16)],
            "sVtok": [("vtok", blk, hv) for blk in range(16) for hv in range(2)],
            "sR": [("rT", ti) for ti in range(TPS)],
            "sM": [("m", c, ti) for c in range(8) for ti in range(TPS)],
            "sZ": [("z", c, ti) for c in range(4) for ti in range(TPS)] + [("zh", 0), ("zh", 1)],
        }
        SEG_SB = {"sQ": qT_t, "sK": kT_t, "sKtok": ktok_t, "sVtok": vtok_t, "sR": rT_t, "sM": m_t, "sZ": z_t}
        ring_advance(0)

        def phase1(x1_dst):
            for ti in range(TPS):
                t0 = ti * T
                tr.dma("pool", xt, xT[:, :, t0:t0 + T], reads=[], writes=XT_ALL)
                rmsnorm_stats()
                rmsnorm_apply(0, xn, XN_ALL)
                ffn()
                tr.dma("pool", x1_dst[:, :, t0:t0 + T], xt, reads=XT_ALL, writes=[("x1s", t0)], is_output=(mode == "A"))
                rmsnorm_apply(1, xn, XN_ALL)
                projection(ti)

        def phase3(x1_src, x1_keyed, yoff, prefetch=None):
            for ti in range(TPS):
                t0 = ti * T
                tr.dma("pool", xt, x1_src[:, :, t0:t0 + T], reads=([("x1s", t0)] if x1_keyed else []), writes=XT_ALL)
                if prefetch is not None:
                    for nm in prefetch.get(ti, []):
                        tr.dma("act", SEG_SB[nm][:, :], segd[nm], reads=[], writes=SEG_KEYS[nm])
                wout(ti)
                rmsnorm_apply(2, xn, XN_ALL)
                ffn()
                if prefetch is not None:
                    sm3 = segd["sM"].rearrange("p (c n) -> p c n", n=L)
                    tr.dma("act", m3[:, :, t0:t0 + T], sm3[:, :, t0:t0 + T], reads=[],
                           writes=[("m", c, ti) for c in range(8)])
                yst = h_t[:, 0:2 * KC * T].bitcast(F32).rearrange("p (c n) -> p c n", n=T)
                YK = [("h", 2 * c) for c in range(KC)] + [("h", 2 * c + 1) for c in range(KC)]
                rmsnorm_apply(3, yst, [("h", 2 * c) for c in range(KC)])
                tr.dma("sp", yT[:, :, yoff + t0:yoff + t0 + T], yst, reads=YK, writes=[("yT", yoff + t0)] + [("h", 2 * c + 1) for c in range(KC)],
                       is_output=True)

        if mode == "A":
            phase1(x1p)
            local_scan_and_exchange(True, False)
            for nm, pp, n in SEGBUFS:
                tr.dma("pool", segd[nm], SEG_SB[nm][:, :], reads=SEG_KEYS[nm], writes=[("segd", nm)], is_output=True)
        else:
            tr.op("pool", lambda e: e.memset(z3[:, :, 0:1], 0.0), writes=[("zh", 0)])
            tr.op("pool", lambda e: e.memset(z3[:, :, L + 1:L + 2], 0.0), writes=[("zh", 1)])
            phase1(x1s)
            phase2(0)
            phase3(x1s, True, 0, prefetch={0: ["sQ", "sK", "sR"], 1: ["sKtok", "sVtok"], 2: ["sZ"]})
            local_scan_and_exchange(False, True)
            phase2(1, zero_init=False)
            phase3(x1p, False, L)
        assert pi[0] == len(panels), (pi[0], len(panels))
        print('TCNT', tr.tcnt, max(tr.dcnt))
        tr.finish()
        tr.emit()
    return nc


def _cmat():
    a = np.arange(128)[:, None]
    b = np.arange(128)[None, :]
    same = (a // 64) == (b // 64)
    f32 = np.float32
    u1 = (same & (a <= b)).astype(f32)
    l1 = (same & (a >= b)).astype(f32)
    su = (same & (a < b)).astype(f32)
    sl = (same & (a > b)).astype(f32)
    ci = ((np.arange(128)[:, None] // 64) == np.arange(2)[None, :]).astype(f32)
    s = f32(-1.0 / 16.0)
    return np.ascontiguousarray(np.concatenate([s * u1, s * l1, s * su, s * sl, u1, l1, s * ci], axis=1))


def _panelize(inputs):
    f32 = np.float32
    out = np.zeros((NPAN, 128, SLOT), f32)

    def kmajor(w):
        K, n = w.shape
        return w.reshape(K // 128, 128, n).transpose(1, 0, 2).reshape(128, (K // 128) * n)
    for f, (pg, pu, pd) in ((1, (P_G1, P_U1, P_D1)), (2, (P_G2, P_U2, P_D2))):
        wg = np.asarray(inputs["ffn%d_w_gate" % f], f32)[0]
        wu = np.asarray(inputs["ffn%d_w_up" % f], f32)[0]
        wd = np.asarray(inputs["ffn%d_w_down" % f], f32)[0]
        for j in range(11):
            out[pg + j, :, :2048] = kmajor(wg[:, 256 * j:256 * j + 256])
            out[pu + j, :, :2048] = kmajor(wu[:, 256 * j:256 * j + 256])
        for m in range(8):
            out[pd + m, :, :2816] = kmajor(wd[:, 128 * m:128 * m + 128])
    w_in = np.asarray(inputs["w_in"], f32)[0]
    w_in_p = np.zeros((D, N_IN_PAD), f32)
    w_in_p[:, 0:1536] = w_in[:, 0:1536]
    w_in_p[:, 1536:1552] = w_in[:, 1536:1552]
    w_in_p[:, 1568:1584] = w_in[:, 1552:1568]
    w_in_p[:, 1600:] = w_in[:, 1568:]
    for pj, (c0, n) in enumerate(IN_PANELS):
        out[P_IN + pj, :, :8 * n] = kmajor(w_in_p[:, c0:c0 + n])
    wo = np.asarray(inputs["w_out"], f32)[0]
    for j in range(4):
        out[P_OUT + j, :, :2048] = kmajor(wo[:, 256 * j:256 * j + 256])
    return out


def _small_inputs(inputs):
    f32 = np.float32

    def vec(v):
        return np.ascontiguousarray(np.asarray(v, f32).reshape(KC, 128).T)
    gains = np.concatenate([vec(inputs["ffn1_norm"][0]), vec(inputs["mix_norm"][0]),
                            vec(inputs["ffn2_norm"][0]), vec(inputs["final_norm"])], axis=1)
    gw = np.zeros((64, 512), f32)
    gw[0:16, 0:256] = np.asarray(inputs["gate_fwd_w"], f32)[0]
    gw[32:48, 256:512] = np.asarray(inputs["gate_bwd_w"], f32)[0]
    gb = np.concatenate([np.asarray(inputs["gate_fwd_b"], f32)[0], np.asarray(inputs["gate_bwd_b"], f32)[0]])[None, :]
    gb = np.ascontiguousarray(np.broadcast_to(gb, (128, 512)))
    cw = np.asarray(inputs["conv_w"], f32)[0]
    vecs = np.zeros((128, 20), f32)
    for c in range(4):
        for i in range(3):
            vecs[:, c * 3 + i] = cw[i, c * 128:(c + 1) * 128]
        vecs[:, 12 + c] = np.asarray(inputs["conv_group_norm"], f32)[0, c * 128:(c + 1) * 128]
        vecs[:, 16 + c] = np.asarray(inputs["gla_head_norm"], f32)[0, c * 128:(c + 1) * 128]
    return {"gains": np.ascontiguousarray(gains), "gw": gw, "gb": gb, "vecs": vecs, "cmat": _cmat()}


def _tmajor(x):
    return np.ascontiguousarray(x.reshape(x.shape[0], KC, 128).transpose(2, 1, 0))


_NC_CACHE = {}


def kernel(**inputs):
    f32 = np.float32
    if "W" not in _NC_CACHE:
        _NC_CACHE["W"] = build_w()
        _NC_CACHE["A"] = build_nc(mode="A")
        _NC_CACHE["B"] = build_nc(mode="B")
    cores = list(range(NCORES))
    pan32 = _panelize(inputs)
    resW = run_bass_kernel_spmd(_NC_CACHE["W"], [{"wsrc": np.ascontiguousarray(pan32[i * NPC:(i + 1) * NPC])}
                                                 for i in cores], core_ids=cores)
    wpan = np.ascontiguousarray(np.concatenate([np.asarray(resW.results[i]["wdst"]) for i in cores], axis=0))
    small = _small_inputs(inputs)
    xp = np.asarray(inputs["x_prompt"], f32)[0]
    xs = np.asarray(inputs["x_sample"], f32)
    mapsA = []
    for i in cores:
        m = dict(small)
        m["wpan"] = wpan
        m["xT"] = _tmajor(xp[i * L:(i + 1) * L])
        mapsA.append(m)
    resA = run_bass_kernel_spmd(_NC_CACHE["A"], mapsA, core_ids=cores)
    xall = np.ascontiguousarray(np.concatenate([np.asarray(resA.results[i]["xout"]) for i in cores], axis=0))
    mapsB = []
    for i in cores:
        m = dict(small)
        m["wpan"] = wpan
        m["xT"] = _tmajor(xs[i])
        m["xall"] = xall
        m["x1p"] = np.asarray(resA.results[i]["x1p"])
        for nm, pp, n in SEGBUFS:
            m[nm] = np.asarray(resA.results[i][nm])
        rm = np.zeros((128, 32), f32)
        for cp in range(NCORES):
            rm[:, cp] = 1.0 if cp < i else 0.0
            rm[:, 8 + cp] = 1.0 if cp > i else 0.0
            rm[:, 16 + cp] = 1.0 if cp == i - 1 else 0.0
            rm[:, 24 + cp] = 1.0 if cp == i + 1 else 0.0
        m["rmask"] = rm
        mapsB.append(m)
    res = run_bass_kernel_spmd(_NC_CACHE["B"], mapsB, core_ids=cores)
    yp = np.empty((1, NCORES * L, D), np.float32)
    ys = np.empty((NCORES, L, D), np.float32)
    for i in range(NCORES):
        y = np.asarray(res.results[i]["yT"])
        yc = y.transpose(2, 1, 0).reshape(TOK, D)
        ys[i] = yc[:L]
        yp[0, i * L:(i + 1) * L] = yc[L:]
    return (yp, ys)
```
